# Optimizing a Trainium2 kernel written in Bass

```python
import math
import jax, jax.numpy as jnp
from jax import lax
import numpy as np

D_MODEL = 1024
BATCH = 1
SEQ = 16384
DEPTH = 1
DEC_BATCH = 32
DEC_SEQ = 16
PAST_LEN = 1024

CHUNK = 64
N_META = 16
Q_BLOCK = 128
HEAD_DIM = 64
N_HEADS_DIFF = 4
N_HEADS_SB = 8
DIFF_WIDTH = N_HEADS_DIFF * 2 * HEAD_DIM
SB_WIDTH = N_HEADS_SB * HEAD_DIM
MIX_WIDTH = DIFF_WIDTH + SB_WIDTH
Q_COLS = DIFF_WIDTH + SB_WIDTH
D_IN = 3 * MIX_WIDTH
D_FF = 4 * D_MODEL
ROPE_THETA = 10000.0
NORM_EPS = 1e-6
NEG_INF = -1e30

kernel_name = "hymba_diff_stickbreak_streaming_step"


def rmsnorm(x, g):
    xf = x.astype(jnp.float32)
    y = xf * lax.rsqrt(jnp.mean(xf * xf, axis=-1, keepdims=True) + NORM_EPS)
    return (y * g.astype(jnp.float32)).astype(x.dtype)


def rope(x, pos):
    half = HEAD_DIM // 2
    inv_freq = ROPE_THETA ** (-jnp.arange(half, dtype=jnp.float32) / half)
    ang = pos.astype(jnp.float32)[:, None] * inv_freq[None, :]
    shape = (pos.shape[0],) + (1,) * (x.ndim - 3) + (half,)
    cos = jnp.cos(ang).reshape(shape)
    sin = jnp.sin(ang).reshape(shape)
    xf = x.astype(jnp.float32)
    x1, x2 = xf[..., :half], xf[..., half:]
    return jnp.concatenate([x1 * cos - x2 * sin, x2 * cos + x1 * sin], axis=-1).astype(x.dtype)


def chunk_id(pos):
    return jnp.where(pos < N_META, -1, (pos - N_META) // CHUNK)


def split_q(zq, pos):
    B, T, _ = zq.shape
    dq, sq = jnp.split(zq, [DIFF_WIDTH], axis=-1)
    dq = rope(dq.reshape(B, T, N_HEADS_DIFF, 2, HEAD_DIM), pos)
    sq = sq.reshape(B, T, N_HEADS_SB, HEAD_DIM)
    return dq, sq


def split_kv(zkv, pos):
    B, T, _ = zkv.shape
    dk, sk, dv, sv = jnp.split(zkv, [DIFF_WIDTH, MIX_WIDTH, MIX_WIDTH + DIFF_WIDTH], axis=-1)
    dk = rope(dk.reshape(B, T, N_HEADS_DIFF, 2, HEAD_DIM), pos)
    dv = dv.reshape(B, T, N_HEADS_DIFF, 2 * HEAD_DIM)
    sk = sk.reshape(B, T, N_HEADS_SB, HEAD_DIM)
    sv = sv.reshape(B, T, N_HEADS_SB, HEAD_DIM)
    return dk, dv, sk, sv


def diff_attn(q, k, v, mask, lam, g_head, lambda_init):
    B, Tq = q.shape[0], q.shape[1]
    s = jnp.einsum('bqhcd,bkhcd->bhcqk', q.astype(jnp.float32), k.astype(jnp.float32)) * (HEAD_DIM ** -0.5)
    p = jax.nn.softmax(jnp.where(mask, s, NEG_INF), axis=-1)
    w = p[:, :, 0] - lam * p[:, :, 1]
    o = jnp.einsum('bhqk,bkhe->bqhe', w, v.astype(jnp.float32))
    o = rmsnorm(o, g_head) * (1.0 - lambda_init)
    return o.reshape(B, Tq, DIFF_WIDTH)


def sb_attn(q, k, v, mask):
    B, Tq = q.shape[0], q.shape[1]
    z = jnp.einsum('bqhd,bkhd->bhqk', q.astype(jnp.float32), k.astype(jnp.float32)) * (HEAD_DIM ** -0.5)
    log_beta = jax.nn.log_sigmoid(z)
    log_keep = jnp.where(mask, jax.nn.log_sigmoid(-z), 0.0)
    rc = lax.cumsum(log_keep, axis=log_keep.ndim - 1, reverse=True)
    after = jnp.concatenate([rc[..., 1:], jnp.zeros_like(rc[..., :1])], axis=-1)
    a = jnp.where(mask, jnp.exp(log_beta + after), 0.0)
    o = jnp.einsum('bhqk,bkhd->bqhd', a, v.astype(jnp.float32))
    return o.reshape(B, Tq, SB_WIDTH)


def attend(dq, sq, dk, dv, sk, sv, pos_q, pos_k, lam, g_head, lambda_init):
    mask_chunk = chunk_id(pos_k)[None, :] <= chunk_id(pos_q)[:, None]
    mask_strict = pos_k[None, :] < pos_q[:, None]
    return jnp.concatenate([diff_attn(dq, dk, dv, mask_chunk, lam, g_head, lambda_init),
                            sb_attn(sq, sk, sv, mask_strict)], axis=-1)


def attend_blocked(dq, sq, dk, dv, sk, sv, pos, lam, g_head, lambda_init):
    B, T = dq.shape[0], dq.shape[1]
    n_blk = -(-T // Q_BLOCK)
    pad = n_blk * Q_BLOCK - T
    dq_p = jnp.pad(dq, ((0, 0), (0, pad), (0, 0), (0, 0), (0, 0)))
    sq_p = jnp.pad(sq, ((0, 0), (0, pad), (0, 0), (0, 0)))

    def blk(i):
        start = i * Q_BLOCK
        qd = lax.dynamic_slice_in_dim(dq_p, start, Q_BLOCK, axis=1)
        qs = lax.dynamic_slice_in_dim(sq_p, start, Q_BLOCK, axis=1)
        pos_q = start + jnp.arange(Q_BLOCK)
        return attend(qd, qs, dk, dv, sk, sv, pos_q, pos, lam, g_head, lambda_init)

    out = lax.map(blk, jnp.arange(n_blk))
    return jnp.moveaxis(out, 0, 1).reshape(B, n_blk * Q_BLOCK, MIX_WIDTH)[:, :T]


def layer(x, pos, past, past_pos, p, blocked):
    h = rmsnorm(x, p['g_mix'])
    z = h @ p['w_in']
    dq, sq = split_q(z[..., :Q_COLS], pos)
    dk, dv, sk, sv = split_kv(z[..., Q_COLS:], pos)
    lam = (jnp.exp(jnp.sum(p['lq1'].astype(jnp.float32) * p['lk1'].astype(jnp.float32)))
           - jnp.exp(jnp.sum(p['lq2'].astype(jnp.float32) * p['lk2'].astype(jnp.float32)))
           + p['lambda_init'])
    if past is None:
        kd, vd, ks, vs, pos_k = dk, dv, sk, sv, pos
    else:
        kd = jnp.concatenate([past[0].astype(dk.dtype), dk], axis=1)
        vd = jnp.concatenate([past[1].astype(dv.dtype), dv], axis=1)
        ks = jnp.concatenate([past[2].astype(sk.dtype), sk], axis=1)
        vs = jnp.concatenate([past[3].astype(sv.dtype), sv], axis=1)
        pos_k = jnp.concatenate([past_pos, pos])
    if blocked:
        mixed = attend_blocked(dq, sq, kd, vd, ks, vs, pos, lam, p['g_head'], p['lambda_init'])
    else:
        mixed = attend(dq, sq, kd, vd, ks, vs, pos, pos_k, lam, p['g_head'], p['lambda_init'])
    x = x + mixed.astype(x.dtype) @ p['w_out']
    h2 = rmsnorm(x, p['g_mlp'])
    x = x + jnp.square(jax.nn.relu(h2 @ p['w_up'])) @ p['w_down']
    return x, (dk, dv, sk, sv)


def setup_inputs(seed: int = 0) -> dict:
    key = jax.random.key(seed)
    ks = jax.random.split(key, 20)
    f32 = jnp.float32
    nrm = lambda k, s, sc: jax.random.normal(k, s, f32) * sc
    return {
        'x_prompt': nrm(ks[0], (BATCH, SEQ, D_MODEL), 1.0),
        'x_sample': nrm(ks[1], (DEC_BATCH, DEC_SEQ, D_MODEL), 1.0),
        'cache_diff_k': nrm(ks[2], (DEPTH, DEC_BATCH, PAST_LEN, N_HEADS_DIFF, 2, HEAD_DIM), 1.0),
        'cache_diff_v': nrm(ks[3], (DEPTH, DEC_BATCH, PAST_LEN, N_HEADS_DIFF, 2 * HEAD_DIM), 1.0),
        'cache_sb_k': nrm(ks[4], (DEPTH, DEC_BATCH, PAST_LEN, N_HEADS_SB, HEAD_DIM), 1.0),
        'cache_sb_v': nrm(ks[5], (DEPTH, DEC_BATCH, PAST_LEN, N_HEADS_SB, HEAD_DIM), 1.0),
        'meta_tokens': nrm(ks[6], (N_META, D_MODEL), 1.0),
        'g_mix': 1.0 + nrm(ks[7], (DEPTH, D_MODEL), 0.02),
        'w_in': nrm(ks[8], (DEPTH, D_MODEL, D_IN), D_MODEL ** -0.5),
        'lambda_q1': nrm(ks[9], (DEPTH, HEAD_DIM), 0.1),
        'lambda_k1': nrm(ks[10], (DEPTH, HEAD_DIM), 0.1),
        'lambda_q2': nrm(ks[11], (DEPTH, HEAD_DIM), 0.1),
        'lambda_k2': nrm(ks[12], (DEPTH, HEAD_DIM), 0.1),
        'g_diff_head': 1.0 + nrm(ks[13], (DEPTH, 2 * HEAD_DIM), 0.02),
        'w_out': nrm(ks[14], (DEPTH, MIX_WIDTH, D_MODEL), MIX_WIDTH ** -0.5),
        'g_mlp': 1.0 + nrm(ks[15], (DEPTH, D_MODEL), 0.02),
        'w_up': nrm(ks[16], (DEPTH, D_MODEL, D_FF), D_MODEL ** -0.5),
        'w_down': nrm(ks[17], (DEPTH, D_FF, D_MODEL), D_FF ** -0.5),
        'g_final': 1.0 + nrm(ks[18], (D_MODEL,), 0.02),
    }


def reference(x_prompt, x_sample, cache_diff_k, cache_diff_v, cache_sb_k, cache_sb_v,
              meta_tokens, g_mix, w_in, lambda_q1, lambda_k1, lambda_q2, lambda_k2,
              g_diff_head, w_out, g_mlp, w_up, w_down, g_final):
    Bp, Tp, D = x_prompt.shape
    Bs, Ts, _ = x_sample.shape
    P = cache_diff_k.shape[2]

    def params(l):
        return {'g_mix': g_mix[l], 'w_in': w_in[l], 'lq1': lambda_q1[l], 'lk1': lambda_k1[l],
                'lq2': lambda_q2[l], 'lk2': lambda_k2[l], 'g_head': g_diff_head[l],
                'w_out': w_out[l], 'g_mlp': g_mlp[l], 'w_up': w_up[l], 'w_down': w_down[l],
                'lambda_init': 0.8 - 0.6 * math.exp(-0.3 * l)}

    xp = jnp.concatenate([jnp.broadcast_to(meta_tokens[None].astype(x_prompt.dtype), (Bp, N_META, D)),
                          x_prompt], axis=1)
    pos_p = jnp.arange(N_META + Tp)
    pk_d, pv_d, pk_s, pv_s = [], [], [], []
    for l in range(DEPTH):
        xp, (dk, dv, sk, sv) = layer(xp, pos_p, None, None, params(l), True)
        pk_d.append(dk); pv_d.append(dv); pk_s.append(sk); pv_s.append(sv)
    y_prompt = rmsnorm(xp, g_final)[:, N_META:]

    pos_meta = jnp.arange(N_META)
    pos_cache = N_META + jnp.arange(P)
    pos_new = N_META + P + jnp.arange(Ts)
    past_pos = jnp.concatenate([pos_meta, pos_cache])
    h_meta = meta_tokens[None].astype(x_sample.dtype)
    xs = x_sample
    sk_d, sv_d, sk_s, sv_s = [], [], [], []
    for l in range(DEPTH):
        p = params(l)
        if l < DEPTH - 1:
            h_meta_next, meta_kv = layer(h_meta, pos_meta, None, None, p, False)
        else:
            h_meta_next = h_meta
            meta_kv = split_kv(rmsnorm(h_meta, p['g_mix']) @ p['w_in'][:, Q_COLS:], pos_meta)
        caches = (cache_diff_k[l], cache_diff_v[l], cache_sb_k[l], cache_sb_v[l])
        past = tuple(jnp.concatenate([jnp.broadcast_to(m.astype(c.dtype), (Bs,) + m.shape[1:]), c], axis=1)
                     for m, c in zip(meta_kv, caches))
        xs, (dk, dv, sk, sv) = layer(xs, pos_new, past, past_pos, p, False)
        sk_d.append(dk); sv_d.append(dv); sk_s.append(sk); sv_s.append(sv)
        h_meta = h_meta_next
    y_sample = rmsnorm(xs, g_final)

    return (y_prompt, y_sample,
            jnp.stack(pk_d), jnp.stack(pv_d), jnp.stack(pk_s), jnp.stack(pv_s),
            jnp.stack(sk_d), jnp.stack(sv_d), jnp.stack(sk_s), jnp.stack(sv_s))
```

```python
import contextlib
import numpy as np
import ml_dtypes
import concourse.bass as bass
import concourse.mybir as mybir
from concourse.bass_utils import run_bass_kernel_spmd

F32 = mybir.dt.float32
BF16 = mybir.dt.bfloat16
AF = mybir.ActivationFunctionType
ALU = mybir.AluOpType

NCORES = 8
D = 1024
SEQ = 16384
NMETA = 16
TP = NMETA + SEQ
TQ = 512
NSLOT = 4
NBLK = 129
NEG = -30000.0
EPS = 1e-6
DEC_B = 32
DEC_T = 16
PAST = 1024
BPC = DEC_B // NCORES
ST = BPC * DEC_T
NDMA = 56
NDMA_HW = 44

C_DK, C_SK, C_DV, C_SV, C_DQ, C_SQ, C_DQS, C_DKS = 0, 512, 1024, 1536, 2048, 2560, 3072, 3584


class Buf:
    __slots__ = ("w", "r")

    def __init__(self):
        self.w = {}
        self.r = {}


class Prog:
    CE = ("pe", "act", "dve", "pool")

    def __init__(self):
        self.q = {e: [] for e in self.CE + ("sp",)}
        self.cnt = {e: 0 for e in self.CE}
        self.waited = {e: {} for e in self.CE + ("sp",)}
        self.dma_vals = [0] * NDMA
        self.rr = 0
        self.rr_sw = 0

    def _wait(self, eng, key, val):
        if self.waited[eng].get(key, 0) >= val:
            return
        self.waited[eng][key] = val
        self.q[eng].append(("w", key, val))

    def _deps(self, eng, reads, writes):
        for b in reads:
            for k, v in b.w.items():
                if not (eng == "pe" and k == "pe"):
                    self._wait(eng, k, v)
        for b in writes:
            for dct in (b.w, b.r):
                for k, v in dct.items():
                    if not (eng == "pe" and k == "pe"):
                        self._wait(eng, k, v)

    def _mark(self, key, val, reads, writes):
        for b in reads:
            if b.r.get(key, 0) < val:
                b.r[key] = val
        for b in writes:
            b.w[key] = val
            b.r = {}

    def op(self, eng, fn, reads=(), writes=()):
        self._deps(eng, reads, writes)
        self.cnt[eng] += 1
        self.q[eng].append(("o", fn))
        self._mark(eng, self.cnt[eng], reads, writes)

    def dma(self, eng, out, in_, reads=(), writes=()):
        if eng == "pool":
            i = NDMA_HW + self.rr_sw
            self.rr_sw = (self.rr_sw + 1) % (NDMA - NDMA_HW)
        else:
            i = self.rr
            self.rr = (i + 1) % NDMA_HW
        key = ("d", i)
        if self.dma_vals[i] > 0:
            self._wait(eng, key, self.dma_vals[i])
        self._deps(eng, reads, writes)
        self.dma_vals[i] += 16
        self.q[eng].append(("d", out, in_, i))
        self._mark(key, self.dma_vals[i], reads, writes)

    def barrier(self):
        for e in self.CE + ("sp",):
            for o in self.CE:
                if o != e and self.cnt[o] > 0:
                    self._wait(e, o, self.cnt[o])
            for i in range(NDMA):
                if self.dma_vals[i] > 0:
                    self._wait(e, ("d", i), self.dma_vals[i])

    def replay(self, eng, e, sems, dsems):
        for it in self.q[eng]:
            if it[0] == "w":
                k = it[1]
                s = dsems[k[1]] if isinstance(k, tuple) else sems[k]
                e.wait_ge(s, it[2])
            elif it[0] == "o":
                it[1](e).then_inc(sems[eng], 1)
            else:
                e.dma_start(out=it[1], in_=it[2]).then_inc(dsems[it[3]], 16)


def mm(out, lhsT, rhs, start=True, stop=True, sgc=False):
    return lambda e: e.matmul(out, lhsT=lhsT, rhs=rhs, start=start, stop=stop, skip_group_check=sgc)


def actf(out, in_, func, **kw):
    return lambda e: e.activation(out=out, in_=in_, func=func, **kw)


def ttf(out, a, b, op):
    return lambda e: e.tensor_tensor(out=out, in0=a, in1=b, op=op)


def tsf(out, a, s1, op0):
    return lambda e: e.tensor_scalar(out=out, in0=a, scalar1=s1, scalar2=None, op0=op0)


def sttf(out, in0, scalar, in1, op0, op1):
    return lambda e: e.scalar_tensor_tensor(out=out, in0=in0, scalar=scalar, in1=in1, op0=op0, op1=op1)


def cpf(out, in_):
    return lambda e: e.tensor_copy(out=out, in_=in_)


def msf(ap, v):
    return lambda e: e.memset(ap, v)


def build_program():
    nc = bass.Bass("TRN2", target_bir_lowering=False)
    P = Prog()
    dr = {}

    def din(n, shape, dt=F32):
        dr[n] = nc.dram_tensor(n, list(shape), dt, kind="ExternalInput").ap()

    def dout(n, shape, dt=F32):
        dr[n] = nc.dram_tensor(n, list(shape), dt, kind="ExternalOutput").ap()

    din("xT", [D, TP]); din("xqT", [NSLOT, D, TQ]); din("xq", [NSLOT, TQ, D])
    din("w_in", [D, 3 * D]); din("w_out", [D, D]); din("w_up", [D, 4 * D]); din("w_down", [4 * D, D])
    din("gcols", [128, 16]); din("gfin", [128, D]); din("ghead", [128, 1]); din("lamv", [128, 256])
    din("cosF", [128, TP]); din("sinF", [128, TP])
    din("cosQ", [NSLOT, 128, TQ]); din("sinQ", [NSLOT, 128, TQ])
    din("cosT", [NSLOT, 128, 4, 32]); din("sinT", [NSLOT, 128, 4, 32])
    din("dmask", [128, 8, 512], BF16); din("kbias", [128, NSLOT * NBLK + 2]); din("consts", [128, 512], BF16)
    din("xsT", [D, ST]); din("xs", [ST, D])
    din("cdk", [BPC, PAST, 512]); din("csk", [BPC, PAST, 512]); din("cdv", [BPC, PAST, 512]); din("csv", [BPC, PAST, 512])
    din("cosS", [128, ST]); din("sinS", [128, ST]); din("cosST", [16, BPC, 32]); din("sinST", [16, BPC, 32])
    din("cosMT", [16, 1, 32]); din("sinMT", [16, 1, 32])
    din("cosTp", [31, 128, 4, 32]); din("sinTp", [31, 128, 4, 32])
    dout("y", [NSLOT, TQ, D]); dout("kvo", [NSLOT, TQ, 2048]); dout("metao", [NMETA, 2048])
    dout("ys", [ST, D]); dout("kvs", [ST, 2048])
    kT_scr = nc.dram_tensor("kT_scr", [8, 128, NBLK * 128], BF16, kind="Internal").ap()
    v_scr = nc.dram_tensor("v_scr", [8, 128, NBLK, 128], BF16, kind="Internal").ap()
    wo_scr = nc.dram_tensor("wo_scr", [2, 128, 8, 512], BF16, kind="Internal").ap()
    wu_scr = nc.dram_tensor("wu_scr", [8, 128, 8, 512], BF16, kind="Internal").ap()
    wd_scr = nc.dram_tensor("wd_scr", [8, 128, 4, 1024], BF16, kind="Internal").ap()
    wq_scr = nc.dram_tensor("wq_scr", [128, 8, 2048], BF16, kind="Internal").ap()

    es = contextlib.ExitStack()
    with es:
        def sb(name, shape, dt):
            return es.enter_context(nc.sbuf_tensor(name, list(shape), dt))

        WBIG = sb("WBIG", [128, 8, 4096], BF16)
        consts = sb("consts_sb", [128, 512], BF16)
        dmask = sb("dmask_sb", [128, 8, 512], BF16)
        kbias = sb("kbias_sb", [128, NSLOT * NBLK + 2], F32)
        gfin = sb("gfin_sb", [128, D], F32)
        gcols = sb("gcols_sb", [128, 16], F32)
        smallc = sb("smallc", [128, 64], F32)
        lamv = sb("lamv_sb", [128, 256], F32)
        qT = sb("qT", [128, 8, 512], BF16)
        kTo = sb("kTo", [128, 8, 512], BF16)
        Vo = sb("Vo", [128, 4, 1024], BF16)
        mixT = sb("mixT", [128, 8, 512], BF16)
        SF = sb("SF", [128, 14336], F32)
        SB = sb("SB", [128, 18432], BF16)
        ps = es.enter_context(nc.psum_tensor("ps", [128, 4096], F32))
        sems = {e: es.enter_context(nc.semaphore("s_" + e)) for e in Prog.CE}
        dsems = [es.enter_context(nc.semaphore("d%d" % i)) for i in range(NDMA)]

        ntri = consts[:, 0:128]; nones = consts[:, 128:256]; ones = consts[:, 256:384]; ident = consts[:, 384:512]
        ZB = NSLOT * NBLK
        MB = NSLOT * NBLK + 1

        bk = [Buf() for _ in range(8)]

        def bank(b, rows=128, w=512):
            return ps[:rows, b * 512: b * 512 + w]

        B_W = Buf(); B_Wq = Buf(); B_c = Buf(); B_qT = Buf(); B_kTo = Buf(); B_Vo = Buf(); B_mix = Buf(); B_small = Buf()

        xin2 = [SF[:, i * 4096:(i + 1) * 4096].rearrange("p (c t) -> p c t", t=512) for i in range(2)]
        zt = [SF[:, 4096 + i * 2048: 4096 + (i + 1) * 2048] for i in range(2)]; B_zt = [Buf(), Buf()]
        B_xin0 = Buf()
        cosF_t = SF[:, 8192:8704]; sinF_t = SF[:, 8704:9216]; B_cs = Buf()
        cosr = SF[:, 9216:9728]; sinr = SF[:, 9728:10240]; B_csr = Buf()
        rstd_b = SF[:, 10240:10752]; B_rb = Buf()
        t1 = [SF[:, 10752 + i * 512: 10752 + (i + 1) * 512] for i in range(2)]
        t2 = [SF[:, 11776 + i * 512: 11776 + (i + 1) * 512] for i in range(2)]
        B_t = [Buf(), Buf()]
        cosT_t = SF[:, 12800:12928].rearrange("p (s i) -> p s i", i=32)
        sinT_t = SF[:, 12928:13056].rearrange("p (s i) -> p s i", i=32); B_cst = Buf()
        rtmp = [SF[:, 13056 + i * 256: 13056 + (i + 1) * 256] for i in range(4)]; B_rt = Buf()
        lntmp = SF[:, 14080:14336 - 128]
        rcol = SF[:, 14336 - 128: 14336 - 120]; B_rc = Buf()
        sq2 = [SB[:, i * 8192: i * 8192 + 4096].rearrange("p (c t) -> p c t", t=512) for i in range(2)]; B_sq2 = [Buf(), Buf()]
        xg2 = [SB[:, i * 8192 + 4096: i * 8192 + 8192].rearrange("p (c t) -> p c t", t=512) for i in range(2)]; B_xg2 = [Buf(), Buf()]
        u_t = [SF[:, i * 1024:(i + 1) * 1024].rearrange("p (h w) -> p h w", h=2) for i in range(2)]; B_u = [Buf(), Buf()]
        dtmp = [SF[:, 2048 + i * 512: 2048 + (i + 1) * 512] for i in range(6)]; B_dt = Buf()
        Pacc = [SF[:, 5120 + i * 512: 5120 + (i + 1) * 512] for i in range(2)]; B_pa = [Buf(), Buf()]
        Kseg = [SB[:, i * 2048:(i + 1) * 2048] for i in range(3)]
        Vseg = [SB[:, 6144 + i * 2048: 6144 + (i + 1) * 2048].rearrange("p (b c) -> p b c", c=128) for i in range(3)]
        B_seg = [Buf(), Buf(), Buf()]
        L_t = [SB[:, 12288 + i * 1024: 12288 + (i + 1) * 1024].rearrange("p (h w) -> p h w", h=2) for i in range(2)]
        A_t = [SB[:, 14336 + i * 1024: 14336 + (i + 1) * 1024].rearrange("p (h w) -> p h w", h=2) for i in range(2)]
        Lacc = [SB[:, 16384 + i * 1024: 16384 + (i + 1) * 1024].rearrange("p (h w) -> p h w", h=2) for i in range(2)]
        B_L = [Buf(), Buf()]; B_A = [Buf(), Buf()]; B_Lacc = [Buf(), Buf()]
        r_t = SF[:, 6144:10240].rearrange("p (s n) -> p s n", n=1024); B_r = Buf()
        xqT_t = SF[:, 10240:14336].rearrange("p (c t) -> p c t", t=512); B_xqT = Buf()
        rl = [SF[:, 4096 + i * 512: 4096 + (i + 1) * 512] for i in range(2)]; B_rl = [Buf(), Buf()]
        x1tmp = [SF[:, 5120 + i * 512: 5120 + (i + 1) * 512] for i in range(2)]; B_x1t = [Buf(), Buf()]
        mcol = smallc[:, 16:48]; B_mc = Buf()
        junk = SF[:, 4096:5120]
        wbuf = [SB[:, i * 4096:(i + 1) * 4096] for i in range(2)]; B_wb = [Buf(), Buf()]
        x1g = qT; B_x1g = B_qT

        def rstd_from(ss_ap, out_ap, n_feat, rows, width, reads, writes, tmp_ap=None):
            P.op("act", actf(out_ap, ss_ap, AF.Ln, scale=1.0 / n_feat, bias=smallc[:rows, 8:9]), reads=reads + [B_small], writes=writes)
            P.op("act", actf(out_ap, out_ap, AF.Exp, scale=-0.5), reads=writes, writes=writes)

        P.dma("sp", consts[:], dr["consts"][:], writes=[B_c])
        P.dma("sp", dmask[:], dr["dmask"][:], writes=[B_c])
        P.dma("sp", kbias[:], dr["kbias"][:], writes=[B_c])
        P.dma("sp", gfin[:], dr["gfin"][:], writes=[B_c])
        P.dma("sp", gcols[:], dr["gcols"][:], writes=[B_c])
        P.dma("sp", lamv[:], dr["lamv"][:], writes=[B_c])
        P.dma("sp", smallc[:, 0:1], dr["ghead"][:], writes=[B_small])
        w_in_v = dr["w_in"].rearrange("(c p) n -> p c n", p=128)
        for c0 in range(0, 8, 2):
            P.dma("pool", WBIG[:, c0:c0 + 2, 0:2048], w_in_v[:, c0:c0 + 2, 1024:3072], writes=[B_W])

        def load_q_weights():
            for c0 in range(0, 8, 2):
                P.dma("pool", WBIG[:, c0:c0 + 2, 2048:3072], w_in_v[:, c0:c0 + 2, 0:1024], writes=[B_Wq])
            for c in range(8):
                for (s0, d0) in ((C_DQ, C_DQS), (C_DK, C_DKS)):
                    src = WBIG[:, c, s0:s0 + 512].rearrange("p (m h i) -> p m h i", h=2, i=32)
                    dst = WBIG[:, c, d0:d0 + 512].rearrange("p (m h i) -> p m h i", h=2, i=32)
                    P.op("pool", cpf(dst[:, :, 0, :], src[:, :, 1, :]), reads=[B_W, B_Wq], writes=[B_Wq])
                    P.op("pool", cpf(dst[:, :, 1, :], src[:, :, 0, :]), reads=[B_W, B_Wq], writes=[B_Wq])

        w_out_v0 = dr["w_out"].rearrange("(c p) n -> p c n", p=128)
        w_up_v0 = dr["w_up"].rearrange("(c p) n -> p c n", p=128)
        w_dn_v0 = dr["w_down"].rearrange("(m p) n -> p m n", p=128)
        conv_jobs = [(wo_scr[i], w_out_v0[:, :, i * 512:(i + 1) * 512]) for i in range(2)]
        for pc in range(8):
            conv_jobs.append((wu_scr[pc], w_up_v0[:, :, pc * 512:(pc + 1) * 512]))
            conv_jobs.append((wd_scr[pc], w_dn_v0[:, pc * 4:(pc + 1) * 4, :]))
        P.op("dve", msf(smallc[:, 8:9], EPS), writes=[B_small])
        P.op("dve", msf(smallc[:, 3:5], 0.0), writes=[B_small])
        P.op("dve", ttf(lamv[:, 0:64], lamv[:, 0:64], lamv[:, 64:128], ALU.mult), reads=[B_c], writes=[B_c])
        P.op("dve", ttf(lamv[:, 128:192], lamv[:, 128:192], lamv[:, 192:256], ALU.mult), reads=[B_c], writes=[B_c])
        P.op("act", actf(lamv[:, 64:128], lamv[:, 0:64], AF.Copy, accum_out=smallc[:, 3:4]), reads=[B_c, B_small], writes=[B_c, B_small])
        P.op("act", actf(lamv[:, 192:256], lamv[:, 128:192], AF.Copy, accum_out=smallc[:, 4:5]), reads=[B_c, B_small], writes=[B_c, B_small])
        P.op("act", actf(smallc[:, 5:7], smallc[:, 3:5], AF.Exp), reads=[B_small], writes=[B_small])
        P.op("dve", sttf(smallc[:, 1:2], smallc[:, 6:7], -0.2, smallc[:, 5:6], ALU.add, ALU.subtract), reads=[B_small], writes=[B_small])
        P.op("dve", tsf(smallc[:, 2:3], smallc[:, 0:1], 0.8, ALU.mult), reads=[B_small], writes=[B_small])

        fm_rot = [0]; tm_rot = [0]; t_rot = [0]; zt_rot = [0]

        cs2 = [(cosF_t, sinF_t), (SF[:, 13056:13568], SF[:, 13568:14080])]

        def project_loads(T, subs, x_ap, cos_ap, sin_ap, own, cosT_ap=None, sinT_ap=None, par=0):
            nsub = len(subs)
            xin = xin2[par]
            B_xinl = [B_xin0] if par == 0 else [B_zt[0], B_zt[1]]
            B_csl = [B_cs] if par == 0 else [B_rt]
            P.dma("sp", xin[:, :, :T], x_ap, writes=B_xinl)
            P.dma("sp", cs2[par][0][:, :T], cos_ap, writes=B_csl)
            P.dma("sp", cs2[par][1][:, :T], sin_ap, writes=B_csl)
            if own:
                P.dma("sp", cosT_t[:subs[0][1], :nsub, :], cosT_ap, writes=[B_cst])
                P.dma("sp", sinT_t[:subs[0][1], :nsub, :], sinT_ap, writes=[B_cst])

        def project(T, subs, x_ap, cos_ap, sin_ap, own, kT_dst, B_kdst, v_dst, B_vdst, q_dst=None, B_qdst=None,
                    zt_out=None, cosT_ap=None, sinT_ap=None, g0=0, par=0, do_loads=True):
            nsub = len(subs)
            xin = xin2[par]; sq = sq2[par]; xg = xg2[par]; B_sq = B_sq2[par]; B_xg = B_xg2[par]
            B_xinl = [B_xin0] if par == 0 else [B_zt[0], B_zt[1]]
            B_csl = [B_cs] if par == 0 else [B_rt]
            cosF_p, sinF_p = cs2[par]
            if do_loads:
                project_loads(T, subs, x_ap, cos_ap, sin_ap, own, cosT_ap, sinT_ap, par)
            P.op("act", actf(sq[:, :, :T], xin[:, :, :T], AF.Square), reads=B_xinl, writes=[B_sq])
            for c in range(8):
                P.op("dve", tsf(xg[:, c, :T], xin[:, c, :T], gcols[:, g0 + c:g0 + c + 1], ALU.mult), reads=B_xinl + [B_c], writes=[B_xg])
            for c in range(8):
                P.op("pe", mm(bank(6, 128, T), ones, sq[:, c, :T], c == 0, c == 7), reads=[B_sq, B_c], writes=[bk[6]])
            for si, (c0, tsz) in enumerate(subs):
                for c in range(8):
                    P.op("pe", mm(ps[:tsz, 7 * 512 + si: 7 * 512 + si + 1], sq[:, c, c0:c0 + tsz], ones[:, 0:1], c == 0, c == 7),
                         reads=[B_sq, B_c], writes=[bk[7]])
            rstd_from(bank(6, 128, T), rstd_b[:, :T], D, 128, T, [bk[6]], [B_rb], lntmp[:, :T] if T <= 128 else junk[:, :T])
            tszm = max(s[1] for s in subs)
            rstd_from(ps[:tszm, 7 * 512: 7 * 512 + nsub], rcol[:tszm, :nsub], D, tszm, nsub, [bk[7]], [B_rc], lntmp[:tszm, :nsub])
            P.op("dve", ttf(cosr[:, :T], cosF_p[:, :T], rstd_b[:, :T], ALU.mult), reads=B_csl + [B_rb], writes=[B_csr])
            P.op("dve", ttf(sinr[:, :T], sinF_p[:, :T], rstd_b[:, :T], ALU.mult), reads=B_csl + [B_rb], writes=[B_csr])

            def fm_plain(wcol, dst, B_dst, sc):
                b = fm_rot[0] % 4; fm_rot[0] += 1
                for c in range(8):
                    P.op("pe", mm(bank(b, 128, T), WBIG[:, c, wcol:wcol + 128], xg[:, c, :T], c == 0, c == 7), reads=[B_W, B_Wq, B_xg], writes=[bk[b]])
                P.op("dve", sttf(dst, bank(b, 128, T), sc, rstd_b[:, :T], ALU.mult, ALU.mult), reads=[bk[b], B_rb], writes=[B_dst])

            def fm_rope(wcol, scol, dst, B_dst, sc):
                b1 = fm_rot[0] % 4; fm_rot[0] += 1
                b2 = fm_rot[0] % 4; fm_rot[0] += 1
                for c in range(8):
                    P.op("pe", mm(bank(b1, 128, T), WBIG[:, c, wcol:wcol + 128], xg[:, c, :T], c == 0, c == 7), reads=[B_W, B_Wq, B_xg], writes=[bk[b1]])
                for c in range(8):
                    P.op("pe", mm(bank(b2, 128, T), WBIG[:, c, scol:scol + 128], xg[:, c, :T], c == 0, c == 7), reads=[B_W, B_Wq, B_xg], writes=[bk[b2]])
                k = t_rot[0] % 2; t_rot[0] += 1
                P.op("dve", sttf(t1[k][:, :T], bank(b1, 128, T), sc, cosr[:, :T], ALU.mult, ALU.mult), reads=[bk[b1], B_csr], writes=[B_t[k]])
                P.op("dve", sttf(t2[k][:, :T], bank(b2, 128, T), sc, sinr[:, :T], ALU.mult, ALU.mult), reads=[bk[b2], B_csr], writes=[B_t[k]])
                P.op("pool", ttf(dst, t1[k][:, :T], t2[k][:, :T], ALU.add), reads=[B_t[k]], writes=[B_dst])

            for g in range(4):
                fm_plain(C_SK + 128 * g, kT_dst(g), B_kdst, 1.0)
            for h in range(4):
                fm_rope(C_DK + 128 * h, C_DKS + 128 * h, kT_dst(4 + h), B_kdst, 1.0)
            if q_dst is not None:
                for g in range(4):
                    fm_plain(C_SQ + 128 * g, q_dst(g), B_qdst, 0.125)
                for h in range(4):
                    fm_rope(C_DQ + 128 * h, C_DQS + 128 * h, q_dst(4 + h), B_qdst, 0.125)
            for si, (c0, tsz) in enumerate(subs):
                if own:
                    zi = zt_rot[0] % 2; zt_rot[0] += 1
                    z = zt[zi]; Bz = B_zt[zi]
                for nb in (range(4) if own else (2, 3)):
                    b = 4 + tm_rot[0] % 2; tm_rot[0] += 1
                    for c in range(8):
                        P.op("pe", mm(bank(b, tsz, 512), xg[:, c, c0:c0 + tsz], WBIG[:, c, nb * 512:(nb + 1) * 512], c == 0, c == 7),
                             reads=[B_W, B_Wq, B_xg], writes=[bk[b]])
                    if own:
                        P.op("act", actf(z[:tsz, nb * 512:(nb + 1) * 512], bank(b, tsz, 512), AF.Copy, scale=rcol[:tsz, si:si + 1]),
                             reads=[bk[b], B_rc], writes=[Bz])
                    else:
                        P.op("act", actf(v_dst(si, tsz)[:, (nb - 2) * 512:(nb - 1) * 512], bank(b, tsz, 512), AF.Copy, scale=rcol[:tsz, si:si + 1]),
                             reads=[bk[b], B_rc], writes=[B_vdst])
                if own:
                    zv = z[:tsz, 0:512].rearrange("p (m h i) -> p m h i", h=2, i=32)
                    x1 = zv[:, :, 0, :]; x2 = zv[:, :, 1, :]
                    cb = cosT_t[:tsz, si, :].unsqueeze(1).to_broadcast([tsz, 8, 32])
                    sbb = sinT_t[:tsz, si, :].unsqueeze(1).to_broadcast([tsz, 8, 32])
                    ta, tb, tc, td = [rtmp[i][:tsz, :].rearrange("p (m i) -> p m i", i=32) for i in range(4)]
                    P.op("dve", ttf(ta, x1, cb, ALU.mult), reads=[Bz, B_cst], writes=[B_rt])
                    P.op("dve", ttf(tb, x2, sbb, ALU.mult), reads=[Bz, B_cst], writes=[B_rt])
                    P.op("dve", ttf(tc, x2, cb, ALU.mult), reads=[Bz, B_cst], writes=[B_rt])
                    P.op("dve", ttf(td, x1, sbb, ALU.mult), reads=[Bz, B_cst], writes=[B_rt])
                    P.op("dve", ttf(x1, ta, tb, ALU.subtract), reads=[B_rt], writes=[Bz])
                    P.op("dve", ttf(x2, tc, td, ALU.add), reads=[B_rt], writes=[Bz])
                    P.op("pool", cpf(v_dst(si, tsz), z[:tsz, 1024:2048]), reads=[Bz], writes=[B_vdst])
                    P.dma("sp", zt_out(si, tsz), z[:tsz, :], reads=[Bz])

        xT_v = dr["xT"].rearrange("(c p) t -> p c t", p=128)
        kst = [kTo, qT]; B_kst = [B_kTo, B_qT]
        vst = [Vo, mixT[:, :, :].rearrange("p (s a) t -> p s (a t)", a=2)]; B_vst = [B_Vo, B_mix]

        def store_scratch(slot, blk0, nblk):
            for g in range(8):
                P.dma("sp", kT_scr[g, :, blk0 * 128:(blk0 + nblk) * 128], kst[slot][:, g, :nblk * 128], reads=[B_kst[slot]])
                vc = 512 + 128 * g if g < 4 else 128 * (g - 4)
                P.dma("sp", v_scr[g, :, blk0:blk0 + nblk, :], vst[slot][:, :nblk, vc:vc + 128], reads=[B_vst[slot]])

        kb = SF[:, 8192:10240].bitcast(BF16).rearrange("p (s n) -> p s n", n=1024)
        B_kb = [B_cs, B_csr]
        cst2 = [(cosT_t, sinT_t, [B_cst]),
                (SF[:, 10240:10368].rearrange("p (s i) -> p s i", i=32), SF[:, 10368:10496].rearrange("p (s i) -> p s i", i=32), [B_rb])]
        p1rot = {"tm": 0, "tr": 0, "zk": 0}

        def p1_loads(i):
            par = (i + 1) % 2
            p0 = NMETA + TQ * i
            B_xinl = [B_xin0] if par == 0 else [B_zt[0], B_zt[1]]
            P.dma("sp", xin2[par][:, :, :], xT_v[:, :, p0:p0 + TQ], writes=B_xinl)
            P.dma("sp", cst2[par][0][:, :, :], dr["cosTp"][i], writes=cst2[par][2])
            P.dma("sp", cst2[par][1][:, :, :], dr["sinTp"][i], writes=cst2[par][2])

        def p1_prep(i):
            par = (i + 1) % 2
            xin = xin2[par]; sq = sq2[par]; xg = xg2[par]; B_sq = B_sq2[par]; B_xg = B_xg2[par]
            B_xinl = [B_xin0] if par == 0 else [B_zt[0], B_zt[1]]
            P.op("act", actf(sq[:, :, :], xin[:, :, :], AF.Square), reads=B_xinl, writes=[B_sq])
            for c in range(8):
                P.op("dve", tsf(xg[:, c, :], xin[:, c, :], gcols[:, c:c + 1], ALU.mult), reads=B_xinl + [B_c], writes=[B_xg])

        def p1_tile(i):
            par = (i + 1) % 2; sl = (i + 1) % 2
            xin = xin2[par]; sq = sq2[par]; xg = xg2[par]; B_sq = B_sq2[par]; B_xg = B_xg2[par]
            cT, sT, B_ct = cst2[par]
            for si in range(4):
                for c in range(8):
                    P.op("pe", mm(ps[:, 7 * 512 + si: 7 * 512 + si + 1], sq[:, c, si * 128:(si + 1) * 128], ones[:, 0:1], c == 0, c == 7),
                         reads=[B_sq, B_c], writes=[bk[7]])
            rstd_from(ps[:, 7 * 512: 7 * 512 + 4], rcol[:, :4], D, 128, 4, [bk[7]], [B_rc])
            for si in range(4):
                for nb in range(4):
                    b_ = p1rot["tm"] % 4; p1rot["tm"] += 1
                    for c in range(8):
                        P.op("pe", mm(bank(b_), xg[:, c, si * 128:(si + 1) * 128], WBIG[:, c, nb * 512:(nb + 1) * 512], c == 0, c == 7),
                             reads=[B_W, B_xg], writes=[bk[b_]])
                    if nb == 0:
                        k_ = p1rot["zk"] % 2; p1rot["zk"] += 1
                        zk = t1[k_]
                        P.op("act", actf(zk, bank(b_), AF.Copy, scale=rcol[:, si:si + 1]), reads=[bk[b_], B_rc], writes=[B_t[k_]])
                        zv = zk.rearrange("p (m h i) -> p m h i", h=2, i=32)
                        kv_ = kb[:, si, 0:512].rearrange("p (m h i) -> p m h i", h=2, i=32)
                        x1 = zv[:, :, 0, :]; x2 = zv[:, :, 1, :]
                        cb = cT[:, si, :].unsqueeze(1).to_broadcast([128, 8, 32])
                        sbb = sT[:, si, :].unsqueeze(1).to_broadcast([128, 8, 32])
                        ta, tb, tc, td = [rtmp[q_][:, :].rearrange("p (m i) -> p m i", i=32) for q_ in range(4)]
                        P.op("dve", ttf(ta, x1, cb, ALU.mult), reads=[B_t[k_]] + B_ct, writes=[B_rt])
                        P.op("dve", ttf(tb, x2, sbb, ALU.mult), reads=[B_t[k_]] + B_ct, writes=[B_rt])
                        P.op("dve", ttf(tc, x2, cb, ALU.mult), reads=[B_t[k_]] + B_ct, writes=[B_rt])
                        P.op("dve", ttf(td, x1, sbb, ALU.mult), reads=[B_t[k_]] + B_ct, writes=[B_rt])
                        P.op("dve", ttf(kv_[:, :, 0, :], ta, tb, ALU.subtract), reads=[B_rt], writes=B_kb)
                        P.op("dve", ttf(kv_[:, :, 1, :], tc, td, ALU.add), reads=[B_rt], writes=B_kb)
                    elif nb == 1:
                        P.op("act", actf(kb[:, si, 512:1024], bank(b_), AF.Copy, scale=rcol[:, si:si + 1]), reads=[bk[b_], B_rc], writes=B_kb)
                    else:
                        P.op("act", actf(vst[sl][:, si, (nb - 2) * 512:(nb - 1) * 512], bank(b_), AF.Copy, scale=rcol[:, si:si + 1]),
                             reads=[bk[b_], B_rc], writes=[B_vst[sl]])
            if i + 1 < 31:
                p1_prep(i + 1)
            for g in range(8):
                fc = 512 + 128 * g if g < 4 else 128 * (g - 4)
                b_ = 4 + p1rot["tr"] % 3; p1rot["tr"] += 1
                for si in range(4):
                    P.op("pe", mm(ps[:, b_ * 512 + si * 128: b_ * 512 + (si + 1) * 128], kb[:, si, fc:fc + 128], ident), reads=B_kb + [B_c], writes=[bk[b_]])
                eng = "dve" if g % 2 == 0 else "act"
                if eng == "dve":
                    P.op("dve", cpf(kst[sl][:, g, :], bank(b_)), reads=[bk[b_]], writes=[B_kst[sl]])
                else:
                    P.op("act", actf(kst[sl][:, g, :], bank(b_), AF.Copy), reads=[bk[b_]], writes=[B_kst[sl]])

        p1_loads(0)
        p1_prep(0)
        for i in range(31):
            sl = (i + 1) % 2
            if i + 1 < 31:
                p1_loads(i + 1)
            p1_tile(i)
            store_scratch(sl, 1 + 4 * i, 4)
            if i == 3:
                load_q_weights()
            if i == 5:
                P.dma("sp", wq_scr[:, :, :], WBIG[:, :, 2048:4096], reads=[B_Wq])
            if 6 <= i < 6 + len(conv_jobs):
                P.dma("pool", conv_jobs[i - 6][0], conv_jobs[i - 6][1])
        P.op("pool", msf(kst[0][:, :, 0:128], 0.0), writes=[B_kst[0]])
        P.op("pool", msf(vst[0][:, 0, :], 0.0), writes=[B_vst[0]])
        project(NMETA, [(0, NMETA)], xT_v[:, :, 0:NMETA], dr["cosF"][:, 0:NMETA], dr["sinF"][:, 0:NMETA], True,
                lambda g: kst[0][:, g, :NMETA], B_kst[0], lambda si, tsz: vst[0][:tsz, 0, :], B_vst[0],
                zt_out=lambda si, tsz: dr["metao"][:, :],
                cosT_ap=dr["cosMT"], sinT_ap=dr["sinMT"])
        store_scratch(0, 0, 1)
        P.barrier()

        def attend(W, q_ap, groups_blocks, finish, mid_hook=None):
            for g in range(8):
                pro, blocks, after = groups_blocks(g)
                pro()
                n = len(blocks)
                if g < 4:
                    S = [(0, 1), (2, 3), (4, 5)]

                    def zmm(i):
                        bl = blocks[i]; s0, s1 = S[i % 3]; nk = bl["nk"]
                        P.op("pe", mm(bank(s0, nk, W), bl["kT"][0:64, :], q_ap(g)[0:64, :]), reads=bl["bufs"] + [B_qT], writes=[bk[s0]])
                        P.op("pe", mm(bank(s1, nk, W), bl["kT"][64:128, :], q_ap(g)[64:128, :]), reads=bl["bufs"] + [B_qT], writes=[bk[s1]])

                    def Sv(i, nk):
                        s0 = S[i % 3][0]
                        return ps[:nk, s0 * 512:(s0 + 2) * 512].rearrange("p (h w) -> p h w", h=2)[:, :, :W]

                    def u_(i):
                        bl = blocks[i]; nk = bl["nk"]; s0, s1 = S[i % 3]
                        P.op("act", actf(u_t[i % 2][:nk, :, :W], Sv(i, nk), AF.Exp, bias=bl["bias"]), reads=[bk[s0], bk[s1], B_c], writes=[B_u[i % 2]])

                    def L_(i):
                        bl = blocks[i]; nk = bl["nk"]
                        P.op("act", actf(L_t[i % 2][:nk, :, :W], u_t[i % 2][:nk, :, :W], AF.Ln, bias=smallc[:nk, 9:10]), reads=[B_u[i % 2], B_small], writes=[B_L[i % 2]])
                        if bl["msb"] is not None:
                            for h in range(2):
                                P.op("dve", ttf(L_t[i % 2][:nk, h, :W], L_t[i % 2][:nk, h, :W], bl["msb"], ALU.mult), reads=[B_L[i % 2], B_c], writes=[B_L[i % 2]])

                    def E_(k):
                        bl = blocks[k]; nk = bl["nk"]; s0, s1 = S[k % 3]
                        if k == 1:
                            P.op("dve", cpf(Lacc[1][:nk, :, :W], L_t[0][:nk, :, :W]), reads=[B_L[0]], writes=[B_Lacc[1]])
                        elif k > 1:
                            P.op("dve", ttf(Lacc[k % 2][:nk, :, :W], Lacc[(k - 1) % 2][:nk, :, :W], L_t[(k - 1) % 2][:nk, :, :W], ALU.add),
                                 reads=[B_L[(k - 1) % 2], B_Lacc[(k - 1) % 2]], writes=[B_Lacc[k % 2]])
                        for h, sb_ in ((0, s0), (1, s1)):
                            P.op("pe", mm(bank(sb_, nk, W), ntri[:nk, :nk], L_t[k % 2][:nk, h, :W], False, k == 0, sgc=True), reads=[B_L[k % 2], B_c], writes=[bk[sb_]])
                            if k > 0:
                                P.op("pe", mm(bank(sb_, nk, W), nones[:nk, :nk], Lacc[k % 2][:nk, h, :W], False, True, sgc=True), reads=[B_Lacc[k % 2], B_c], writes=[bk[sb_]])

                    zmm(0)
                    if n > 1:
                        zmm(1)
                    if n > 2:
                        zmm(2)
                    u_(0); L_(0)
                    if n > 1:
                        u_(1)
                    E_(0)
                    for i in range(n):
                        bl = blocks[i]; nk = bl["nk"]; s0, s1 = S[i % 3]
                        if i + 1 < n:
                            L_(i + 1)
                        if i + 2 < n:
                            u_(i + 2)
                        P.op("act", actf(A_t[i % 2][:nk, :, :W], Sv(i, nk), AF.Exp, bias=bl["bias"]), reads=[bk[s0], bk[s1], B_c], writes=[B_A[i % 2]])
                        if bl["msb"] is not None:
                            for h in range(2):
                                P.op("dve" if h == 0 else "pool", ttf(A_t[i % 2][:nk, h, :W], A_t[i % 2][:nk, h, :W], bl["msb"], ALU.mult), reads=[B_A[i % 2], B_c], writes=[B_A[i % 2]])
                        if i + 3 < n:
                            zmm(i + 3)
                        if i + 1 < n:
                            E_(i + 1)
                        for h in range(2):
                            P.op("pe", mm(bank(6 + h, 128, W), bl["v"], A_t[i % 2][:nk, h, :W], i == 0, i == n - 1), reads=bl["bufs"] + [B_A[i % 2]], writes=[bk[6 + h]])
                        after(i)
                else:
                    S = [(0, 1), (2, 3)]
                    A3 = [A_t[0], A_t[1], Lacc[0]]; B_A3 = [B_A[0], B_A[1], B_Lacc[0]]

                    def qk(i):
                        bl = blocks[i]; s0, s1 = S[i % 2]; nk = bl["nk"]
                        P.op("pe", mm(bank(s0, nk, W), bl["kT"][0:64, :], q_ap(g)[0:64, :]), reads=bl["bufs"] + [B_qT], writes=[bk[s0]])
                        P.op("pe", mm(bank(s1, nk, W), bl["kT"][64:128, :], q_ap(g)[64:128, :]), reads=bl["bufs"] + [B_qT], writes=[bk[s1]])

                    qk(0)
                    if n > 1:
                        qk(1)
                    for i in range(n):
                        bl = blocks[i]; nk = bl["nk"]; s0, s1 = S[i % 2]
                        At = A3[i % 3]; BAt = B_A3[i % 3]
                        Sv_ = ps[:nk, s0 * 512:(s0 + 2) * 512].rearrange("p (h w) -> p h w", h=2)[:, :, :W]
                        P.op("act", actf(At[:nk, :, :W], Sv_, AF.Exp, bias=bl["bias"]), reads=[bk[s0], bk[s1], B_c], writes=[BAt])
                        if bl["mdf"] is not None:
                            for h in range(2):
                                P.op("dve" if h == 0 else "pool", ttf(At[:nk, h, :W], At[:nk, h, :W], bl["mdf"], ALU.mult), reads=[BAt, B_c], writes=[BAt])
                        if i + 2 < n:
                            qk(i + 2)
                        for c in range(2):
                            P.op("pe", mm(bank(4 + 2 * c, 128, W), bl["v"], At[:nk, c, :W], i == 0, i == n - 1), reads=bl["bufs"] + [BAt], writes=[bk[4 + 2 * c]])
                        P.op("pe", mm(bank(5, 128, W), ones[:nk, :], At[:nk, 0, :W], i == 0, i == n - 1), reads=[BAt, B_c], writes=[bk[5]])
                        if i == 0:
                            if nk < 128:
                                P.op("dve", msf(Pacc[1][:, :W], 0.0), writes=[B_pa[1]])
                            P.op("dve", cpf(Pacc[1][:nk, :W], At[:nk, 1, :W]), reads=[BAt], writes=[B_pa[1]])
                        else:
                            P.op("dve", ttf(Pacc[1][:nk, :W], Pacc[1][:nk, :W], At[:nk, 1, :W], ALU.add), reads=[BAt], writes=[B_pa[1]])
                        after(i)
                    hi = L_t[0][:, 1, :W]; lo = L_t[1][:, 1, :W]
                    P.op("dve", cpf(hi, Pacc[1][:, :W]), reads=[B_pa[1]], writes=[B_L[0]])
                    P.op("dve", ttf(lo, Pacc[1][:, :W], hi, ALU.subtract), reads=[B_pa[1], B_L[0]], writes=[B_L[1]])
                    P.op("pe", mm(bank(7, 128, W), ones, hi, True, False), reads=[B_L[0], B_c], writes=[bk[7]])
                    P.op("pe", mm(bank(7, 128, W), ones, lo, False, True), reads=[B_L[1], B_c], writes=[bk[7]])
                finish(g)
                if g == 0 and mid_hook is not None:
                    mid_hook()

        def make_finish(W, mix_dst):
            def finish(g):
                if g < 4:
                    P.op("dve", cpf(mix_dst(4 + g)[0:64, :], bank(6, 128, W)[0:64, :]), reads=[bk[6]], writes=[B_mix])
                    P.op("dve", cpf(mix_dst(4 + g)[64:128, :], bank(7, 128, W)[64:128, :]), reads=[bk[7]], writes=[B_mix])
                else:
                    h = g - 4
                    rl0, rl1, o0, o1, od, rs = [d_[:, :W] for d_ in dtmp]
                    P.op("act", actf(rl0, bank(5, 128, W), AF.Ln), reads=[bk[5]], writes=[B_dt])
                    P.op("act", actf(rl0, rl0, AF.Exp, scale=-1.0), reads=[B_dt], writes=[B_dt])
                    P.op("act", actf(rl1, bank(7, 128, W), AF.Ln), reads=[bk[7]], writes=[B_dt])
                    P.op("act", actf(rl1, rl1, AF.Exp, scale=-1.0), reads=[B_dt], writes=[B_dt])
                    P.op("dve", ttf(o0, bank(4, 128, W), rl0, ALU.mult), reads=[bk[4], B_dt], writes=[B_dt])
                    P.op("dve", ttf(o1, bank(6, 128, W), rl1, ALU.mult), reads=[bk[6], B_dt], writes=[B_dt])
                    P.op("dve", sttf(od, o1, smallc[:, 1:2], o0, ALU.mult, ALU.add), reads=[B_dt, B_small], writes=[B_dt])
                    P.op("act", actf(A_t[0][:, 0, :W], od, AF.Square), reads=[B_dt], writes=[B_A[0]])
                    P.op("pe", mm(bank(0, 128, W), ones, A_t[0][:, 0, :W]), reads=[B_A[0], B_c], writes=[bk[0]])
                    rstd_from(bank(0, 128, W), rs, 128, 128, W, [bk[0]], [B_dt], rl0)
                    P.op("dve", sttf(mix_dst(h), od, smallc[:, 2:3], rs, ALU.mult, ALU.mult), reads=[B_dt, B_small], writes=[B_mix])
            return finish

        def mlp_loads(T, subs, xq_ap, xqT_ap):
            P.dma("sp", r_t[:subs[0][1], :len(subs), :], xq_ap, writes=[B_r])
            P.dma("sp", xqT_t[:, :, :T], xqT_ap, writes=[B_xqT])

        def mlp(T, subs, xq_ap, xqT_ap, y_out, preloaded=False, after_down=None):
            nsub = len(subs)
            if not preloaded:
                mlp_loads(T, subs, xq_ap, xqT_ap)
            w_out_v = dr["w_out"].rearrange("(c p) n -> p c n", p=128)
            wo = [wbuf[i].rearrange("p (c n) -> p c n", n=512) for i in range(2)]
            for i in range(2):
                P.dma("sp", wo[i], wo_scr[i], writes=[B_wb[i]])
            P.op("dve", msf(mcol[:, :], 0.0), writes=[B_mc])
            rot = 0
            for si, (c0, tsz) in enumerate(subs):
                for nh in range(2):
                    b = rot % 2; rot += 1
                    for c in range(8):
                        P.op("pe", mm(bank(b, tsz, 512), mixT[:, c, c0:c0 + tsz], wo[nh][:, c, :], c == 0, c == 7), reads=[B_mix, B_wb[nh]], writes=[bk[b]])
                    P.op("dve", ttf(r_t[:tsz, si, nh * 512:(nh + 1) * 512], bank(b, tsz, 512), r_t[:tsz, si, nh * 512:(nh + 1) * 512], ALU.add),
                         reads=[bk[b], B_r], writes=[B_r])
                P.op("act", actf(junk[:tsz, :], r_t[:tsz, si, :], AF.Square, accum_out=mcol[:tsz, si:si + 1]), reads=[B_r, B_mc], writes=[B_mc, B_rt])
            tszm = max(s[1] for s in subs)
            rstd_from(mcol[:tszm, 0:nsub], mcol[:tszm, 8:8 + nsub], D, tszm, nsub, [B_mc], [B_mc], lntmp[:tszm, :nsub])
            P.op("dve", ttf(mcol[:tszm, 16:16 + nsub], mcol[:tszm, 8:8 + nsub], mcol[:tszm, 8:8 + nsub], ALU.mult), reads=[B_mc], writes=[B_mc])
            for nchunk in range(8):
                b = 2 + rot % 2; rot += 1
                for c in range(8):
                    P.op("pe", mm(bank(b, 128, T), wo[nchunk // 4][:, c, (nchunk % 4) * 128:(nchunk % 4 + 1) * 128], mixT[:, c, :T], c == 0, c == 7),
                         reads=[B_mix, B_wb[nchunk // 4]], writes=[bk[b]])
                k = nchunk % 2
                P.op("dve", ttf(x1tmp[k][:, :T], bank(b, 128, T), xqT_t[:, nchunk, :T], ALU.add), reads=[bk[b], B_xqT], writes=[B_x1t[k]])
                P.op("act", actf(x1g[:, nchunk, :T], x1tmp[k][:, :T], AF.Copy, scale=gcols[:, 8 + nchunk:9 + nchunk]), reads=[B_x1t[k], B_c], writes=[B_x1g])
            w_up_v = dr["w_up"].rearrange("(c p) n -> p c n", p=128)
            for pc in range(8):
                wi = pc % 2
                wu = wbuf[wi].rearrange("p (c n) -> p c n", n=512)
                P.dma("sp", wu, wu_scr[pc], writes=[B_wb[wi]])
                for mc in range(4):
                    m = pc * 4 + mc
                    b = 4 + rot % 4; rot += 1
                    for c in range(8):
                        P.op("pe", mm(bank(b, 128, T), wu[:, c, mc * 128:(mc + 1) * 128], x1g[:, c, :T], c == 0, c == 7), reads=[B_x1g, B_wb[wi]], writes=[bk[b]])
                    k = m % 2
                    P.op("act", actf(rl[k][:, :T], bank(b, 128, T), AF.Relu), reads=[bk[b]], writes=[B_rl[k]])
                    a_m = WBIG[:, m // 4, 2048 + (m % 4) * 512: 2048 + (m % 4) * 512 + T]
                    P.op("dve", ttf(a_m, rl[k][:, :T], rl[k][:, :T], ALU.mult), reads=[B_rl[k]], writes=[B_Wq])
            w_dn_v = dr["w_down"].rearrange("(m p) n -> p m n", p=128)
            for pc in range(8):
                wi = pc % 2
                wd = wbuf[wi].rearrange("p (m n) -> p m n", n=1024)
                P.dma("sp", wd, wd_scr[pc], writes=[B_wb[wi]])
                for mc in range(4):
                    m = pc * 4 + mc
                    a_m = WBIG[:, m // 4, 2048 + (m % 4) * 512: 2048 + (m % 4) * 512 + T]
                    for si, (c0, tsz) in enumerate(subs):
                        for nh in range(2):
                            b = 2 * si + nh
                            P.op("pe", mm(bank(b, tsz, 512), a_m[:, c0:c0 + tsz], wd[:, mc, nh * 512:(nh + 1) * 512], m == 0, m == 31),
                                 reads=[B_Wq, B_wb[wi]], writes=[bk[b]])
            if after_down is not None:
                after_down()
            for si, (c0, tsz) in enumerate(subs):
                for nh in range(2):
                    b = 2 * si + nh
                    P.op("dve", sttf(r_t[:tsz, si, nh * 512:(nh + 1) * 512], bank(b, tsz, 512), mcol[:tsz, 16 + si:17 + si], r_t[:tsz, si, nh * 512:(nh + 1) * 512], ALU.mult, ALU.add),
                         reads=[bk[b], B_mc, B_r], writes=[B_r])
                P.op("act", actf(junk[:tsz, :], r_t[:tsz, si, :], AF.Square, accum_out=mcol[:tsz, 4 + si:5 + si]), reads=[B_r, B_mc], writes=[B_mc, B_rt])
            rstd_from(mcol[:tszm, 4:4 + nsub], mcol[:tszm, 12:12 + nsub], D, tszm, nsub, [B_mc], [B_mc], lntmp[:tszm, :nsub])
            for si, (c0, tsz) in enumerate(subs):
                P.op("dve", sttf(r_t[:tsz, si, :], r_t[:tsz, si, :], mcol[:tsz, 12 + si:13 + si], gfin[:tsz, :], ALU.mult, ALU.mult), reads=[B_r, B_mc, B_c], writes=[B_r])
            P.dma("sp", y_out, r_t[:subs[0][1], :nsub, :], reads=[B_r])

        P.op("dve", msf(smallc[:, 9:10], 1.0), writes=[B_small])

        subs4 = [(s * 128, 128) for s in range(4)]
        for j in range(NSLOT):
            if j > 0:
                P.dma("sp", cosF_t[:, :TQ], dr["cosQ"][j], writes=[B_cs])
                P.dma("sp", sinF_t[:, :TQ], dr["sinQ"][j], writes=[B_cs])
                P.dma("sp", cosT_t[:, :4, :], dr["cosT"][j], writes=[B_cst])
                P.dma("sp", sinT_t[:, :4, :], dr["sinT"][j], writes=[B_cst])
            project(TQ, subs4, dr["xqT"][j].rearrange("(c p) t -> p c t", p=128), dr["cosQ"][j], dr["sinQ"][j], True,
                    lambda g: kTo[:, g, :], B_kTo, lambda si, tsz: Vo[:, si, :], B_Vo,
                    q_dst=lambda g: qT[:, g, :], B_qdst=B_qT,
                    zt_out=lambda si, tsz, j=j: dr["kvo"][j, si * 128:(si + 1) * 128, :],
                    cosT_ap=dr["cosT"][j], sinT_ap=dr["sinT"][j], do_loads=(j == 0))
            P.barrier()
            NB = 32 * j + 29
            nseg = (NB + 15) // 16
            segs = list(range(nseg - 1, -1, -1))
            items = [(g, s_) for g in range(8) for s_ in segs]
            loaded = {"n": 0}

            def load_item(t, NB=NB):
                g, s_ = items[t]
                sl = t % 3
                nb = min(16, NB - 16 * s_)
                P.dma("sp", Kseg[sl][:, :nb * 128], kT_scr[g, :, s_ * 2048: s_ * 2048 + nb * 128], writes=[B_seg[sl]])
                P.dma("sp", Vseg[sl][:, :nb, :], v_scr[g, :, 16 * s_:16 * s_ + nb, :], writes=[B_seg[sl]])

            def groups_blocks(g, j=j, NB=NB, nseg=nseg, segs=segs, items=items, loaded=loaded, load_item=load_item):
                def pro():
                    while loaded["n"] < min(3, len(items)):
                        load_item(loaded["n"]); loaded["n"] += 1

                vc = 512 + 128 * g if g < 4 else 128 * (g - 4)
                blocks = []
                for r in (3, 2, 1, 0):
                    blocks.append(dict(kT=kTo[:, g, r * 128:(r + 1) * 128], v=Vo[:, r, vc:vc + 128], bias=kbias[:, ZB:ZB + 1], nk=128,
                                       msb=dmask[:, r, :], mdf=dmask[:, 4 + r, :], bufs=[B_kTo, B_Vo], item=None, last=False))
                for sidx, s_ in enumerate(segs):
                    t = g * nseg + sidx
                    sl = t % 3
                    nb = min(16, NB - 16 * s_)
                    for bb in range(nb - 1, -1, -1):
                        b_ = 16 * s_ + bb
                        blocks.append(dict(kT=Kseg[sl][:, bb * 128:(bb + 1) * 128], v=Vseg[sl][:, bb, :], bias=kbias[:, j * NBLK + b_: j * NBLK + b_ + 1], nk=128,
                                           msb=None, mdf=None, bufs=[B_seg[sl]], item=t, last=(bb == 0)))

                def after(i):
                    bl = blocks[i]
                    if bl["item"] is not None and bl["last"] and loaded["n"] < len(items):
                        load_item(loaded["n"]); loaded["n"] += 1
                return pro, blocks, after

            xq_ap_j = dr["xq"][j].rearrange("(s p) n -> p s n", p=128)
            xqT_ap_j = dr["xqT"][j].rearrange("(c p) t -> p c t", p=128)
            attend(TQ, lambda g: qT[:, g, :], groups_blocks, make_finish(TQ, lambda ch: mixT[:, ch, :]),
                   mid_hook=lambda: mlp_loads(TQ, subs4, xq_ap_j, xqT_ap_j))
            P.barrier()
            def next_prefetch(j=j):
                if j + 1 < NSLOT:
                    P.dma("sp", WBIG[:, :, 2048:4096], wq_scr[:, :, :], writes=[B_Wq])
            if j + 1 < NSLOT:
                P.dma("sp", xin2[0][:, :, :TQ], dr["xqT"][j + 1].rearrange("(c p) t -> p c t", p=128), writes=[B_xin0])
            mlp(TQ, subs4, xq_ap_j, xqT_ap_j, dr["y"][j].rearrange("(s p) n -> p s n", p=128), preloaded=True, after_down=next_prefetch)
            P.barrier()

        P.dma("sp", WBIG[:, :, 2048:4096], wq_scr[:, :, :], writes=[B_Wq])
        subs_s = [(b * DEC_T, DEC_T) for b in range(BPC)]
        project(ST, subs_s, dr["xsT"].rearrange("(c p) t -> p c t", p=128), dr["cosS"][:, :], dr["sinS"][:, :], True,
                lambda g: kTo[:, g, :ST], B_kTo, lambda si, tsz: Vo[:tsz, si, :], B_Vo,
                q_dst=lambda g: qT[:, g, :ST], B_qdst=B_qT,
                zt_out=lambda si, tsz: dr["kvs"][si * DEC_T:(si + 1) * DEC_T, :],
                cosT_ap=dr["cosST"], sinT_ap=dr["sinST"])
        P.barrier()
        ckb = [SB[:, i * 8192:(i + 1) * 8192].rearrange("p (b n) -> p b n", n=1024) for i in range(2)]
        cvb = [WBIG[:, :, 2048 + i * 1024: 3072 + i * 1024] for i in range(2)]
        kTcb = [SF[:, 6144 + i * 4096: 10240 + i * 4096].bitcast(BF16).rearrange("p (g t) -> p g t", t=1024) for i in range(2)]
        B_ck = [Buf(), Buf()]; B_cv = [Buf(), Buf()]; B_kTc = [Buf(), Buf()]
        sm = SB[:, 16384:18432]
        L_f = [sm[:, i * 128:(i + 1) * 128] for i in range(2)]; B_Lf = [Buf(), Buf()]
        A_f = [sm[:, 256 + i * 128: 256 + (i + 1) * 128] for i in range(2)]; B_Af = [Buf(), Buf()]
        La_f = [sm[:, 512 + i * 128: 512 + (i + 1) * 128] for i in range(2)]; B_Laf = [Buf(), Buf()]
        P_f = [sm[:, 768 + i * 128: 768 + (i + 1) * 128] for i in range(3)]; B_Pf = [Buf(), Buf(), Buf()]
        sq_s = sm[:, 1152:1216]; B_sqs = Buf()
        u_f = [SF[:, i * 128:(i + 1) * 128] for i in range(2)]; B_uf = [Buf(), Buf()]
        mKV = SF[:, 3072:4096].bitcast(BF16)
        metaK = mKV[:, 0:1024].rearrange("p (g k) -> p g k", k=128); metaV = mKV[:, 1024:2048].rearrange("p (g k) -> p g k", k=128); B_mkv = Buf()
        rl_s = SF[:, 2048:2176]; o_s = SF[:, 2176:2304]; od_s = SF[:, 2304:2368]; rs_s = SF[:, 2368:2432]; B_fs = Buf()
        qz = SF[:, 2560:3072].bitcast(BF16).rearrange("p (m t) -> p m t", t=ST); B_qz = Buf()
        P.op("pool", msf(qz[:, :, :], 0.0), writes=[B_qz])
        qzs = qz[:, 0:8, :].rearrange("p (g two) t -> p g two t", two=2)
        qzd = qz[:, 8:16, :].rearrange("p (g two) t -> p g two t", two=2)
        P.op("pool", cpf(qzs[0:64, :, 0, :], qT[0:64, 0:4, 0:ST]), reads=[B_qT], writes=[B_qz])
        P.op("pool", cpf(qzs[64:128, :, 1, :], qT[64:128, 0:4, 0:ST]), reads=[B_qT], writes=[B_qz])
        P.op("pool", cpf(qzd[0:64, :, 0, :], qT[0:64, 4:8, 0:ST]), reads=[B_qT], writes=[B_qz])
        P.op("pool", cpf(qzd[64:128, :, 1, :], qT[64:128, 4:8, 0:ST]), reads=[B_qT], writes=[B_qz])
        for g in range(8):
            P.dma("sp", metaK[:, g, :], kT_scr[g, :, 0:128], writes=[B_mkv])
            P.dma("sp", metaV[:, g, :], v_scr[g, :, 0, :], writes=[B_mkv])

        def s_loads(bi):
            p_ = bi % 2
            for (nm, off) in (("cdk", 0), ("csk", 512)):
                P.dma("pool", ckb[p_][:, :, off:off + 512], dr[nm][bi].rearrange("(b p) n -> p b n", p=128), writes=[B_ck[p_]])
            for (nm, off) in (("cdv", 0), ("csv", 512)):
                P.dma("pool", cvb[p_][:, :, off:off + 512], dr[nm][bi].rearrange("(b p) n -> p b n", p=128), writes=[B_cv[p_], B_Wq])

        def s_transposes(bi):
            p_ = bi % 2
            rotk = 0
            for g in range(8):
                fc = 512 + 128 * g if g < 4 else 128 * (g - 4)
                for half in range(2):
                    b_ = rotk % 2; rotk += 1
                    for q4 in range(4):
                        blk = half * 4 + q4
                        P.op("pe", mm(ps[:, b_ * 512 + q4 * 128: b_ * 512 + (q4 + 1) * 128], ckb[p_][:, blk, fc:fc + 128], ident), reads=[B_ck[p_], B_c], writes=[bk[b_]])
                    P.op("dve", cpf(kTcb[p_][:, g, half * 512:(half + 1) * 512], bank(b_)), reads=[bk[b_]], writes=[B_kTc[p_]])

        W16 = DEC_T

        def s_attend(bi):
            p_ = bi % 2
            q0 = bi * W16
            vcol = lambda g: 512 + 128 * g if g < 4 else 128 * (g - 4)
            blocks = [dict(kT=lambda g: kTo[:, g, q0:q0 + W16], v=lambda g: Vo[:W16, bi, vcol(g):vcol(g) + 128], bias=kbias[:W16, ZB:ZB + 1], nk=W16,
                           mask=True, bufs=[B_kTo, B_Vo])]
            for blk in range(7, -1, -1):
                blocks.append(dict(kT=lambda g, blk=blk: kTcb[p_][:, g, blk * 128:(blk + 1) * 128], v=lambda g, blk=blk: cvb[p_][:, blk, vcol(g):vcol(g) + 128],
                                   bias=kbias[:, ZB:ZB + 1], nk=128, mask=False, bufs=[B_kTc[p_], B_cv[p_]]))
            blocks.append(dict(kT=lambda g: metaK[:, g, :], v=lambda g: metaV[:, g, :], bias=kbias[:, MB:MB + 1], nk=128, mask=False, bufs=[B_mkv]))
            n = len(blocks)
            SS = [0, 1, 2]; SD = [3, 4, 5]
            mask3 = dmask[:W16, 0, :W16].unsqueeze(1).to_broadcast([W16, 8, W16])

            def zmm(i):
                bl = blocks[i]; nk = bl["nk"]; sb_ = SS[i % 3]
                for h in range(8):
                    g = h // 2
                    P.op("pe", mm(ps[:nk, sb_ * 512 + h * W16: sb_ * 512 + (h + 1) * W16], bl["kT"](g), qz[:, h, q0:q0 + W16], h == 0, h == 7),
                         reads=bl["bufs"] + [B_qz], writes=[bk[sb_]])

            def uL(i):
                bl = blocks[i]; nk = bl["nk"]; sb_ = SS[i % 3]
                P.op("act", actf(u_f[i % 2][:nk, :], bank(sb_, nk, 128), AF.Exp, bias=bl["bias"]), reads=[bk[sb_], B_c], writes=[B_uf[i % 2]])
                P.op("act", actf(L_f[i % 2][:nk, :], u_f[i % 2][:nk, :], AF.Ln, bias=smallc[:nk, 9:10]), reads=[B_uf[i % 2], B_small], writes=[B_Lf[i % 2]])
                if bl["mask"]:
                    Lv = L_f[i % 2][:nk, :].rearrange("p (h w) -> p h w", w=W16)
                    P.op("dve", ttf(Lv, Lv, mask3, ALU.mult), reads=[B_Lf[i % 2], B_c], writes=[B_Lf[i % 2]])

            def qk(i):
                bl = blocks[i]; nk = bl["nk"]; sd_ = SD[i % 3]
                for m in range(8):
                    g = 4 + m // 2
                    P.op("pe", mm(ps[:nk, sd_ * 512 + m * W16: sd_ * 512 + (m + 1) * W16], bl["kT"](g), qz[:, 8 + m, q0:q0 + W16], m == 0, m == 7),
                         reads=bl["bufs"] + [B_qz], writes=[bk[sd_]])

            zmm(0); qk(0)
            if n > 1:
                zmm(1); qk(1)
            uL(0)
            for i in range(n):
                bl = blocks[i]; nk = bl["nk"]; sb_ = SS[i % 3]; sd_ = SD[i % 3]
                P.op("act", actf(P_f[i % 3][:nk, :], bank(sd_, nk, 128), AF.Exp, bias=bl["bias"]), reads=[bk[sd_], B_c], writes=[B_Pf[i % 3]])
                if i + 2 < n:
                    zmm(i + 2)
                P.op("pe", mm(bank(sb_, nk, 128), ntri[:nk, :nk], L_f[i % 2][:nk, :], False, i == 0, sgc=True), reads=[B_Lf[i % 2], B_c], writes=[bk[sb_]])
                if i > 0:
                    pk = blocks[i - 1]["nk"]
                    P.op("pe", mm(bank(sb_, nk, 128), nones[:pk, :nk], La_f[i % 2][:pk, :], False, True, sgc=True), reads=[B_Laf[i % 2], B_c], writes=[bk[sb_]])
                if i + 1 < n:
                    if i == 0:
                        P.op("dve", cpf(La_f[1][:nk, :], L_f[0][:nk, :]), reads=[B_Lf[0]], writes=[B_Laf[1]])
                    else:
                        pk = blocks[i - 1]["nk"]
                        if pk < nk:
                            P.op("dve", cpf(La_f[(i + 1) % 2][:nk, :], L_f[i % 2][:nk, :]), reads=[B_Lf[i % 2]], writes=[B_Laf[(i + 1) % 2]])
                            P.op("dve", ttf(La_f[(i + 1) % 2][:pk, :], La_f[(i + 1) % 2][:pk, :], La_f[i % 2][:pk, :], ALU.add), reads=[B_Laf[i % 2]], writes=[B_Laf[(i + 1) % 2]])
                        else:
                            P.op("dve", ttf(La_f[(i + 1) % 2][:nk, :], La_f[i % 2][:nk, :], L_f[i % 2][:nk, :], ALU.add), reads=[B_Lf[i % 2], B_Laf[i % 2]], writes=[B_Laf[(i + 1) % 2]])
                    uL(i + 1)
                if i + 2 < n:
                    qk(i + 2)
                for m in range(8):
                    g = 4 + m // 2
                    P.op("pe", mm(ps[:, 7 * 512 + m * W16: 7 * 512 + (m + 1) * W16], bl["v"](g), P_f[i % 3][:nk, m * W16:(m + 1) * W16], i == 0 and m == 0, False),
                         reads=bl["bufs"] + [B_Pf[i % 3]], writes=[bk[7]])
                P.op("pe", mm(ps[:, 7 * 512 + 128: 7 * 512 + 256], ones[:nk, :], P_f[i % 3][:nk, :], False, i == n - 1), reads=[B_Pf[i % 3], B_c], writes=[bk[7]])
                P.op("act", actf(A_f[i % 2][:nk, :], bank(sb_, nk, 128), AF.Exp, bias=bl["bias"]), reads=[bk[sb_], B_c], writes=[B_Af[i % 2]])
                if bl["mask"]:
                    Av = A_f[i % 2][:nk, :].rearrange("p (h w) -> p h w", w=W16)
                    P.op("dve", ttf(Av, Av, mask3, ALU.mult), reads=[B_Af[i % 2], B_c], writes=[B_Af[i % 2]])
                for h in range(8):
                    g = h // 2
                    P.op("pe", mm(ps[:, 6 * 512 + h * W16: 6 * 512 + (h + 1) * W16], bl["v"](g), A_f[i % 2][:nk, h * W16:(h + 1) * W16], i == 0 and h == 0, i == n - 1 and h == 7),
                         reads=bl["bufs"] + [B_Af[i % 2]], writes=[bk[6]])
            acc6 = ps[:, 6 * 512: 6 * 512 + 128].rearrange("p (g two w) -> p g two w", two=2, w=W16)
            P.op("dve", cpf(mixT[0:64, 4:8, q0:q0 + W16], acc6[0:64, :, 0, :]), reads=[bk[6]], writes=[B_mix])
            P.op("dve", cpf(mixT[64:128, 4:8, q0:q0 + W16], acc6[64:128, :, 1, :]), reads=[bk[6]], writes=[B_mix])
            P.op("dve", lambda e: e.reciprocal(out=rl_s, in_=ps[:, 7 * 512 + 128: 7 * 512 + 256]), reads=[bk[7]], writes=[B_fs])
            P.op("dve", ttf(o_s, ps[:, 7 * 512: 7 * 512 + 128], rl_s, ALU.mult), reads=[bk[7], B_fs], writes=[B_fs])
            ov = o_s.rearrange("p (h c w) -> p h c w", c=2, w=W16)
            odv = od_s.rearrange("p (h w) -> p h w", w=W16)
            P.op("dve", sttf(odv, ov[:, :, 1, :], smallc[:, 1:2], ov[:, :, 0, :], ALU.mult, ALU.add), reads=[B_fs, B_small], writes=[B_fs])
            P.op("act", actf(sq_s, od_s, AF.Square), reads=[B_fs], writes=[B_sqs])
            P.op("pe", mm(ps[:, 3 * 512: 3 * 512 + 64], ones, sq_s), reads=[B_sqs, B_c], writes=[bk[3]])
            rstd_from(ps[:, 3 * 512: 3 * 512 + 64], rs_s, 128, 128, 64, [bk[3]], [B_fs])
            P.op("dve", sttf(mixT[:, 0:4, q0:q0 + W16], odv, smallc[:, 2:3], rs_s.rearrange("p (h w) -> p h w", w=W16), ALU.mult, ALU.mult),
                 reads=[B_fs, B_small], writes=[B_mix])

        s_loads(0)
        s_loads(1)
        s_transposes(0)
        for bi in range(BPC):
            s_attend(bi)
            if bi + 2 < BPC:
                s_loads(bi + 2)
            if bi + 1 < BPC:
                s_transposes(bi + 1)
        P.barrier()
        mlp(ST, [(0, ST)], dr["xs"][:, :].unsqueeze(1), dr["xsT"].rearrange("(c p) t -> p c t", p=128), dr["ys"][:, :].unsqueeze(1))
        P.barrier()

        with nc.Block() as block:
            @block.tensor
            def _(e):
                P.replay("pe", e, sems, dsems)

            @block.scalar
            def _(e):
                P.replay("act", e, sems, dsems)

            @block.vector
            def _(e):
                P.replay("dve", e, sems, dsems)

            @block.gpsimd
            def _(e):
                P.replay("pool", e, sems, dsems)

            @block.sync
            def _(e):
                P.replay("sp", e, sems, dsems)
    return nc


_NC = None


def _rope_tables(pos):
    half = 32
    inv = (np.float32(10000.0) ** (-(np.arange(half, dtype=np.float32) / np.float32(half)))).astype(np.float32)
    ang = (pos.astype(np.float32)[:, None] * inv[None, :]).astype(np.float32)
    return np.cos(ang).astype(np.float32), np.sin(ang).astype(np.float32)


def _fmajor(cos, sin):
    p = np.arange(128); d = p % 64; f = d % 32
    sgn = np.where(d < 32, -1.0, 1.0).astype(np.float32)
    return np.ascontiguousarray(cos[:, f].T), np.ascontiguousarray((sin[:, f] * sgn[None, :]).T)


def kernel(x_prompt, x_sample, cache_diff_k, cache_diff_v, cache_sb_k, cache_sb_v, meta_tokens, g_mix, w_in,
           lambda_q1, lambda_k1, lambda_q2, lambda_k2, g_diff_head, w_out, g_mlp, w_up, w_down, g_final):
    global _NC
    f32 = np.float32
    bf = ml_dtypes.bfloat16
    A = lambda a: np.ascontiguousarray(np.asarray(a, dtype=f32))
    xp = A(x_prompt)[0]
    meta = A(meta_tokens)
    xT = np.ascontiguousarray(np.concatenate([meta, xp], axis=0).T)
    cosP, sinP = _rope_tables(np.arange(TP))
    cosF, sinF = _fmajor(cosP, sinP)
    gcols = np.concatenate([A(g_mix)[0].reshape(8, 128).T, A(g_mlp)[0].reshape(8, 128).T], axis=1)
    gfin = np.ascontiguousarray(np.broadcast_to(A(g_final)[None, :], (128, D)))
    ghead = A(g_diff_head)[0].reshape(128, 1)
    lamv = np.ascontiguousarray(np.broadcast_to(np.concatenate([A(lambda_q1)[0], A(lambda_k1)[0], A(lambda_q2)[0], A(lambda_k2)[0]])[None, :], (128, 256)))
    k = np.arange(128)[:, None]; q = np.arange(512)[None, :]
    dmask = np.zeros((128, 8, 512), f32)
    for r in range(4):
        dmask[:, r, :] = (128 * r + k < q)
        dmask[:, 4 + r, :] = ((128 * r + k) // 64 <= q // 64)
    dmask = dmask.astype(bf)
    consts = np.zeros((128, 512), f32)
    consts[:, 0:128] = -((np.arange(128)[:, None] >= np.arange(128)[None, :]).astype(f32))
    consts[:, 128:256] = -1.0
    consts[:, 256:384] = 1.0
    consts[:, 384:512] = np.eye(128)
    consts = consts.astype(bf)
    cosS_, sinS_ = _rope_tables(NMETA + PAST + np.arange(DEC_T))
    cosSF, sinSF = _fmajor(np.tile(cosS_, (BPC, 1)), np.tile(sinS_, (BPC, 1)))
    xs_all = A(x_sample)
    cdk = A(cache_diff_k)[0].reshape(DEC_B, PAST, 512); csk = A(cache_sb_k)[0].reshape(DEC_B, PAST, 512)
    cdv = A(cache_diff_v)[0].reshape(DEC_B, PAST, 512); csv = A(cache_sb_v)[0].reshape(DEC_B, PAST, 512)
    shared = dict(xT=xT, w_in=A(w_in)[0], w_out=A(w_out)[0], w_up=A(w_up)[0], w_down=A(w_down)[0], gcols=np.ascontiguousarray(gcols),
                  gfin=gfin, ghead=np.ascontiguousarray(ghead), lamv=lamv, cosF=cosF, sinF=sinF, dmask=dmask, consts=consts,
                  cosS=cosSF, sinS=sinSF, cosST=np.ascontiguousarray(np.broadcast_to(cosS_[:, None, :], (DEC_T, BPC, 32))), sinST=np.ascontiguousarray(np.broadcast_to(sinS_[:, None, :], (DEC_T, BPC, 32))),
                  cosMT=np.ascontiguousarray(cosP[0:NMETA, None, :]), sinMT=np.ascontiguousarray(sinP[0:NMETA, None, :]),
                  cosTp=np.ascontiguousarray(cosP[NMETA:NMETA + 31 * TQ].reshape(31, 4, 128, 32).transpose(0, 2, 1, 3)),
                  sinTp=np.ascontiguousarray(sinP[NMETA:NMETA + 31 * TQ].reshape(31, 4, 128, 32).transpose(0, 2, 1, 3)))
    in_maps = []
    for c in range(NCORES):
        m = dict(shared)
        tiles = [8 * j + c for j in range(NSLOT)]
        m["xqT"] = np.ascontiguousarray(np.stack([xp[TQ * g:TQ * (g + 1)].T for g in tiles]))
        m["xq"] = np.ascontiguousarray(np.stack([xp[TQ * g:TQ * (g + 1)] for g in tiles]))
        m["cosQ"] = np.ascontiguousarray(np.stack([cosF[:, NMETA + TQ * g: NMETA + TQ * (g + 1)] for g in tiles]))
        m["sinQ"] = np.ascontiguousarray(np.stack([sinF[:, NMETA + TQ * g: NMETA + TQ * (g + 1)] for g in tiles]))
        m["cosT"] = np.ascontiguousarray(np.stack([cosP[NMETA + TQ * g: NMETA + TQ * (g + 1)].reshape(4, 128, 32).transpose(1, 0, 2) for g in tiles]))
        m["sinT"] = np.ascontiguousarray(np.stack([sinP[NMETA + TQ * g: NMETA + TQ * (g + 1)].reshape(4, 128, 32).transpose(1, 0, 2) for g in tiles]))
        if c == 0:
            pass
        kb = np.zeros((128, NSLOT * NBLK + 2), f32)
        for j in range(NSLOT):
            for b in range(NBLK):
                if b == 0:
                    kb[16:, j * NBLK] = NEG
                elif (b - 1) >= 4 * (8 * j + c):
                    kb[:, j * NBLK + b] = NEG
        kb[16:, NSLOT * NBLK + 1] = NEG
        m["kbias"] = kb
        bs = slice(BPC * c, BPC * (c + 1))
        m["xsT"] = np.ascontiguousarray(xs_all[bs].reshape(ST, D).T)
        m["xs"] = np.ascontiguousarray(xs_all[bs].reshape(ST, D))
        m["cdk"] = np.ascontiguousarray(cdk[bs]); m["csk"] = np.ascontiguousarray(csk[bs])
        m["cdv"] = np.ascontiguousarray(cdv[bs]); m["csv"] = np.ascontiguousarray(csv[bs])
        in_maps.append(m)
    if _NC is None:
        _NC = build_program()
    res = run_bass_kernel_spmd(_NC, in_maps, core_ids=list(range(NCORES)))
    y_prompt = np.zeros((1, SEQ, D), f32)
    kv_p = np.zeros((TP, 2048), f32)
    y_sample = np.zeros((DEC_B, DEC_T, D), f32)
    kv_s = np.zeros((DEC_B, DEC_T, 2048), f32)
    for c in range(NCORES):
        r = res.results[c]
        for j in range(NSLOT):
            g = 8 * j + c
            y_prompt[0, TQ * g:TQ * (g + 1)] = r["y"][j]
            kv_p[NMETA + TQ * g: NMETA + TQ * (g + 1)] = r["kvo"][j]
        if c == 0:
            kv_p[0:NMETA] = r["metao"]
        y_sample[BPC * c:BPC * (c + 1)] = np.asarray(r["ys"]).reshape(BPC, DEC_T, D)
        kv_s[BPC * c:BPC * (c + 1)] = np.asarray(r["kvs"]).reshape(BPC, DEC_T, 2048)
    outs = (y_prompt, y_sample,
            kv_p[:, 0:512].reshape(1, 1, TP, 4, 2, 64).copy(), kv_p[:, 1024:1536].reshape(1, 1, TP, 4, 128).copy(),
            kv_p[:, 512:1024].reshape(1, 1, TP, 8, 64).copy(), kv_p[:, 1536:2048].reshape(1, 1, TP, 8, 64).copy(),
            kv_s[:, :, 0:512].reshape(1, DEC_B, DEC_T, 4, 2, 64).copy(), kv_s[:, :, 1024:1536].reshape(1, DEC_B, DEC_T, 4, 128).copy(),
            kv_s[:, :, 512:1024].reshape(1, DEC_B, DEC_T, 8, 64).copy(), kv_s[:, :, 1536:2048].reshape(1, DEC_B, DEC_T, 8, 64).copy())
    return outs
```

```python
import contextlib
import numpy as np
import ml_dtypes
import concourse.bass as bass
import concourse.mybir as mybir
from concourse.bass_utils import run_bass_kernel_spmd

F32 = mybir.dt.float32
BF16 = mybir.dt.bfloat16
AF = mybir.ActivationFunctionType
ALU = mybir.AluOpType

NCORES = 8
D = 1024
SEQ = 16384
NMETA = 16
TP = NMETA + SEQ
TQ = 512
NSLOT = 4
NBLK = 129
NEG = -30000.0
EPS = 1e-6
DEC_B = 32
DEC_T = 16
PAST = 1024
BPC = DEC_B // NCORES
ST = BPC * DEC_T
NDMA = 56
NDMA_HW = 44

C_DK, C_SK, C_DV, C_SV, C_DQ, C_SQ, C_DQS, C_DKS = 0, 512, 1024, 1536, 2048, 2560, 3072, 3584


class Buf:
    __slots__ = ("w", "r")

    def __init__(self):
        self.w = {}
        self.r = {}


class Prog:
    CE = ("pe", "act", "dve", "pool")

    def __init__(self):
        self.q = {e: [] for e in self.CE + ("sp",)}
        self.cnt = {e: 0 for e in self.CE}
        self.waited = {e: {} for e in self.CE + ("sp",)}
        self.dma_vals = [0] * NDMA
        self.rr = 0
        self.rr_sw = 0

    def _wait(self, eng, key, val):
        if self.waited[eng].get(key, 0) >= val:
            return
        self.waited[eng][key] = val
        self.q[eng].append(("w", key, val))

    def _deps(self, eng, reads, writes):
        for b in reads:
            for k, v in b.w.items():
                if not (eng == "pe" and k == "pe"):
                    self._wait(eng, k, v)
        for b in writes:
            for dct in (b.w, b.r):
                for k, v in dct.items():
                    if not (eng == "pe" and k == "pe"):
                        self._wait(eng, k, v)

    def _mark(self, key, val, reads, writes):
        for b in reads:
            if b.r.get(key, 0) < val:
                b.r[key] = val
        for b in writes:
            b.w[key] = val
            b.r = {}

    def op(self, eng, fn, reads=(), writes=()):
        self._deps(eng, reads, writes)
        self.cnt[eng] += 1
        self.q[eng].append(("o", fn))
        self._mark(eng, self.cnt[eng], reads, writes)

    def dma(self, eng, out, in_, reads=(), writes=()):
        if eng == "pool":
            i = NDMA_HW + self.rr_sw
            self.rr_sw = (self.rr_sw + 1) % (NDMA - NDMA_HW)
        else:
            i = self.rr
            self.rr = (i + 1) % NDMA_HW
        key = ("d", i)
        if self.dma_vals[i] > 0:
            self._wait(eng, key, self.dma_vals[i])
        self._deps(eng, reads, writes)
        self.dma_vals[i] += 16
        self.q[eng].append(("d", out, in_, i))
        self._mark(key, self.dma_vals[i], reads, writes)

    def barrier(self):
        for e in self.CE + ("sp",):
            for o in self.CE:
                if o != e and self.cnt[o] > 0:
                    self._wait(e, o, self.cnt[o])
            for i in range(NDMA):
                if self.dma_vals[i] > 0:
                    self._wait(e, ("d", i), self.dma_vals[i])

    def replay(self, eng, e, sems, dsems):
        for it in self.q[eng]:
            if it[0] == "w":
                k = it[1]
                s = dsems[k[1]] if isinstance(k, tuple) else sems[k]
                e.wait_ge(s, it[2])
            elif it[0] == "o":
                it[1](e).then_inc(sems[eng], 1)
            else:
                e.dma_start(out=it[1], in_=it[2]).then_inc(dsems[it[3]], 16)


def mm(out, lhsT, rhs, start=True, stop=True, sgc=False):
    return lambda e: e.matmul(out, lhsT=lhsT, rhs=rhs, start=start, stop=stop, skip_group_check=sgc)


def actf(out, in_, func, **kw):
    return lambda e: e.activation(out=out, in_=in_, func=func, **kw)


def ttf(out, a, b, op):
    return lambda e: e.tensor_tensor(out=out, in0=a, in1=b, op=op)


def tsf(out, a, s1, op0):
    return lambda e: e.tensor_scalar(out=out, in0=a, scalar1=s1, scalar2=None, op0=op0)


def sttf(out, in0, scalar, in1, op0, op1):
    return lambda e: e.scalar_tensor_tensor(out=out, in0=in0, scalar=scalar, in1=in1, op0=op0, op1=op1)


def cpf(out, in_):
    return lambda e: e.tensor_copy(out=out, in_=in_)


def msf(ap, v):
    return lambda e: e.memset(ap, v)


def build_program():
    nc = bass.Bass("TRN2", target_bir_lowering=False)
    P = Prog()
    dr = {}

    def din(n, shape, dt=F32):
        dr[n] = nc.dram_tensor(n, list(shape), dt, kind="ExternalInput").ap()

    def dout(n, shape, dt=F32):
        dr[n] = nc.dram_tensor(n, list(shape), dt, kind="ExternalOutput").ap()

    din("xT", [D, TP]); din("xqT", [NSLOT, D, TQ]); din("xq", [NSLOT, TQ, D])
    din("w_in", [D, 3 * D]); din("w_out", [D, D]); din("w_up", [D, 4 * D]); din("w_down", [4 * D, D])
    din("gcols", [128, 16]); din("gfin", [128, D]); din("ghead", [128, 1]); din("lamv", [128, 256])
    din("cosF", [128, TP]); din("sinF", [128, TP])
    din("cosQ", [NSLOT, 128, TQ]); din("sinQ", [NSLOT, 128, TQ])
    din("cosT", [NSLOT, 128, 4, 32]); din("sinT", [NSLOT, 128, 4, 32])
    din("dmask", [128, 8, 512], BF16); din("kbias", [128, NSLOT * NBLK + 2]); din("consts", [128, 512], BF16)
    din("xsT", [D, ST]); din("xs", [ST, D])
    din("cdk", [BPC, PAST, 512]); din("csk", [BPC, PAST, 512]); din("cdv", [BPC, PAST, 512]); din("csv", [BPC, PAST, 512])
    din("cosS", [128, ST]); din("sinS", [128, ST]); din("cosST", [16, BPC, 32]); din("sinST", [16, BPC, 32])
    din("cosMT", [16, 1, 32]); din("sinMT", [16, 1, 32])
    din("cosTp", [31, 128, 4, 32]); din("sinTp", [31, 128, 4, 32])
    dout("y", [NSLOT, TQ, D]); dout("kvo", [NSLOT, TQ, 2048]); dout("metao", [NMETA, 2048])
    dout("ys", [ST, D]); dout("kvs", [ST, 2048])
    kT_scr = nc.dram_tensor("kT_scr", [8, 128, NBLK * 128], BF16, kind="Internal").ap()
    v_scr = nc.dram_tensor("v_scr", [8, 128, NBLK, 128], BF16, kind="Internal").ap()
    wo_scr = nc.dram_tensor("wo_scr", [2, 128, 8, 512], BF16, kind="Internal").ap()
    wu_scr = nc.dram_tensor("wu_scr", [8, 128, 8, 512], BF16, kind="Internal").ap()
    wd_scr = nc.dram_tensor("wd_scr", [8, 128, 4, 1024], BF16, kind="Internal").ap()
    wq_scr = nc.dram_tensor("wq_scr", [128, 8, 2048], BF16, kind="Internal").ap()

    es = contextlib.ExitStack()
    with es:
        def sb(name, shape, dt):
            return es.enter_context(nc.sbuf_tensor(name, list(shape), dt))

        WBIG = sb("WBIG", [128, 8, 4096], BF16)
        consts = sb("consts_sb", [128, 512], BF16)
        dmask = sb("dmask_sb", [128, 8, 512], BF16)
        kbias = sb("kbias_sb", [128, NSLOT * NBLK + 2], F32)
        gfin = sb("gfin_sb", [128, D], F32)
        gcols = sb("gcols_sb", [128, 16], F32)
        smallc = sb("smallc", [128, 64], F32)
        lamv = sb("lamv_sb", [128, 256], F32)
        qT = sb("qT", [128, 8, 512], BF16)
        kTo = sb("kTo", [128, 8, 512], BF16)
        Vo = sb("Vo", [128, 4, 1024], BF16)
        mixT = sb("mixT", [128, 8, 512], BF16)
        SF = sb("SF", [128, 14336], F32)
        SB = sb("SB", [128, 18432], BF16)
        ps = es.enter_context(nc.psum_tensor("ps", [128, 4096], F32))
        sems = {e: es.enter_context(nc.semaphore("s_" + e)) for e in Prog.CE}
        dsems = [es.enter_context(nc.semaphore("d%d" % i)) for i in range(NDMA)]

        ntri = consts[:, 0:128]; nones = consts[:, 128:256]; ones = consts[:, 256:384]; ident = consts[:, 384:512]
        ZB = NSLOT * NBLK
        MB = NSLOT * NBLK + 1

        bk = [Buf() for _ in range(8)]

        def bank(b, rows=128, w=512):
            return ps[:rows, b * 512: b * 512 + w]

        B_W = Buf(); B_Wq = Buf(); B_c = Buf(); B_qT = Buf(); B_kTo = Buf(); B_Vo = Buf(); B_mix = Buf(); B_small = Buf()

        xin2 = [SF[:, i * 4096:(i + 1) * 4096].rearrange("p (c t) -> p c t", t=512) for i in range(2)]
        zt = [SF[:, 4096 + i * 2048: 4096 + (i + 1) * 2048] for i in range(2)]; B_zt = [Buf(), Buf()]
        B_xin0 = Buf()
        cosF_t = SF[:, 8192:8704]; sinF_t = SF[:, 8704:9216]; B_cs = Buf()
        cosr = SF[:, 9216:9728]; sinr = SF[:, 9728:10240]; B_csr = Buf()
        rstd_b = SF[:, 10240:10752]; B_rb = Buf()
        t1 = [SF[:, 10752 + i * 512: 10752 + (i + 1) * 512] for i in range(2)]
        t2 = [SF[:, 11776 + i * 512: 11776 + (i + 1) * 512] for i in range(2)]
        B_t = [Buf(), Buf()]
        cosT_t = SF[:, 12800:12928].rearrange("p (s i) -> p s i", i=32)
        sinT_t = SF[:, 12928:13056].rearrange("p (s i) -> p s i", i=32); B_cst = Buf()
        rtmp = [SF[:, 13056 + i * 256: 13056 + (i + 1) * 256] for i in range(4)]; B_rt = Buf()
        lntmp = SF[:, 14080:14336 - 128]
        rcol = SF[:, 14336 - 128: 14336 - 120]; B_rc = Buf()
        sq2 = [SB[:, i * 8192: i * 8192 + 4096].rearrange("p (c t) -> p c t", t=512) for i in range(2)]; B_sq2 = [Buf(), Buf()]
        xg2 = [SB[:, i * 8192 + 4096: i * 8192 + 8192].rearrange("p (c t) -> p c t", t=512) for i in range(2)]; B_xg2 = [Buf(), Buf()]
        u_t = [SF[:, i * 1024:(i + 1) * 1024].rearrange("p (h w) -> p h w", h=2) for i in range(2)]; B_u = [Buf(), Buf()]
        dtmp = [SF[:, 2048 + i * 512: 2048 + (i + 1) * 512] for i in range(6)]; B_dt = Buf()
        Pacc = [SF[:, 5120 + i * 512: 5120 + (i + 1) * 512] for i in range(2)]; B_pa = [Buf(), Buf()]
        Kseg = [SB[:, i * 2048:(i + 1) * 2048] for i in range(3)]
        Vseg = [SB[:, 6144 + i * 2048: 6144 + (i + 1) * 2048].rearrange("p (b c) -> p b c", c=128) for i in range(3)]
        B_seg = [Buf(), Buf(), Buf()]
        L_t = [SB[:, 12288 + i * 1024: 12288 + (i + 1) * 1024].rearrange("p (h w) -> p h w", h=2) for i in range(2)]
        A_t = [SB[:, 14336 + i * 1024: 14336 + (i + 1) * 1024].rearrange("p (h w) -> p h w", h=2) for i in range(2)]
        Lacc = [SB[:, 16384 + i * 1024: 16384 + (i + 1) * 1024].rearrange("p (h w) -> p h w", h=2) for i in range(2)]
        B_L = [Buf(), Buf()]; B_A = [Buf(), Buf()]; B_Lacc = [Buf(), Buf()]
        r_t = SF[:, 6144:10240].rearrange("p (s n) -> p s n", n=1024); B_r = Buf()
        xqT_t = SF[:, 10240:14336].rearrange("p (c t) -> p c t", t=512); B_xqT = Buf()
        rl = [SF[:, 4096 + i * 512: 4096 + (i + 1) * 512] for i in range(2)]; B_rl = [Buf(), Buf()]
        x1tmp = [SF[:, 5120 + i * 512: 5120 + (i + 1) * 512] for i in range(2)]; B_x1t = [Buf(), Buf()]
        mcol = smallc[:, 16:48]; B_mc = Buf()
        junk = SF[:, 4096:5120]
        wbuf = [SB[:, i * 4096:(i + 1) * 4096] for i in range(2)]; B_wb = [Buf(), Buf()]
        x1g = qT; B_x1g = B_qT

        def rstd_from(ss_ap, out_ap, n_feat, rows, width, reads, writes, tmp_ap=None):
            P.op("act", actf(out_ap, ss_ap, AF.Ln, scale=1.0 / n_feat, bias=smallc[:rows, 8:9]), reads=reads + [B_small], writes=writes)
            P.op("act", actf(out_ap, out_ap, AF.Exp, scale=-0.5), reads=writes, writes=writes)

        P.dma("sp", consts[:], dr["consts"][:], writes=[B_c])
        P.dma("sp", dmask[:], dr["dmask"][:], writes=[B_c])
        P.dma("sp", kbias[:], dr["kbias"][:], writes=[B_c])
        P.dma("sp", gfin[:], dr["gfin"][:], writes=[B_c])
        P.dma("sp", gcols[:], dr["gcols"][:], writes=[B_c])
        P.dma("sp", lamv[:], dr["lamv"][:], writes=[B_c])
        P.dma("sp", smallc[:, 0:1], dr["ghead"][:], writes=[B_small])
        w_in_v = dr["w_in"].rearrange("(c p) n -> p c n", p=128)
        for c0 in range(0, 8, 2):
            P.dma("pool", WBIG[:, c0:c0 + 2, 0:2048], w_in_v[:, c0:c0 + 2, 1024:3072], writes=[B_W])

        def load_q_weights():
            for c0 in range(0, 8, 2):
                P.dma("pool", WBIG[:, c0:c0 + 2, 2048:3072], w_in_v[:, c0:c0 + 2, 0:1024], writes=[B_Wq])
            for c in range(8):
                for (s0, d0) in ((C_DQ, C_DQS), (C_DK, C_DKS)):
                    src = WBIG[:, c, s0:s0 + 512].rearrange("p (m h i) -> p m h i", h=2, i=32)
                    dst = WBIG[:, c, d0:d0 + 512].rearrange("p (m h i) -> p m h i", h=2, i=32)
                    P.op("pool", cpf(dst[:, :, 0, :], src[:, :, 1, :]), reads=[B_W, B_Wq], writes=[B_Wq])
                    P.op("pool", cpf(dst[:, :, 1, :], src[:, :, 0, :]), reads=[B_W, B_Wq], writes=[B_Wq])

        w_out_v0 = dr["w_out"].rearrange("(c p) n -> p c n", p=128)
        w_up_v0 = dr["w_up"].rearrange("(c p) n -> p c n", p=128)
        w_dn_v0 = dr["w_down"].rearrange("(m p) n -> p m n", p=128)
        conv_jobs = [(wo_scr[i], w_out_v0[:, :, i * 512:(i + 1) * 512]) for i in range(2)]
        for pc in range(8):
            conv_jobs.append((wu_scr[pc], w_up_v0[:, :, pc * 512:(pc + 1) * 512]))
            conv_jobs.append((wd_scr[pc], w_dn_v0[:, pc * 4:(pc + 1) * 4, :]))
        P.op("dve", msf(smallc[:, 8:9], EPS), writes=[B_small])
        P.op("dve", msf(smallc[:, 3:5], 0.0), writes=[B_small])
        P.op("dve", ttf(lamv[:, 0:64], lamv[:, 0:64], lamv[:, 64:128], ALU.mult), reads=[B_c], writes=[B_c])
        P.op("dve", ttf(lamv[:, 128:192], lamv[:, 128:192], lamv[:, 192:256], ALU.mult), reads=[B_c], writes=[B_c])
        P.op("act", actf(lamv[:, 64:128], lamv[:, 0:64], AF.Copy, accum_out=smallc[:, 3:4]), reads=[B_c, B_small], writes=[B_c, B_small])
        P.op("act", actf(lamv[:, 192:256], lamv[:, 128:192], AF.Copy, accum_out=smallc[:, 4:5]), reads=[B_c, B_small], writes=[B_c, B_small])
        P.op("act", actf(smallc[:, 5:7], smallc[:, 3:5], AF.Exp), reads=[B_small], writes=[B_small])
        P.op("dve", sttf(smallc[:, 1:2], smallc[:, 6:7], -0.2, smallc[:, 5:6], ALU.add, ALU.subtract), reads=[B_small], writes=[B_small])
        P.op("dve", tsf(smallc[:, 2:3], smallc[:, 0:1], 0.8, ALU.mult), reads=[B_small], writes=[B_small])

        fm_rot = [0]; tm_rot = [0]; t_rot = [0]; zt_rot = [0]

        cs2 = [(cosF_t, sinF_t), (SF[:, 13056:13568], SF[:, 13568:14080])]

        def project_loads(T, subs, x_ap, cos_ap, sin_ap, own, cosT_ap=None, sinT_ap=None, par=0):
            nsub = len(subs)
            xin = xin2[par]
            B_xinl = [B_xin0] if par == 0 else [B_zt[0], B_zt[1]]
            B_csl = [B_cs] if par == 0 else [B_rt]
            P.dma("sp", xin[:, :, :T], x_ap, writes=B_xinl)
            P.dma("sp", cs2[par][0][:, :T], cos_ap, writes=B_csl)
            P.dma("sp", cs2[par][1][:, :T], sin_ap, writes=B_csl)
            if own:
                P.dma("sp", cosT_t[:subs[0][1], :nsub, :], cosT_ap, writes=[B_cst])
                P.dma("sp", sinT_t[:subs[0][1], :nsub, :], sinT_ap, writes=[B_cst])

        def project(T, subs, x_ap, cos_ap, sin_ap, own, kT_dst, B_kdst, v_dst, B_vdst, q_dst=None, B_qdst=None,
                    zt_out=None, cosT_ap=None, sinT_ap=None, g0=0, par=0, do_loads=True):
            nsub = len(subs)
            xin = xin2[par]; sq = sq2[par]; xg = xg2[par]; B_sq = B_sq2[par]; B_xg = B_xg2[par]
            B_xinl = [B_xin0] if par == 0 else [B_zt[0], B_zt[1]]
            B_csl = [B_cs] if par == 0 else [B_rt]
            cosF_p, sinF_p = cs2[par]
            if do_loads:
                project_loads(T, subs, x_ap, cos_ap, sin_ap, own, cosT_ap, sinT_ap, par)
            P.op("act", actf(sq[:, :, :T], xin[:, :, :T], AF.Square), reads=B_xinl, writes=[B_sq])
            for c in range(8):
                P.op("dve", tsf(xg[:, c, :T], xin[:, c, :T], gcols[:, g0 + c:g0 + c + 1], ALU.mult), reads=B_xinl + [B_c], writes=[B_xg])
            for c in range(8):
                P.op("pe", mm(bank(6, 128, T), ones, sq[:, c, :T], c == 0, c == 7), reads=[B_sq, B_c], writes=[bk[6]])
            for si, (c0, tsz) in enumerate(subs):
                for c in range(8):
                    P.op("pe", mm(ps[:tsz, 7 * 512 + si: 7 * 512 + si + 1], sq[:, c, c0:c0 + tsz], ones[:, 0:1], c == 0, c == 7),
                         reads=[B_sq, B_c], writes=[bk[7]])
            rstd_from(bank(6, 128, T), rstd_b[:, :T], D, 128, T, [bk[6]], [B_rb], lntmp[:, :T] if T <= 128 else junk[:, :T])
            tszm = max(s[1] for s in subs)
            rstd_from(ps[:tszm, 7 * 512: 7 * 512 + nsub], rcol[:tszm, :nsub], D, tszm, nsub, [bk[7]], [B_rc], lntmp[:tszm, :nsub])
            P.op("dve", ttf(cosr[:, :T], cosF_p[:, :T], rstd_b[:, :T], ALU.mult), reads=B_csl + [B_rb], writes=[B_csr])
            P.op("dve", ttf(sinr[:, :T], sinF_p[:, :T], rstd_b[:, :T], ALU.mult), reads=B_csl + [B_rb], writes=[B_csr])

            def fm_plain(wcol, dst, B_dst, sc):
                b = fm_rot[0] % 4; fm_rot[0] += 1
                for c in range(8):
                    P.op("pe", mm(bank(b, 128, T), WBIG[:, c, wcol:wcol + 128], xg[:, c, :T], c == 0, c == 7), reads=[B_W, B_Wq, B_xg], writes=[bk[b]])
                P.op("dve", sttf(dst, bank(b, 128, T), sc, rstd_b[:, :T], ALU.mult, ALU.mult), reads=[bk[b], B_rb], writes=[B_dst])

            def fm_rope(wcol, scol, dst, B_dst, sc):
                b1 = fm_rot[0] % 4; fm_rot[0] += 1
                b2 = fm_rot[0] % 4; fm_rot[0] += 1
                for c in range(8):
                    P.op("pe", mm(bank(b1, 128, T), WBIG[:, c, wcol:wcol + 128], xg[:, c, :T], c == 0, c == 7), reads=[B_W, B_Wq, B_xg], writes=[bk[b1]])
                for c in range(8):
                    P.op("pe", mm(bank(b2, 128, T), WBIG[:, c, scol:scol + 128], xg[:, c, :T], c == 0, c == 7), reads=[B_W, B_Wq, B_xg], writes=[bk[b2]])
                k = t_rot[0] % 2; t_rot[0] += 1
                P.op("dve", sttf(t1[k][:, :T], bank(b1, 128, T), sc, cosr[:, :T], ALU.mult, ALU.mult), reads=[bk[b1], B_csr], writes=[B_t[k]])
                P.op("dve", sttf(t2[k][:, :T], bank(b2, 128, T), sc, sinr[:, :T], ALU.mult, ALU.mult), reads=[bk[b2], B_csr], writes=[B_t[k]])
                P.op("pool", ttf(dst, t1[k][:, :T], t2[k][:, :T], ALU.add), reads=[B_t[k]], writes=[B_dst])

            for g in range(4):
                fm_plain(C_SK + 128 * g, kT_dst(g), B_kdst, 1.0)
            for h in range(4):
                fm_rope(C_DK + 128 * h, C_DKS + 128 * h, kT_dst(4 + h), B_kdst, 1.0)
            if q_dst is not None:
                for g in range(4):
                    fm_plain(C_SQ + 128 * g, q_dst(g), B_qdst, 0.125)
                for h in range(4):
                    fm_rope(C_DQ + 128 * h, C_DQS + 128 * h, q_dst(4 + h), B_qdst, 0.125)
            for si, (c0, tsz) in enumerate(subs):
                if own:
                    zi = zt_rot[0] % 2; zt_rot[0] += 1
                    z = zt[zi]; Bz = B_zt[zi]
                for nb in (range(4) if own else (2, 3)):
                    b = 4 + tm_rot[0] % 2; tm_rot[0] += 1
                    for c in range(8):
                        P.op("pe", mm(bank(b, tsz, 512), xg[:, c, c0:c0 + tsz], WBIG[:, c, nb * 512:(nb + 1) * 512], c == 0, c == 7),
                             reads=[B_W, B_Wq, B_xg], writes=[bk[b]])
                    if own:
                        P.op("act", actf(z[:tsz, nb * 512:(nb + 1) * 512], bank(b, tsz, 512), AF.Copy, scale=rcol[:tsz, si:si + 1]),
                             reads=[bk[b], B_rc], writes=[Bz])
                    else:
                        P.op("act", actf(v_dst(si, tsz)[:, (nb - 2) * 512:(nb - 1) * 512], bank(b, tsz, 512), AF.Copy, scale=rcol[:tsz, si:si + 1]),
                             reads=[bk[b], B_rc], writes=[B_vdst])
                if own:
                    zv = z[:tsz, 0:512].rearrange("p (m h i) -> p m h i", h=2, i=32)
                    x1 = zv[:, :, 0, :]; x2 = zv[:, :, 1, :]
                    cb = cosT_t[:tsz, si, :].unsqueeze(1).to_broadcast([tsz, 8, 32])
                    sbb = sinT_t[:tsz, si, :].unsqueeze(1).to_broadcast([tsz, 8, 32])
                    ta, tb, tc, td = [rtmp[i][:tsz, :].rearrange("p (m i) -> p m i", i=32) for i in range(4)]
                    P.op("dve", ttf(ta, x1, cb, ALU.mult), reads=[Bz, B_cst], writes=[B_rt])
                    P.op("dve", ttf(tb, x2, sbb, ALU.mult), reads=[Bz, B_cst], writes=[B_rt])
                    P.op("dve", ttf(tc, x2, cb, ALU.mult), reads=[Bz, B_cst], writes=[B_rt])
                    P.op("dve", ttf(td, x1, sbb, ALU.mult), reads=[Bz, B_cst], writes=[B_rt])
                    P.op("dve", ttf(x1, ta, tb, ALU.subtract), reads=[B_rt], writes=[Bz])
                    P.op("dve", ttf(x2, tc, td, ALU.add), reads=[B_rt], writes=[Bz])
                    P.op("pool", cpf(v_dst(si, tsz), z[:tsz, 1024:2048]), reads=[Bz], writes=[B_vdst])
                    P.dma("sp", zt_out(si, tsz), z[:tsz, :], reads=[Bz])

        xT_v = dr["xT"].rearrange("(c p) t -> p c t", p=128)
        kst = [kTo, qT]; B_kst = [B_kTo, B_qT]
        vst = [Vo, mixT[:, :, :].rearrange("p (s a) t -> p s (a t)", a=2)]; B_vst = [B_Vo, B_mix]

        def store_scratch(slot, blk0, nblk):
            for g in range(8):
                P.dma("sp", kT_scr[g, :, blk0 * 128:(blk0 + nblk) * 128], kst[slot][:, g, :nblk * 128], reads=[B_kst[slot]])
                vc = 512 + 128 * g if g < 4 else 128 * (g - 4)
                P.dma("sp", v_scr[g, :, blk0:blk0 + nblk, :], vst[slot][:, :nblk, vc:vc + 128], reads=[B_vst[slot]])

        kb = SF[:, 8192:10240].bitcast(BF16).rearrange("p (s n) -> p s n", n=1024)
        B_kb = [B_cs, B_csr]
        cst2 = [(cosT_t, sinT_t, [B_cst]),
                (SF[:, 10240:10368].rearrange("p (s i) -> p s i", i=32), SF[:, 10368:10496].rearrange("p (s i) -> p s i", i=32), [B_rb])]
        p1rot = {"tm": 0, "tr": 0, "zk": 0}

        def p1_loads(i):
            par = (i + 1) % 2
            p0 = NMETA + TQ * i
            B_xinl = [B_xin0] if par == 0 else [B_zt[0], B_zt[1]]
            P.dma("sp", xin2[par][:, :, :], xT_v[:, :, p0:p0 + TQ], writes=B_xinl)
            P.dma("sp", cst2[par][0][:, :, :], dr["cosTp"][i], writes=cst2[par][2])
            P.dma("sp", cst2[par][1][:, :, :], dr["sinTp"][i], writes=cst2[par][2])

        def p1_prep(i):
            par = (i + 1) % 2
            xin = xin2[par]; sq = sq2[par]; xg = xg2[par]; B_sq = B_sq2[par]; B_xg = B_xg2[par]
            B_xinl = [B_xin0] if par == 0 else [B_zt[0], B_zt[1]]
            P.op("act", actf(sq[:, :, :], xin[:, :, :], AF.Square), reads=B_xinl, writes=[B_sq])
            for c in range(8):
                P.op("dve", tsf(xg[:, c, :], xin[:, c, :], gcols[:, c:c + 1], ALU.mult), reads=B_xinl + [B_c], writes=[B_xg])

        def p1_tile(i):
            par = (i + 1) % 2; sl = (i + 1) % 2
            xin = xin2[par]; sq = sq2[par]; xg = xg2[par]; B_sq = B_sq2[par]; B_xg = B_xg2[par]
            cT, sT, B_ct = cst2[par]
            for si in range(4):
                for c in range(8):
                    P.op("pe", mm(ps[:, 7 * 512 + si: 7 * 512 + si + 1], sq[:, c, si * 128:(si + 1) * 128], ones[:, 0:1], c == 0, c == 7),
                         reads=[B_sq, B_c], writes=[bk[7]])
            rstd_from(ps[:, 7 * 512: 7 * 512 + 4], rcol[:, :4], D, 128, 4, [bk[7]], [B_rc])
            for si in range(4):
                if si == 2 and i + 1 < 31:
                    p1_prep(i + 1)
                for nb in range(4):
                    b_ = p1rot["tm"] % 4; p1rot["tm"] += 1
                    for c in range(8):
                        P.op("pe", mm(bank(b_), xg[:, c, si * 128:(si + 1) * 128], WBIG[:, c, nb * 512:(nb + 1) * 512], c == 0, c == 7),
                             reads=[B_W, B_xg], writes=[bk[b_]])
                    if nb == 0:
                        k_ = p1rot["zk"] % 2; p1rot["zk"] += 1
                        zk = t1[k_]
                        P.op("act", actf(zk, bank(b_), AF.Copy, scale=rcol[:, si:si + 1]), reads=[bk[b_], B_rc], writes=[B_t[k_]])
                        zv = zk.rearrange("p (m h i) -> p m h i", h=2, i=32)
                        kv_ = kb[:, si, 0:512].rearrange("p (m h i) -> p m h i", h=2, i=32)
                        x1 = zv[:, :, 0, :]; x2 = zv[:, :, 1, :]
                        cb = cT[:, si, :].unsqueeze(1).to_broadcast([128, 8, 32])
                        sbb = sT[:, si, :].unsqueeze(1).to_broadcast([128, 8, 32])
                        ta, tb, tc, td = [rtmp[q_][:, :].rearrange("p (m i) -> p m i", i=32) for q_ in range(4)]
                        P.op("dve", ttf(ta, x1, cb, ALU.mult), reads=[B_t[k_]] + B_ct, writes=[B_rt])
                        P.op("dve", ttf(tb, x2, sbb, ALU.mult), reads=[B_t[k_]] + B_ct, writes=[B_rt])
                        P.op("dve", ttf(tc, x2, cb, ALU.mult), reads=[B_t[k_]] + B_ct, writes=[B_rt])
                        P.op("dve", ttf(td, x1, sbb, ALU.mult), reads=[B_t[k_]] + B_ct, writes=[B_rt])
                        P.op("dve", ttf(kv_[:, :, 0, :], ta, tb, ALU.subtract), reads=[B_rt], writes=B_kb)
                        P.op("dve", ttf(kv_[:, :, 1, :], tc, td, ALU.add), reads=[B_rt], writes=B_kb)
                    elif nb == 1:
                        P.op("act", actf(kb[:, si, 512:1024], bank(b_), AF.Copy, scale=rcol[:, si:si + 1]), reads=[bk[b_], B_rc], writes=B_kb)
                    else:
                        P.op("act", actf(vst[sl][:, si, (nb - 2) * 512:(nb - 1) * 512], bank(b_), AF.Copy, scale=rcol[:, si:si + 1]),
                             reads=[bk[b_], B_rc], writes=[B_vst[sl]])
            for g in range(8):
                fc = 512 + 128 * g if g < 4 else 128 * (g - 4)
                b_ = 4 + p1rot["tr"] % 3; p1rot["tr"] += 1
                for si in range(4):
                    P.op("pe", mm(ps[:, b_ * 512 + si * 128: b_ * 512 + (si + 1) * 128], kb[:, si, fc:fc + 128], ident), reads=B_kb + [B_c], writes=[bk[b_]])
                eng = "dve" if g % 2 == 0 else "act"
                if eng == "dve":
                    P.op("dve", cpf(kst[sl][:, g, :], bank(b_)), reads=[bk[b_]], writes=[B_kst[sl]])
                else:
                    P.op("act", actf(kst[sl][:, g, :], bank(b_), AF.Copy), reads=[bk[b_]], writes=[B_kst[sl]])

        p1_loads(0)
        p1_prep(0)
        for i in range(31):
            sl = (i + 1) % 2
            if i + 1 < 31:
                p1_loads(i + 1)
            p1_tile(i)
            store_scratch(sl, 1 + 4 * i, 4)
            if i == 3:
                load_q_weights()
            if i == 5:
                P.dma("sp", wq_scr[:, :, :], WBIG[:, :, 2048:4096], reads=[B_Wq])
            if 6 <= i < 6 + len(conv_jobs):
                P.dma("pool", conv_jobs[i - 6][0], conv_jobs[i - 6][1])
        P.op("pool", msf(kst[0][:, :, 0:128], 0.0), writes=[B_kst[0]])
        P.op("pool", msf(vst[0][:, 0, :], 0.0), writes=[B_vst[0]])
        project(NMETA, [(0, NMETA)], xT_v[:, :, 0:NMETA], dr["cosF"][:, 0:NMETA], dr["sinF"][:, 0:NMETA], True,
                lambda g: kst[0][:, g, :NMETA], B_kst[0], lambda si, tsz: vst[0][:tsz, 0, :], B_vst[0],
                zt_out=lambda si, tsz: dr["metao"][:, :],
                cosT_ap=dr["cosMT"], sinT_ap=dr["sinMT"])
        store_scratch(0, 0, 1)
        P.barrier()

        def attend(W, q_ap, groups_blocks, finish, mid_hook=None):
            for g in range(8):
                pro, blocks, after = groups_blocks(g)
                pro()
                n = len(blocks)
                if g < 4:
                    S = [(0, 1), (2, 3), (4, 5)]

                    def zmm(i):
                        bl = blocks[i]; s0, s1 = S[i % 3]; nk = bl["nk"]
                        P.op("pe", mm(bank(s0, nk, W), bl["kT"][0:64, :], q_ap(g)[0:64, :]), reads=bl["bufs"] + [B_qT], writes=[bk[s0]])
                        P.op("pe", mm(bank(s1, nk, W), bl["kT"][64:128, :], q_ap(g)[64:128, :]), reads=bl["bufs"] + [B_qT], writes=[bk[s1]])

                    def Sv(i, nk):
                        s0 = S[i % 3][0]
                        return ps[:nk, s0 * 512:(s0 + 2) * 512].rearrange("p (h w) -> p h w", h=2)[:, :, :W]

                    def u_(i):
                        bl = blocks[i]; nk = bl["nk"]; s0, s1 = S[i % 3]
                        P.op("act", actf(u_t[i % 2][:nk, :, :W], Sv(i, nk), AF.Exp, bias=bl["bias"]), reads=[bk[s0], bk[s1], B_c], writes=[B_u[i % 2]])

                    def L_(i):
                        bl = blocks[i]; nk = bl["nk"]
                        P.op("act", actf(L_t[i % 2][:nk, :, :W], u_t[i % 2][:nk, :, :W], AF.Ln, bias=smallc[:nk, 9:10]), reads=[B_u[i % 2], B_small], writes=[B_L[i % 2]])
                        if bl["msb"] is not None:
                            for h in range(2):
                                P.op("dve", ttf(L_t[i % 2][:nk, h, :W], L_t[i % 2][:nk, h, :W], bl["msb"], ALU.mult), reads=[B_L[i % 2], B_c], writes=[B_L[i % 2]])

                    def E_(k):
                        bl = blocks[k]; nk = bl["nk"]; s0, s1 = S[k % 3]
                        if k == 1:
                            P.op("dve", cpf(Lacc[1][:nk, :, :W], L_t[0][:nk, :, :W]), reads=[B_L[0]], writes=[B_Lacc[1]])
                        elif k > 1:
                            P.op("dve", ttf(Lacc[k % 2][:nk, :, :W], Lacc[(k - 1) % 2][:nk, :, :W], L_t[(k - 1) % 2][:nk, :, :W], ALU.add),
                                 reads=[B_L[(k - 1) % 2], B_Lacc[(k - 1) % 2]], writes=[B_Lacc[k % 2]])
                        for h, sb_ in ((0, s0), (1, s1)):
                            P.op("pe", mm(bank(sb_, nk, W), ntri[:nk, :nk], L_t[k % 2][:nk, h, :W], False, k == 0, sgc=True), reads=[B_L[k % 2], B_c], writes=[bk[sb_]])
                            if k > 0:
                                P.op("pe", mm(bank(sb_, nk, W), nones[:nk, :nk], Lacc[k % 2][:nk, h, :W], False, True, sgc=True), reads=[B_Lacc[k % 2], B_c], writes=[bk[sb_]])

                    zmm(0)
                    if n > 1:
                        zmm(1)
                    if n > 2:
                        zmm(2)
                    u_(0); L_(0)
                    if n > 1:
                        u_(1)
                    E_(0)
                    for i in range(n):
                        bl = blocks[i]; nk = bl["nk"]; s0, s1 = S[i % 3]
                        if i + 1 < n:
                            L_(i + 1)
                        if i + 2 < n:
                            u_(i + 2)
                        P.op("act", actf(A_t[i % 2][:nk, :, :W], Sv(i, nk), AF.Exp, bias=bl["bias"]), reads=[bk[s0], bk[s1], B_c], writes=[B_A[i % 2]])
                        if bl["msb"] is not None:
                            for h in range(2):
                                P.op("dve" if h == 0 else "pool", ttf(A_t[i % 2][:nk, h, :W], A_t[i % 2][:nk, h, :W], bl["msb"], ALU.mult), reads=[B_A[i % 2], B_c], writes=[B_A[i % 2]])
                        if i + 3 < n:
                            zmm(i + 3)
                        if i + 1 < n:
                            E_(i + 1)
                        for h in range(2):
                            P.op("pe", mm(bank(6 + h, 128, W), bl["v"], A_t[i % 2][:nk, h, :W], i == 0, i == n - 1), reads=bl["bufs"] + [B_A[i % 2]], writes=[bk[6 + h]])
                        after(i)
                else:
                    S = [(0, 1), (2, 3)]
                    A3 = [A_t[0], A_t[1], Lacc[0]]; B_A3 = [B_A[0], B_A[1], B_Lacc[0]]

                    def qk(i):
                        bl = blocks[i]; s0, s1 = S[i % 2]; nk = bl["nk"]
                        P.op("pe", mm(bank(s0, nk, W), bl["kT"][0:64, :], q_ap(g)[0:64, :]), reads=bl["bufs"] + [B_qT], writes=[bk[s0]])
                        P.op("pe", mm(bank(s1, nk, W), bl["kT"][64:128, :], q_ap(g)[64:128, :]), reads=bl["bufs"] + [B_qT], writes=[bk[s1]])

                    qk(0)
                    if n > 1:
                        qk(1)
                    for i in range(n):
                        bl = blocks[i]; nk = bl["nk"]; s0, s1 = S[i % 2]
                        At = A3[i % 3]; BAt = B_A3[i % 3]
                        Sv_ = ps[:nk, s0 * 512:(s0 + 2) * 512].rearrange("p (h w) -> p h w", h=2)[:, :, :W]
                        P.op("act", actf(At[:nk, :, :W], Sv_, AF.Exp, bias=bl["bias"]), reads=[bk[s0], bk[s1], B_c], writes=[BAt])
                        if bl["mdf"] is not None:
                            for h in range(2):
                                P.op("dve" if h == 0 else "pool", ttf(At[:nk, h, :W], At[:nk, h, :W], bl["mdf"], ALU.mult), reads=[BAt, B_c], writes=[BAt])
                        if i + 2 < n:
                            qk(i + 2)
                        for c in range(2):
                            P.op("pe", mm(bank(4 + 2 * c, 128, W), bl["v"], At[:nk, c, :W], i == 0, i == n - 1), reads=bl["bufs"] + [BAt], writes=[bk[4 + 2 * c]])
                        P.op("pe", mm(bank(5, 128, W), ones[:nk, :], At[:nk, 0, :W], i == 0, i == n - 1), reads=[BAt, B_c], writes=[bk[5]])
                        if i == 0:
                            if nk < 128:
                                P.op("dve", msf(Pacc[1][:, :W], 0.0), writes=[B_pa[1]])
                            P.op("dve", cpf(Pacc[1][:nk, :W], At[:nk, 1, :W]), reads=[BAt], writes=[B_pa[1]])
                        else:
                            P.op("dve", ttf(Pacc[1][:nk, :W], Pacc[1][:nk, :W], At[:nk, 1, :W], ALU.add), reads=[BAt], writes=[B_pa[1]])
                        after(i)
                    hi = L_t[0][:, 1, :W]; lo = L_t[1][:, 1, :W]
                    P.op("dve", cpf(hi, Pacc[1][:, :W]), reads=[B_pa[1]], writes=[B_L[0]])
                    P.op("dve", ttf(lo, Pacc[1][:, :W], hi, ALU.subtract), reads=[B_pa[1], B_L[0]], writes=[B_L[1]])
                    P.op("pe", mm(bank(7, 128, W), ones, hi, True, False), reads=[B_L[0], B_c], writes=[bk[7]])
                    P.op("pe", mm(bank(7, 128, W), ones, lo, False, True), reads=[B_L[1], B_c], writes=[bk[7]])
                finish(g)
                if g == 0 and mid_hook is not None:
                    mid_hook()

        def make_finish(W, mix_dst):
            def finish(g):
                if g < 4:
                    P.op("dve", cpf(mix_dst(4 + g)[0:64, :], bank(6, 128, W)[0:64, :]), reads=[bk[6]], writes=[B_mix])
                    P.op("dve", cpf(mix_dst(4 + g)[64:128, :], bank(7, 128, W)[64:128, :]), reads=[bk[7]], writes=[B_mix])
                else:
                    h = g - 4
                    rl0, rl1, o0, o1, od, rs = [d_[:, :W] for d_ in dtmp]
                    P.op("act", actf(rl0, bank(5, 128, W), AF.Ln), reads=[bk[5]], writes=[B_dt])
                    P.op("act", actf(rl0, rl0, AF.Exp, scale=-1.0), reads=[B_dt], writes=[B_dt])
                    P.op("act", actf(rl1, bank(7, 128, W), AF.Ln), reads=[bk[7]], writes=[B_dt])
                    P.op("act", actf(rl1, rl1, AF.Exp, scale=-1.0), reads=[B_dt], writes=[B_dt])
                    P.op("dve", ttf(o0, bank(4, 128, W), rl0, ALU.mult), reads=[bk[4], B_dt], writes=[B_dt])
                    P.op("dve", ttf(o1, bank(6, 128, W), rl1, ALU.mult), reads=[bk[6], B_dt], writes=[B_dt])
                    P.op("dve", sttf(od, o1, smallc[:, 1:2], o0, ALU.mult, ALU.add), reads=[B_dt, B_small], writes=[B_dt])
                    P.op("act", actf(A_t[0][:, 0, :W], od, AF.Square), reads=[B_dt], writes=[B_A[0]])
                    P.op("pe", mm(bank(0, 128, W), ones, A_t[0][:, 0, :W]), reads=[B_A[0], B_c], writes=[bk[0]])
                    rstd_from(bank(0, 128, W), rs, 128, 128, W, [bk[0]], [B_dt], rl0)
                    P.op("dve", sttf(mix_dst(h), od, smallc[:, 2:3], rs, ALU.mult, ALU.mult), reads=[B_dt, B_small], writes=[B_mix])
            return finish

        def mlp_loads(T, subs, xq_ap, xqT_ap):
            P.dma("sp", r_t[:subs[0][1], :len(subs), :], xq_ap, writes=[B_r])
            P.dma("sp", xqT_t[:, :, :T], xqT_ap, writes=[B_xqT])

        def mlp(T, subs, xq_ap, xqT_ap, y_out, preloaded=False, after_down=None):
            nsub = len(subs)
            if not preloaded:
                mlp_loads(T, subs, xq_ap, xqT_ap)
            w_out_v = dr["w_out"].rearrange("(c p) n -> p c n", p=128)
            wo = [wbuf[i].rearrange("p (c n) -> p c n", n=512) for i in range(2)]
            for i in range(2):
                P.dma("sp", wo[i], wo_scr[i], writes=[B_wb[i]])
            P.op("dve", msf(mcol[:, :], 0.0), writes=[B_mc])
            rot = 0
            for si, (c0, tsz) in enumerate(subs):
                for nh in range(2):
                    b = rot % 2; rot += 1
                    for c in range(8):
                        P.op("pe", mm(bank(b, tsz, 512), mixT[:, c, c0:c0 + tsz], wo[nh][:, c, :], c == 0, c == 7), reads=[B_mix, B_wb[nh]], writes=[bk[b]])
                    P.op("dve", ttf(r_t[:tsz, si, nh * 512:(nh + 1) * 512], bank(b, tsz, 512), r_t[:tsz, si, nh * 512:(nh + 1) * 512], ALU.add),
                         reads=[bk[b], B_r], writes=[B_r])
                P.op("act", actf(junk[:tsz, :], r_t[:tsz, si, :], AF.Square, accum_out=mcol[:tsz, si:si + 1]), reads=[B_r, B_mc], writes=[B_mc, B_rt])
            tszm = max(s[1] for s in subs)
            rstd_from(mcol[:tszm, 0:nsub], mcol[:tszm, 8:8 + nsub], D, tszm, nsub, [B_mc], [B_mc], lntmp[:tszm, :nsub])
            P.op("dve", ttf(mcol[:tszm, 16:16 + nsub], mcol[:tszm, 8:8 + nsub], mcol[:tszm, 8:8 + nsub], ALU.mult), reads=[B_mc], writes=[B_mc])
            for nchunk in range(8):
                b = 2 + rot % 2; rot += 1
                for c in range(8):
                    P.op("pe", mm(bank(b, 128, T), wo[nchunk // 4][:, c, (nchunk % 4) * 128:(nchunk % 4 + 1) * 128], mixT[:, c, :T], c == 0, c == 7),
                         reads=[B_mix, B_wb[nchunk // 4]], writes=[bk[b]])
                k = nchunk % 2
                P.op("dve", ttf(x1tmp[k][:, :T], bank(b, 128, T), xqT_t[:, nchunk, :T], ALU.add), reads=[bk[b], B_xqT], writes=[B_x1t[k]])
                P.op("act", actf(x1g[:, nchunk, :T], x1tmp[k][:, :T], AF.Copy, scale=gcols[:, 8 + nchunk:9 + nchunk]), reads=[B_x1t[k], B_c], writes=[B_x1g])
            w_up_v = dr["w_up"].rearrange("(c p) n -> p c n", p=128)
            for pc in range(8):
                wi = pc % 2
                wu = wbuf[wi].rearrange("p (c n) -> p c n", n=512)
                P.dma("sp", wu, wu_scr[pc], writes=[B_wb[wi]])
                for mc in range(4):
                    m = pc * 4 + mc
                    b = 4 + rot % 4; rot += 1
                    for c in range(8):
                        P.op("pe", mm(bank(b, 128, T), wu[:, c, mc * 128:(mc + 1) * 128], x1g[:, c, :T], c == 0, c == 7), reads=[B_x1g, B_wb[wi]], writes=[bk[b]])
                    k = m % 2
                    P.op("act", actf(rl[k][:, :T], bank(b, 128, T), AF.Relu), reads=[bk[b]], writes=[B_rl[k]])
                    a_m = WBIG[:, m // 4, 2048 + (m % 4) * 512: 2048 + (m % 4) * 512 + T]
                    P.op("dve", ttf(a_m, rl[k][:, :T], rl[k][:, :T], ALU.mult), reads=[B_rl[k]], writes=[B_Wq])
            w_dn_v = dr["w_down"].rearrange("(m p) n -> p m n", p=128)
            for pc in range(8):
                wi = pc % 2
                wd = wbuf[wi].rearrange("p (m n) -> p m n", n=1024)
                P.dma("sp", wd, wd_scr[pc], writes=[B_wb[wi]])
                for mc in range(4):
                    m = pc * 4 + mc
                    a_m = WBIG[:, m // 4, 2048 + (m % 4) * 512: 2048 + (m % 4) * 512 + T]
                    for si, (c0, tsz) in enumerate(subs):
                        for nh in range(2):
                            b = 2 * si + nh
                            P.op("pe", mm(bank(b, tsz, 512), a_m[:, c0:c0 + tsz], wd[:, mc, nh * 512:(nh + 1) * 512], m == 0, m == 31),
                                 reads=[B_Wq, B_wb[wi]], writes=[bk[b]])
            if after_down is not None:
                after_down()
            for si, (c0, tsz) in enumerate(subs):
                for nh in range(2):
                    b = 2 * si + nh
                    P.op("dve", sttf(r_t[:tsz, si, nh * 512:(nh + 1) * 512], bank(b, tsz, 512), mcol[:tsz, 16 + si:17 + si], r_t[:tsz, si, nh * 512:(nh + 1) * 512], ALU.mult, ALU.add),
                         reads=[bk[b], B_mc, B_r], writes=[B_r])
                P.op("act", actf(junk[:tsz, :], r_t[:tsz, si, :], AF.Square, accum_out=mcol[:tsz, 4 + si:5 + si]), reads=[B_r, B_mc], writes=[B_mc, B_rt])
            rstd_from(mcol[:tszm, 4:4 + nsub], mcol[:tszm, 12:12 + nsub], D, tszm, nsub, [B_mc], [B_mc], lntmp[:tszm, :nsub])
            for si, (c0, tsz) in enumerate(subs):
                P.op("dve", sttf(r_t[:tsz, si, :], r_t[:tsz, si, :], mcol[:tsz, 12 + si:13 + si], gfin[:tsz, :], ALU.mult, ALU.mult), reads=[B_r, B_mc, B_c], writes=[B_r])
            P.dma("sp", y_out, r_t[:subs[0][1], :nsub, :], reads=[B_r])

        P.op("dve", msf(smallc[:, 9:10], 1.0), writes=[B_small])

        subs4 = [(s * 128, 128) for s in range(4)]
        for j in range(NSLOT):
            if j > 0:
                P.dma("sp", cosF_t[:, :TQ], dr["cosQ"][j], writes=[B_cs])
                P.dma("sp", sinF_t[:, :TQ], dr["sinQ"][j], writes=[B_cs])
                P.dma("sp", cosT_t[:, :4, :], dr["cosT"][j], writes=[B_cst])
                P.dma("sp", sinT_t[:, :4, :], dr["sinT"][j], writes=[B_cst])
            project(TQ, subs4, dr["xqT"][j].rearrange("(c p) t -> p c t", p=128), dr["cosQ"][j], dr["sinQ"][j], True,
                    lambda g: kTo[:, g, :], B_kTo, lambda si, tsz: Vo[:, si, :], B_Vo,
                    q_dst=lambda g: qT[:, g, :], B_qdst=B_qT,
                    zt_out=lambda si, tsz, j=j: dr["kvo"][j, si * 128:(si + 1) * 128, :],
                    cosT_ap=dr["cosT"][j], sinT_ap=dr["sinT"][j], do_loads=(j == 0))
            P.barrier()
            NB = 32 * j + 29
            nseg = (NB + 15) // 16
            segs = list(range(nseg - 1, -1, -1))
            items = [(g, s_) for g in range(8) for s_ in segs]
            loaded = {"n": 0}

            def load_item(t, NB=NB):
                g, s_ = items[t]
                sl = t % 3
                nb = min(16, NB - 16 * s_)
                P.dma("sp", Kseg[sl][:, :nb * 128], kT_scr[g, :, s_ * 2048: s_ * 2048 + nb * 128], writes=[B_seg[sl]])
                P.dma("sp", Vseg[sl][:, :nb, :], v_scr[g, :, 16 * s_:16 * s_ + nb, :], writes=[B_seg[sl]])

            def groups_blocks(g, j=j, NB=NB, nseg=nseg, segs=segs, items=items, loaded=loaded, load_item=load_item):
                def pro():
                    while loaded["n"] < min(3, len(items)):
                        load_item(loaded["n"]); loaded["n"] += 1

                vc = 512 + 128 * g if g < 4 else 128 * (g - 4)
                blocks = []
                for r in (3, 2, 1, 0):
                    blocks.append(dict(kT=kTo[:, g, r * 128:(r + 1) * 128], v=Vo[:, r, vc:vc + 128], bias=kbias[:, ZB:ZB + 1], nk=128,
                                       msb=dmask[:, r, :], mdf=dmask[:, 4 + r, :], bufs=[B_kTo, B_Vo], item=None, last=False))
                for sidx, s_ in enumerate(segs):
                    t = g * nseg + sidx
                    sl = t % 3
                    nb = min(16, NB - 16 * s_)
                    for bb in range(nb - 1, -1, -1):
                        b_ = 16 * s_ + bb
                        blocks.append(dict(kT=Kseg[sl][:, bb * 128:(bb + 1) * 128], v=Vseg[sl][:, bb, :], bias=kbias[:, j * NBLK + b_: j * NBLK + b_ + 1], nk=128,
                                           msb=None, mdf=None, bufs=[B_seg[sl]], item=t, last=(bb == 0)))

                def after(i):
                    bl = blocks[i]
                    if bl["item"] is not None and bl["last"] and loaded["n"] < len(items):
                        load_item(loaded["n"]); loaded["n"] += 1
                return pro, blocks, after

            xq_ap_j = dr["xq"][j].rearrange("(s p) n -> p s n", p=128)
            xqT_ap_j = dr["xqT"][j].rearrange("(c p) t -> p c t", p=128)
            attend(TQ, lambda g: qT[:, g, :], groups_blocks, make_finish(TQ, lambda ch: mixT[:, ch, :]),
                   mid_hook=lambda: mlp_loads(TQ, subs4, xq_ap_j, xqT_ap_j))
            P.barrier()
            def next_prefetch(j=j):
                if j + 1 < NSLOT:
                    P.dma("sp", WBIG[:, :, 2048:4096], wq_scr[:, :, :], writes=[B_Wq])
            if j + 1 < NSLOT:
                P.dma("sp", xin2[0][:, :, :TQ], dr["xqT"][j + 1].rearrange("(c p) t -> p c t", p=128), writes=[B_xin0])
            mlp(TQ, subs4, xq_ap_j, xqT_ap_j, dr["y"][j].rearrange("(s p) n -> p s n", p=128), preloaded=True, after_down=next_prefetch)
            P.barrier()

        P.dma("sp", WBIG[:, :, 2048:4096], wq_scr[:, :, :], writes=[B_Wq])
        subs_s = [(b * DEC_T, DEC_T) for b in range(BPC)]
        project(ST, subs_s, dr["xsT"].rearrange("(c p) t -> p c t", p=128), dr["cosS"][:, :], dr["sinS"][:, :], True,
                lambda g: kTo[:, g, :ST], B_kTo, lambda si, tsz: Vo[:tsz, si, :], B_Vo,
                q_dst=lambda g: qT[:, g, :ST], B_qdst=B_qT,
                zt_out=lambda si, tsz: dr["kvs"][si * DEC_T:(si + 1) * DEC_T, :],
                cosT_ap=dr["cosST"], sinT_ap=dr["sinST"])
        P.barrier()
        ckb = [SB[:, i * 8192:(i + 1) * 8192].rearrange("p (b n) -> p b n", n=1024) for i in range(2)]
        cvb = [WBIG[:, :, 2048 + i * 1024: 3072 + i * 1024] for i in range(2)]
        kTcb = [SF[:, 6144 + i * 4096: 10240 + i * 4096].bitcast(BF16).rearrange("p (g t) -> p g t", t=1024) for i in range(2)]
        B_ck = [Buf(), Buf()]; B_cv = [Buf(), Buf()]; B_kTc = [Buf(), Buf()]
        sm = SB[:, 16384:18432]
        L_f = [sm[:, i * 128:(i + 1) * 128] for i in range(2)]; B_Lf = [Buf(), Buf()]
        A_f = [sm[:, 256 + i * 128: 256 + (i + 1) * 128] for i in range(2)]; B_Af = [Buf(), Buf()]
        La_f = [sm[:, 512 + i * 128: 512 + (i + 1) * 128] for i in range(2)]; B_Laf = [Buf(), Buf()]
        P_f = [sm[:, 768 + i * 128: 768 + (i + 1) * 128] for i in range(3)]; B_Pf = [Buf(), Buf(), Buf()]
        sq_s = sm[:, 1152:1216]; B_sqs = Buf()
        u_f = [SF[:, i * 128:(i + 1) * 128] for i in range(2)]; B_uf = [Buf(), Buf()]
        mKV = SF[:, 3072:4096].bitcast(BF16)
        metaK = mKV[:, 0:1024].rearrange("p (g k) -> p g k", k=128); metaV = mKV[:, 1024:2048].rearrange("p (g k) -> p g k", k=128); B_mkv = Buf()
        rl_s = SF[:, 2048:2176]; o_s = SF[:, 2176:2304]; od_s = SF[:, 2304:2368]; rs_s = SF[:, 2368:2432]; B_fs = Buf()
        qz = SF[:, 2560:3072].bitcast(BF16).rearrange("p (m t) -> p m t", t=ST); B_qz = Buf()
        P.op("pool", msf(qz[:, :, :], 0.0), writes=[B_qz])
        qzs = qz[:, 0:8, :].rearrange("p (g two) t -> p g two t", two=2)
        qzd = qz[:, 8:16, :].rearrange("p (g two) t -> p g two t", two=2)
        P.op("pool", cpf(qzs[0:64, :, 0, :], qT[0:64, 0:4, 0:ST]), reads=[B_qT], writes=[B_qz])
        P.op("pool", cpf(qzs[64:128, :, 1, :], qT[64:128, 0:4, 0:ST]), reads=[B_qT], writes=[B_qz])
        P.op("pool", cpf(qzd[0:64, :, 0, :], qT[0:64, 4:8, 0:ST]), reads=[B_qT], writes=[B_qz])
        P.op("pool", cpf(qzd[64:128, :, 1, :], qT[64:128, 4:8, 0:ST]), reads=[B_qT], writes=[B_qz])
        for g in range(8):
            P.dma("sp", metaK[:, g, :], kT_scr[g, :, 0:128], writes=[B_mkv])
            P.dma("sp", metaV[:, g, :], v_scr[g, :, 0, :], writes=[B_mkv])

        def s_loads(bi):
            p_ = bi % 2
            for (nm, off) in (("cdk", 0), ("csk", 512)):
                P.dma("pool", ckb[p_][:, :, off:off + 512], dr[nm][bi].rearrange("(b p) n -> p b n", p=128), writes=[B_ck[p_]])
            for (nm, off) in (("cdv", 0), ("csv", 512)):
                P.dma("pool", cvb[p_][:, :, off:off + 512], dr[nm][bi].rearrange("(b p) n -> p b n", p=128), writes=[B_cv[p_], B_Wq])

        def s_transposes(bi):
            p_ = bi % 2
            rotk = 0
            for g in range(8):
                fc = 512 + 128 * g if g < 4 else 128 * (g - 4)
                for half in range(2):
                    b_ = rotk % 2; rotk += 1
                    for q4 in range(4):
                        blk = half * 4 + q4
                        P.op("pe", mm(ps[:, b_ * 512 + q4 * 128: b_ * 512 + (q4 + 1) * 128], ckb[p_][:, blk, fc:fc + 128], ident), reads=[B_ck[p_], B_c], writes=[bk[b_]])
                    P.op("dve", cpf(kTcb[p_][:, g, half * 512:(half + 1) * 512], bank(b_)), reads=[bk[b_]], writes=[B_kTc[p_]])

        W16 = DEC_T

        def s_attend(bi):
            p_ = bi % 2
            q0 = bi * W16
            vcol = lambda g: 512 + 128 * g if g < 4 else 128 * (g - 4)
            blocks = [dict(kT=lambda g: kTo[:, g, q0:q0 + W16], v=lambda g: Vo[:W16, bi, vcol(g):vcol(g) + 128], bias=kbias[:W16, ZB:ZB + 1], nk=W16,
                           mask=True, bufs=[B_kTo, B_Vo])]
            for blk in range(7, -1, -1):
                blocks.append(dict(kT=lambda g, blk=blk: kTcb[p_][:, g, blk * 128:(blk + 1) * 128], v=lambda g, blk=blk: cvb[p_][:, blk, vcol(g):vcol(g) + 128],
                                   bias=kbias[:, ZB:ZB + 1], nk=128, mask=False, bufs=[B_kTc[p_], B_cv[p_]]))
            blocks.append(dict(kT=lambda g: metaK[:, g, :], v=lambda g: metaV[:, g, :], bias=kbias[:, MB:MB + 1], nk=128, mask=False, bufs=[B_mkv]))
            n = len(blocks)
            SS = [0, 1, 2]; SD = [3, 4, 5]
            mask3 = dmask[:W16, 0, :W16].unsqueeze(1).to_broadcast([W16, 8, W16])

            def zmm(i):
                bl = blocks[i]; nk = bl["nk"]; sb_ = SS[i % 3]
                for h in range(8):
                    g = h // 2
                    P.op("pe", mm(ps[:nk, sb_ * 512 + h * W16: sb_ * 512 + (h + 1) * W16], bl["kT"](g), qz[:, h, q0:q0 + W16], h == 0, h == 7),
                         reads=bl["bufs"] + [B_qz], writes=[bk[sb_]])

            def uL(i):
                bl = blocks[i]; nk = bl["nk"]; sb_ = SS[i % 3]
                P.op("act", actf(u_f[i % 2][:nk, :], bank(sb_, nk, 128), AF.Exp, bias=bl["bias"]), reads=[bk[sb_], B_c], writes=[B_uf[i % 2]])
                P.op("act", actf(L_f[i % 2][:nk, :], u_f[i % 2][:nk, :], AF.Ln, bias=smallc[:nk, 9:10]), reads=[B_uf[i % 2], B_small], writes=[B_Lf[i % 2]])
                if bl["mask"]:
                    Lv = L_f[i % 2][:nk, :].rearrange("p (h w) -> p h w", w=W16)
                    P.op("dve", ttf(Lv, Lv, mask3, ALU.mult), reads=[B_Lf[i % 2], B_c], writes=[B_Lf[i % 2]])

            def qk(i):
                bl = blocks[i]; nk = bl["nk"]; sd_ = SD[i % 3]
                for m in range(8):
                    g = 4 + m // 2
                    P.op("pe", mm(ps[:nk, sd_ * 512 + m * W16: sd_ * 512 + (m + 1) * W16], bl["kT"](g), qz[:, 8 + m, q0:q0 + W16], m == 0, m == 7),
                         reads=bl["bufs"] + [B_qz], writes=[bk[sd_]])

            zmm(0); qk(0)
            if n > 1:
                zmm(1); qk(1)
            uL(0)
            for i in range(n):
                bl = blocks[i]; nk = bl["nk"]; sb_ = SS[i % 3]; sd_ = SD[i % 3]
                P.op("act", actf(P_f[i % 3][:nk, :], bank(sd_, nk, 128), AF.Exp, bias=bl["bias"]), reads=[bk[sd_], B_c], writes=[B_Pf[i % 3]])
                if i + 2 < n:
                    zmm(i + 2)
                P.op("pe", mm(bank(sb_, nk, 128), ntri[:nk, :nk], L_f[i % 2][:nk, :], False, i == 0, sgc=True), reads=[B_Lf[i % 2], B_c], writes=[bk[sb_]])
                if i > 0:
                    pk = blocks[i - 1]["nk"]
                    P.op("pe", mm(bank(sb_, nk, 128), nones[:pk, :nk], La_f[i % 2][:pk, :], False, True, sgc=True), reads=[B_Laf[i % 2], B_c], writes=[bk[sb_]])
                if i + 1 < n:
                    if i == 0:
                        P.op("dve", cpf(La_f[1][:nk, :], L_f[0][:nk, :]), reads=[B_Lf[0]], writes=[B_Laf[1]])
                    else:
                        pk = blocks[i - 1]["nk"]
                        if pk < nk:
                            P.op("dve", cpf(La_f[(i + 1) % 2][:nk, :], L_f[i % 2][:nk, :]), reads=[B_Lf[i % 2]], writes=[B_Laf[(i + 1) % 2]])
                            P.op("dve", ttf(La_f[(i + 1) % 2][:pk, :], La_f[(i + 1) % 2][:pk, :], La_f[i % 2][:pk, :], ALU.add), reads=[B_Laf[i % 2]], writes=[B_Laf[(i + 1) % 2]])
                        else:
                            P.op("dve", ttf(La_f[(i + 1) % 2][:nk, :], La_f[i % 2][:nk, :], L_f[i % 2][:nk, :], ALU.add), reads=[B_Lf[i % 2], B_Laf[i % 2]], writes=[B_Laf[(i + 1) % 2]])
                    uL(i + 1)
                if i + 2 < n:
                    qk(i + 2)
                for m in range(8):
                    g = 4 + m // 2
                    P.op("pe", mm(ps[:, 7 * 512 + m * W16: 7 * 512 + (m + 1) * W16], bl["v"](g), P_f[i % 3][:nk, m * W16:(m + 1) * W16], i == 0 and m == 0, False),
                         reads=bl["bufs"] + [B_Pf[i % 3]], writes=[bk[7]])
                P.op("pe", mm(ps[:, 7 * 512 + 128: 7 * 512 + 256], ones[:nk, :], P_f[i % 3][:nk, :], False, i == n - 1), reads=[B_Pf[i % 3], B_c], writes=[bk[7]])
                P.op("act", actf(A_f[i % 2][:nk, :], bank(sb_, nk, 128), AF.Exp, bias=bl["bias"]), reads=[bk[sb_], B_c], writes=[B_Af[i % 2]])
                if bl["mask"]:
                    Av = A_f[i % 2][:nk, :].rearrange("p (h w) -> p h w", w=W16)
                    P.op("dve", ttf(Av, Av, mask3, ALU.mult), reads=[B_Af[i % 2], B_c], writes=[B_Af[i % 2]])
                for h in range(8):
                    g = h // 2
                    P.op("pe", mm(ps[:, 6 * 512 + h * W16: 6 * 512 + (h + 1) * W16], bl["v"](g), A_f[i % 2][:nk, h * W16:(h + 1) * W16], i == 0 and h == 0, i == n - 1 and h == 7),
                         reads=bl["bufs"] + [B_Af[i % 2]], writes=[bk[6]])
            acc6 = ps[:, 6 * 512: 6 * 512 + 128].rearrange("p (g two w) -> p g two w", two=2, w=W16)
            P.op("dve", cpf(mixT[0:64, 4:8, q0:q0 + W16], acc6[0:64, :, 0, :]), reads=[bk[6]], writes=[B_mix])
            P.op("dve", cpf(mixT[64:128, 4:8, q0:q0 + W16], acc6[64:128, :, 1, :]), reads=[bk[6]], writes=[B_mix])
            P.op("dve", lambda e: e.reciprocal(out=rl_s, in_=ps[:, 7 * 512 + 128: 7 * 512 + 256]), reads=[bk[7]], writes=[B_fs])
            P.op("dve", ttf(o_s, ps[:, 7 * 512: 7 * 512 + 128], rl_s, ALU.mult), reads=[bk[7], B_fs], writes=[B_fs])
            ov = o_s.rearrange("p (h c w) -> p h c w", c=2, w=W16)
            odv = od_s.rearrange("p (h w) -> p h w", w=W16)
            P.op("dve", sttf(odv, ov[:, :, 1, :], smallc[:, 1:2], ov[:, :, 0, :], ALU.mult, ALU.add), reads=[B_fs, B_small], writes=[B_fs])
            P.op("act", actf(sq_s, od_s, AF.Square), reads=[B_fs], writes=[B_sqs])
            P.op("pe", mm(ps[:, 3 * 512: 3 * 512 + 64], ones, sq_s), reads=[B_sqs, B_c], writes=[bk[3]])
            rstd_from(ps[:, 3 * 512: 3 * 512 + 64], rs_s, 128, 128, 64, [bk[3]], [B_fs])
            P.op("dve", sttf(mixT[:, 0:4, q0:q0 + W16], odv, smallc[:, 2:3], rs_s.rearrange("p (h w) -> p h w", w=W16), ALU.mult, ALU.mult),
                 reads=[B_fs, B_small], writes=[B_mix])

        s_loads(0)
        s_loads(1)
        s_transposes(0)
        for bi in range(BPC):
            s_attend(bi)
            if bi + 2 < BPC:
                s_loads(bi + 2)
            if bi + 1 < BPC:
                s_transposes(bi + 1)
        P.barrier()
        mlp(ST, [(0, ST)], dr["xs"][:, :].unsqueeze(1), dr["xsT"].rearrange("(c p) t -> p c t", p=128), dr["ys"][:, :].unsqueeze(1))
        P.barrier()

        with nc.Block() as block:
            @block.tensor
            def _(e):
                P.replay("pe", e, sems, dsems)

            @block.scalar
            def _(e):
                P.replay("act", e, sems, dsems)

            @block.vector
            def _(e):
                P.replay("dve", e, sems, dsems)

            @block.gpsimd
            def _(e):
                P.replay("pool", e, sems, dsems)

            @block.sync
            def _(e):
                P.replay("sp", e, sems, dsems)
    return nc


_NC = None


def _rope_tables(pos):
    half = 32
    inv = (np.float32(10000.0) ** (-(np.arange(half, dtype=np.float32) / np.float32(half)))).astype(np.float32)
    ang = (pos.astype(np.float32)[:, None] * inv[None, :]).astype(np.float32)
    return np.cos(ang).astype(np.float32), np.sin(ang).astype(np.float32)


def _fmajor(cos, sin):
    p = np.arange(128); d = p % 64; f = d % 32
    sgn = np.where(d < 32, -1.0, 1.0).astype(np.float32)
    return np.ascontiguousarray(cos[:, f].T), np.ascontiguousarray((sin[:, f] * sgn[None, :]).T)


def kernel(x_prompt, x_sample, cache_diff_k, cache_diff_v, cache_sb_k, cache_sb_v, meta_tokens, g_mix, w_in,
           lambda_q1, lambda_k1, lambda_q2, lambda_k2, g_diff_head, w_out, g_mlp, w_up, w_down, g_final):
    global _NC
    f32 = np.float32
    bf = ml_dtypes.bfloat16
    A = lambda a: np.ascontiguousarray(np.asarray(a, dtype=f32))
    xp = A(x_prompt)[0]
    meta = A(meta_tokens)
    xT = np.ascontiguousarray(np.concatenate([meta, xp], axis=0).T)
    cosP, sinP = _rope_tables(np.arange(TP))
    cosF, sinF = _fmajor(cosP, sinP)
    gcols = np.concatenate([A(g_mix)[0].reshape(8, 128).T, A(g_mlp)[0].reshape(8, 128).T], axis=1)
    gfin = np.ascontiguousarray(np.broadcast_to(A(g_final)[None, :], (128, D)))
    ghead = A(g_diff_head)[0].reshape(128, 1)
    lamv = np.ascontiguousarray(np.broadcast_to(np.concatenate([A(lambda_q1)[0], A(lambda_k1)[0], A(lambda_q2)[0], A(lambda_k2)[0]])[None, :], (128, 256)))
    k = np.arange(128)[:, None]; q = np.arange(512)[None, :]
    dmask = np.zeros((128, 8, 512), f32)
    for r in range(4):
        dmask[:, r, :] = (128 * r + k < q)
        dmask[:, 4 + r, :] = ((128 * r + k) // 64 <= q // 64)
    dmask = dmask.astype(bf)
    consts = np.zeros((128, 512), f32)
    consts[:, 0:128] = -((np.arange(128)[:, None] >= np.arange(128)[None, :]).astype(f32))
    consts[:, 128:256] = -1.0
    consts[:, 256:384] = 1.0
    consts[:, 384:512] = np.eye(128)
    consts = consts.astype(bf)
    cosS_, sinS_ = _rope_tables(NMETA + PAST + np.arange(DEC_T))
    cosSF, sinSF = _fmajor(np.tile(cosS_, (BPC, 1)), np.tile(sinS_, (BPC, 1)))
    xs_all = A(x_sample)
    cdk = A(cache_diff_k)[0].reshape(DEC_B, PAST, 512); csk = A(cache_sb_k)[0].reshape(DEC_B, PAST, 512)
    cdv = A(cache_diff_v)[0].reshape(DEC_B, PAST, 512); csv = A(cache_sb_v)[0].reshape(DEC_B, PAST, 512)
    shared = dict(xT=xT, w_in=A(w_in)[0], w_out=A(w_out)[0], w_up=A(w_up)[0], w_down=A(w_down)[0], gcols=np.ascontiguousarray(gcols),
                  gfin=gfin, ghead=np.ascontiguousarray(ghead), lamv=lamv, cosF=cosF, sinF=sinF, dmask=dmask, consts=consts,
                  cosS=cosSF, sinS=sinSF, cosST=np.ascontiguousarray(np.broadcast_to(cosS_[:, None, :], (DEC_T, BPC, 32))), sinST=np.ascontiguousarray(np.broadcast_to(sinS_[:, None, :], (DEC_T, BPC, 32))),
                  cosMT=np.ascontiguousarray(cosP[0:NMETA, None, :]), sinMT=np.ascontiguousarray(sinP[0:NMETA, None, :]),
                  cosTp=np.ascontiguousarray(cosP[NMETA:NMETA + 31 * TQ].reshape(31, 4, 128, 32).transpose(0, 2, 1, 3)),
                  sinTp=np.ascontiguousarray(sinP[NMETA:NMETA + 31 * TQ].reshape(31, 4, 128, 32).transpose(0, 2, 1, 3)))
    in_maps = []
    for c in range(NCORES):
        m = dict(shared)
        tiles = [8 * j + c for j in range(NSLOT)]
        m["xqT"] = np.ascontiguousarray(np.stack([xp[TQ * g:TQ * (g + 1)].T for g in tiles]))
        m["xq"] = np.ascontiguousarray(np.stack([xp[TQ * g:TQ * (g + 1)] for g in tiles]))
        m["cosQ"] = np.ascontiguousarray(np.stack([cosF[:, NMETA + TQ * g: NMETA + TQ * (g + 1)] for g in tiles]))
        m["sinQ"] = np.ascontiguousarray(np.stack([sinF[:, NMETA + TQ * g: NMETA + TQ * (g + 1)] for g in tiles]))
        m["cosT"] = np.ascontiguousarray(np.stack([cosP[NMETA + TQ * g: NMETA + TQ * (g + 1)].reshape(4, 128, 32).transpose(1, 0, 2) for g in tiles]))
        m["sinT"] = np.ascontiguousarray(np.stack([sinP[NMETA + TQ * g: NMETA + TQ * (g + 1)].reshape(4, 128, 32).transpose(1, 0, 2) for g in tiles]))
        if c == 0:
            pass
        kb = np.zeros((128, NSLOT * NBLK + 2), f32)
        for j in range(NSLOT):
            for b in range(NBLK):
                if b == 0:
                    kb[16:, j * NBLK] = NEG
                elif (b - 1) >= 4 * (8 * j + c):
                    kb[:, j * NBLK + b] = NEG
        kb[16:, NSLOT * NBLK + 1] = NEG
        m["kbias"] = kb
        bs = slice(BPC * c, BPC * (c + 1))
        m["xsT"] = np.ascontiguousarray(xs_all[bs].reshape(ST, D).T)
        m["xs"] = np.ascontiguousarray(xs_all[bs].reshape(ST, D))
        m["cdk"] = np.ascontiguousarray(cdk[bs]); m["csk"] = np.ascontiguousarray(csk[bs])
        m["cdv"] = np.ascontiguousarray(cdv[bs]); m["csv"] = np.ascontiguousarray(csv[bs])
        in_maps.append(m)
    if _NC is None:
        _NC = build_program()
    res = run_bass_kernel_spmd(_NC, in_maps, core_ids=list(range(NCORES)))
    y_prompt = np.zeros((1, SEQ, D), f32)
    kv_p = np.zeros((TP, 2048), f32)
    y_sample = np.zeros((DEC_B, DEC_T, D), f32)
    kv_s = np.zeros((DEC_B, DEC_T, 2048), f32)
    for c in range(NCORES):
        r = res.results[c]
        for j in range(NSLOT):
            g = 8 * j + c
            y_prompt[0, TQ * g:TQ * (g + 1)] = r["y"][j]
            kv_p[NMETA + TQ * g: NMETA + TQ * (g + 1)] = r["kvo"][j]
        if c == 0:
            kv_p[0:NMETA] = r["metao"]
        y_sample[BPC * c:BPC * (c + 1)] = np.asarray(r["ys"]).reshape(BPC, DEC_T, D)
        kv_s[BPC * c:BPC * (c + 1)] = np.asarray(r["kvs"]).reshape(BPC, DEC_T, 2048)
    outs = (y_prompt, y_sample,
            kv_p[:, 0:512].reshape(1, 1, TP, 4, 2, 64).copy(), kv_p[:, 1024:1536].reshape(1, 1, TP, 4, 128).copy(),
            kv_p[:, 512:1024].reshape(1, 1, TP, 8, 64).copy(), kv_p[:, 1536:2048].reshape(1, 1, TP, 8, 64).copy(),
            kv_s[:, :, 0:512].reshape(1, DEC_B, DEC_T, 4, 2, 64).copy(), kv_s[:, :, 1024:1536].reshape(1, DEC_B, DEC_T, 4, 128).copy(),
            kv_s[:, :, 512:1024].reshape(1, DEC_B, DEC_T, 8, 64).copy(), kv_s[:, :, 1536:2048].reshape(1, DEC_B, DEC_T, 8, 64).copy())
    return outs
```

```python
import contextlib
import numpy as np
import ml_dtypes
import concourse.bass as bass
import concourse.mybir as mybir
from concourse.bass_utils import run_bass_kernel_spmd

F32 = mybir.dt.float32
BF16 = mybir.dt.bfloat16
AF = mybir.ActivationFunctionType
ALU = mybir.AluOpType

NCORES = 8
D = 1024
SEQ = 16384
NMETA = 16
TP = NMETA + SEQ
TQ = 512
NSLOT = 4
NBLK = 129
NEG = -30000.0
EPS = 1e-6
DEC_B = 32
DEC_T = 16
PAST = 1024
BPC = DEC_B // NCORES
ST = BPC * DEC_T
NDMA = 56
NDMA_HW = 44

C_DK, C_SK, C_DV, C_SV, C_DQ, C_SQ, C_DQS, C_DKS = 0, 512, 1024, 1536, 2048, 2560, 3072, 3584


class Buf:
    __slots__ = ("w", "r")

    def __init__(self):
        self.w = {}
        self.r = {}


class Prog:
    CE = ("pe", "act", "dve", "pool")

    def __init__(self):
        self.q = {e: [] for e in self.CE + ("sp",)}
        self.cnt = {e: 0 for e in self.CE}
        self.waited = {e: {} for e in self.CE + ("sp",)}
        self.dma_vals = [0] * NDMA
        self.rr = 0
        self.rr_sw = 0

    def _wait(self, eng, key, val):
        if self.waited[eng].get(key, 0) >= val:
            return
        self.waited[eng][key] = val
        self.q[eng].append(("w", key, val))

    def _deps(self, eng, reads, writes):
        for b in reads:
            for k, v in b.w.items():
                if not (eng == "pe" and k == "pe"):
                    self._wait(eng, k, v)
        for b in writes:
            for dct in (b.w, b.r):
                for k, v in dct.items():
                    if not (eng == "pe" and k == "pe"):
                        self._wait(eng, k, v)

    def _mark(self, key, val, reads, writes):
        for b in reads:
            if b.r.get(key, 0) < val:
                b.r[key] = val
        for b in writes:
            b.w[key] = val
            b.r = {}

    def op(self, eng, fn, reads=(), writes=()):
        self._deps(eng, reads, writes)
        self.cnt[eng] += 1
        self.q[eng].append(("o", fn))
        self._mark(eng, self.cnt[eng], reads, writes)

    def dma(self, eng, out, in_, reads=(), writes=()):
        if eng == "pool":
            i = NDMA_HW + self.rr_sw
            self.rr_sw = (self.rr_sw + 1) % (NDMA - NDMA_HW)
        else:
            i = self.rr
            self.rr = (i + 1) % NDMA_HW
        key = ("d", i)
        if self.dma_vals[i] > 0:
            self._wait(eng, key, self.dma_vals[i])
        self._deps(eng, reads, writes)
        self.dma_vals[i] += 16
        self.q[eng].append(("d", out, in_, i))
        self._mark(key, self.dma_vals[i], reads, writes)

    def barrier(self):
        for e in self.CE + ("sp",):
            for o in self.CE:
                if o != e and self.cnt[o] > 0:
                    self._wait(e, o, self.cnt[o])
            for i in range(NDMA):
                if self.dma_vals[i] > 0:
                    self._wait(e, ("d", i), self.dma_vals[i])

    def replay(self, eng, e, sems, dsems):
        for it in self.q[eng]:
            if it[0] == "w":
                k = it[1]
                s = dsems[k[1]] if isinstance(k, tuple) else sems[k]
                e.wait_ge(s, it[2])
            elif it[0] == "o":
                it[1](e).then_inc(sems[eng], 1)
            else:
                e.dma_start(out=it[1], in_=it[2]).then_inc(dsems[it[3]], 16)


def mm(out, lhsT, rhs, start=True, stop=True, sgc=False):
    return lambda e: e.matmul(out, lhsT=lhsT, rhs=rhs, start=start, stop=stop, skip_group_check=sgc)


def actf(out, in_, func, **kw):
    return lambda e: e.activation(out=out, in_=in_, func=func, **kw)


def ttf(out, a, b, op):
    return lambda e: e.tensor_tensor(out=out, in0=a, in1=b, op=op)


def tsf(out, a, s1, op0):
    return lambda e: e.tensor_scalar(out=out, in0=a, scalar1=s1, scalar2=None, op0=op0)


def sttf(out, in0, scalar, in1, op0, op1):
    return lambda e: e.scalar_tensor_tensor(out=out, in0=in0, scalar=scalar, in1=in1, op0=op0, op1=op1)


def cpf(out, in_):
    return lambda e: e.tensor_copy(out=out, in_=in_)


def msf(ap, v):
    return lambda e: e.memset(ap, v)


def build_program():
    nc = bass.Bass("TRN2", target_bir_lowering=False)
    P = Prog()
    dr = {}

    def din(n, shape, dt=F32):
        dr[n] = nc.dram_tensor(n, list(shape), dt, kind="ExternalInput").ap()

    def dout(n, shape, dt=F32):
        dr[n] = nc.dram_tensor(n, list(shape), dt, kind="ExternalOutput").ap()

    din("xT", [D, TP]); din("xqT", [NSLOT, D, TQ]); din("xq", [NSLOT, TQ, D])
    din("w_in", [D, 3 * D]); din("w_out", [D, D]); din("w_up", [D, 4 * D]); din("w_down", [4 * D, D])
    din("gcols", [128, 16]); din("gfin", [128, D]); din("ghead", [128, 1]); din("lamv", [128, 256])
    din("cosF", [128, TP]); din("sinF", [128, TP])
    din("cosQ", [NSLOT, 128, TQ]); din("sinQ", [NSLOT, 128, TQ])
    din("cosT", [NSLOT, 128, 4, 32]); din("sinT", [NSLOT, 128, 4, 32])
    din("dmask", [128, 8, 512], BF16); din("kbias", [128, NSLOT * NBLK + 2]); din("consts", [128, 512], BF16)
    din("xsT", [D, ST]); din("xs", [ST, D])
    din("cdk", [BPC, PAST, 512]); din("csk", [BPC, PAST, 512]); din("cdv", [BPC, PAST, 512]); din("csv", [BPC, PAST, 512])
    din("cosS", [128, ST]); din("sinS", [128, ST]); din("cosST", [16, BPC, 32]); din("sinST", [16, BPC, 32])
    din("cosMT", [16, 1, 32]); din("sinMT", [16, 1, 32])
    din("cosTp", [31, 128, 4, 32]); din("sinTp", [31, 128, 4, 32])
    dout("y", [NSLOT, TQ, D]); dout("kvo", [NSLOT, TQ, 2048]); dout("metao", [NMETA, 2048])
    dout("ys", [ST, D]); dout("kvs", [ST, 2048])
    kT_scr = nc.dram_tensor("kT_scr", [8, 128, NBLK * 128], BF16, kind="Internal").ap()
    v_scr = nc.dram_tensor("v_scr", [8, 128, NBLK, 128], BF16, kind="Internal").ap()
    wo_scr = nc.dram_tensor("wo_scr", [2, 128, 8, 512], BF16, kind="Internal").ap()
    wu_scr = nc.dram_tensor("wu_scr", [8, 128, 8, 512], BF16, kind="Internal").ap()
    wd_scr = nc.dram_tensor("wd_scr", [8, 128, 4, 1024], BF16, kind="Internal").ap()
    wq_scr = nc.dram_tensor("wq_scr", [128, 8, 2048], BF16, kind="Internal").ap()

    es = contextlib.ExitStack()
    with es:
        def sb(name, shape, dt):
            return es.enter_context(nc.sbuf_tensor(name, list(shape), dt))

        WBIG = sb("WBIG", [128, 8, 4096], BF16)
        consts = sb("consts_sb", [128, 512], BF16)
        dmask = sb("dmask_sb", [128, 8, 512], BF16)
        kbias = sb("kbias_sb", [128, NSLOT * NBLK + 2], F32)
        gfin = sb("gfin_sb", [128, D], F32)
        gcols = sb("gcols_sb", [128, 16], F32)
        smallc = sb("smallc", [128, 64], F32)
        lamv = sb("lamv_sb", [128, 256], F32)
        qT = sb("qT", [128, 8, 512], BF16)
        kTo = sb("kTo", [128, 8, 512], BF16)
        Vo = sb("Vo", [128, 4, 1024], BF16)
        mixT = sb("mixT", [128, 8, 512], BF16)
        SF = sb("SF", [128, 14336], F32)
        SB = sb("SB", [128, 18432], BF16)
        ps = es.enter_context(nc.psum_tensor("ps", [128, 4096], F32))
        sems = {e: es.enter_context(nc.semaphore("s_" + e)) for e in Prog.CE}
        dsems = [es.enter_context(nc.semaphore("d%d" % i)) for i in range(NDMA)]

        ntri = consts[:, 0:128]; nones = consts[:, 128:256]; ones = consts[:, 256:384]; ident = consts[:, 384:512]
        ZB = NSLOT * NBLK
        MB = NSLOT * NBLK + 1

        bk = [Buf() for _ in range(8)]

        def bank(b, rows=128, w=512):
            return ps[:rows, b * 512: b * 512 + w]

        B_W = Buf(); B_Wq = Buf(); B_c = Buf(); B_qT = Buf(); B_kTo = Buf(); B_Vo = Buf(); B_mix = Buf(); B_small = Buf()

        xin2 = [SF[:, i * 4096:(i + 1) * 4096].rearrange("p (c t) -> p c t", t=512) for i in range(2)]
        zt = [SF[:, 4096 + i * 2048: 4096 + (i + 1) * 2048] for i in range(2)]; B_zt = [Buf(), Buf()]
        B_xin0 = Buf()
        cosF_t = SF[:, 8192:8704]; sinF_t = SF[:, 8704:9216]; B_cs = Buf()
        cosr = SF[:, 9216:9728]; sinr = SF[:, 9728:10240]; B_csr = Buf()
        rstd_b = SF[:, 10240:10752]; B_rb = Buf()
        t1 = [SF[:, 10752 + i * 512: 10752 + (i + 1) * 512] for i in range(2)]
        t2 = [SF[:, 11776 + i * 512: 11776 + (i + 1) * 512] for i in range(2)]
        B_t = [Buf(), Buf()]
        cosT_t = SF[:, 12800:12928].rearrange("p (s i) -> p s i", i=32)
        sinT_t = SF[:, 12928:13056].rearrange("p (s i) -> p s i", i=32); B_cst = Buf()
        rtmp = [SF[:, 13056 + i * 256: 13056 + (i + 1) * 256] for i in range(4)]; B_rt = Buf()
        lntmp = SF[:, 14080:14336 - 128]
        rcol = SF[:, 14336 - 128: 14336 - 120]; B_rc = Buf()
        sq2 = [SB[:, i * 8192: i * 8192 + 4096].rearrange("p (c t) -> p c t", t=512) for i in range(2)]; B_sq2 = [Buf(), Buf()]
        xg2 = [SB[:, i * 8192 + 4096: i * 8192 + 8192].rearrange("p (c t) -> p c t", t=512) for i in range(2)]; B_xg2 = [Buf(), Buf()]
        u_t = [SF[:, i * 1024:(i + 1) * 1024].rearrange("p (h w) -> p h w", h=2) for i in range(2)]; B_u = [Buf(), Buf()]
        dtmp = [SF[:, 2048 + i * 512: 2048 + (i + 1) * 512] for i in range(6)]; B_dt = Buf()
        Pacc = [SF[:, 5120 + i * 512: 5120 + (i + 1) * 512] for i in range(2)]; B_pa = [Buf(), Buf()]
        Kseg = [SB[:, i * 2048:(i + 1) * 2048] for i in range(3)]
        Vseg = [SB[:, 6144 + i * 2048: 6144 + (i + 1) * 2048].rearrange("p (b c) -> p b c", c=128) for i in range(3)]
        B_seg = [Buf(), Buf(), Buf()]
        L_t = [SB[:, 12288 + i * 1024: 12288 + (i + 1) * 1024].rearrange("p (h w) -> p h w", h=2) for i in range(2)]
        A_t = [SB[:, 14336 + i * 1024: 14336 + (i + 1) * 1024].rearrange("p (h w) -> p h w", h=2) for i in range(2)]
        Lacc = [SB[:, 16384 + i * 1024: 16384 + (i + 1) * 1024].rearrange("p (h w) -> p h w", h=2) for i in range(2)]
        B_L = [Buf(), Buf()]; B_A = [Buf(), Buf()]; B_Lacc = [Buf(), Buf()]
        r_t = SF[:, 6144:10240].rearrange("p (s n) -> p s n", n=1024); B_r = Buf()
        xqT_t = SF[:, 10240:14336].rearrange("p (c t) -> p c t", t=512); B_xqT = Buf()
        rl = [SF[:, 4096 + i * 512: 4096 + (i + 1) * 512] for i in range(2)]; B_rl = [Buf(), Buf()]
        x1tmp = [SF[:, 5120 + i * 512: 5120 + (i + 1) * 512] for i in range(2)]; B_x1t = [Buf(), Buf()]
        mcol = smallc[:, 16:48]; B_mc = Buf()
        junk = SF[:, 4096:5120]
        wbuf = [SB[:, i * 4096:(i + 1) * 4096] for i in range(2)]; B_wb = [Buf(), Buf()]
        x1g = qT; B_x1g = B_qT

        def rstd_from(ss_ap, out_ap, n_feat, rows, width, reads, writes, tmp_ap=None):
            P.op("act", actf(out_ap, ss_ap, AF.Ln, scale=1.0 / n_feat, bias=smallc[:rows, 8:9]), reads=reads + [B_small], writes=writes)
            P.op("act", actf(out_ap, out_ap, AF.Exp, scale=-0.5), reads=writes, writes=writes)

        P.dma("sp", consts[:], dr["consts"][:], writes=[B_c])
        P.dma("sp", dmask[:], dr["dmask"][:], writes=[B_c])
        P.dma("sp", kbias[:], dr["kbias"][:], writes=[B_c])
        P.dma("sp", gfin[:], dr["gfin"][:], writes=[B_c])
        P.dma("sp", gcols[:], dr["gcols"][:], writes=[B_c])
        P.dma("sp", lamv[:], dr["lamv"][:], writes=[B_c])
        P.dma("sp", smallc[:, 0:1], dr["ghead"][:], writes=[B_small])
        w_in_v = dr["w_in"].rearrange("(c p) n -> p c n", p=128)
        for c0 in range(0, 8, 2):
            P.dma("pool", WBIG[:, c0:c0 + 2, 0:2048], w_in_v[:, c0:c0 + 2, 1024:3072], writes=[B_W])

        def load_q_weights():
            for c0 in range(0, 8, 2):
                P.dma("pool", WBIG[:, c0:c0 + 2, 2048:3072], w_in_v[:, c0:c0 + 2, 0:1024], writes=[B_Wq])
            for c in range(8):
                for (s0, d0) in ((C_DQ, C_DQS), (C_DK, C_DKS)):
                    src = WBIG[:, c, s0:s0 + 512].rearrange("p (m h i) -> p m h i", h=2, i=32)
                    dst = WBIG[:, c, d0:d0 + 512].rearrange("p (m h i) -> p m h i", h=2, i=32)
                    P.op("pool", cpf(dst[:, :, 0, :], src[:, :, 1, :]), reads=[B_W, B_Wq], writes=[B_Wq])
                    P.op("pool", cpf(dst[:, :, 1, :], src[:, :, 0, :]), reads=[B_W, B_Wq], writes=[B_Wq])

        w_out_v0 = dr["w_out"].rearrange("(c p) n -> p c n", p=128)
        w_up_v0 = dr["w_up"].rearrange("(c p) n -> p c n", p=128)
        w_dn_v0 = dr["w_down"].rearrange("(m p) n -> p m n", p=128)
        conv_jobs = [(wo_scr[i], w_out_v0[:, :, i * 512:(i + 1) * 512]) for i in range(2)]
        for pc in range(8):
            conv_jobs.append((wu_scr[pc], w_up_v0[:, :, pc * 512:(pc + 1) * 512]))
            conv_jobs.append((wd_scr[pc], w_dn_v0[:, pc * 4:(pc + 1) * 4, :]))
        P.op("dve", msf(smallc[:, 8:9], EPS), writes=[B_small])
        P.op("dve", msf(smallc[:, 3:5], 0.0), writes=[B_small])
        P.op("dve", ttf(lamv[:, 0:64], lamv[:, 0:64], lamv[:, 64:128], ALU.mult), reads=[B_c], writes=[B_c])
        P.op("dve", ttf(lamv[:, 128:192], lamv[:, 128:192], lamv[:, 192:256], ALU.mult), reads=[B_c], writes=[B_c])
        P.op("act", actf(lamv[:, 64:128], lamv[:, 0:64], AF.Copy, accum_out=smallc[:, 3:4]), reads=[B_c, B_small], writes=[B_c, B_small])
        P.op("act", actf(lamv[:, 192:256], lamv[:, 128:192], AF.Copy, accum_out=smallc[:, 4:5]), reads=[B_c, B_small], writes=[B_c, B_small])
        P.op("act", actf(smallc[:, 5:7], smallc[:, 3:5], AF.Exp), reads=[B_small], writes=[B_small])
        P.op("dve", sttf(smallc[:, 1:2], smallc[:, 6:7], -0.2, smallc[:, 5:6], ALU.add, ALU.subtract), reads=[B_small], writes=[B_small])
        P.op("dve", tsf(smallc[:, 2:3], smallc[:, 0:1], 0.8, ALU.mult), reads=[B_small], writes=[B_small])

        fm_rot = [0]; tm_rot = [0]; t_rot = [0]; zt_rot = [0]

        cs2 = [(cosF_t, sinF_t), (SF[:, 13056:13568], SF[:, 13568:14080])]

        def project_loads(T, subs, x_ap, cos_ap, sin_ap, own, cosT_ap=None, sinT_ap=None, par=0):
            nsub = len(subs)
            xin = xin2[par]
            B_xinl = [B_xin0] if par == 0 else [B_zt[0], B_zt[1]]
            B_csl = [B_cs] if par == 0 else [B_rt]
            P.dma("sp", xin[:, :, :T], x_ap, writes=B_xinl)
            P.dma("sp", cs2[par][0][:, :T], cos_ap, writes=B_csl)
            P.dma("sp", cs2[par][1][:, :T], sin_ap, writes=B_csl)
            if own:
                P.dma("sp", cosT_t[:subs[0][1], :nsub, :], cosT_ap, writes=[B_cst])
                P.dma("sp", sinT_t[:subs[0][1], :nsub, :], sinT_ap, writes=[B_cst])

        def project(T, subs, x_ap, cos_ap, sin_ap, own, kT_dst, B_kdst, v_dst, B_vdst, q_dst=None, B_qdst=None,
                    zt_out=None, cosT_ap=None, sinT_ap=None, g0=0, par=0, do_loads=True):
            nsub = len(subs)
            xin = xin2[par]; sq = sq2[par]; xg = xg2[par]; B_sq = B_sq2[par]; B_xg = B_xg2[par]
            B_xinl = [B_xin0] if par == 0 else [B_zt[0], B_zt[1]]
            B_csl = [B_cs] if par == 0 else [B_rt]
            cosF_p, sinF_p = cs2[par]
            if do_loads:
                project_loads(T, subs, x_ap, cos_ap, sin_ap, own, cosT_ap, sinT_ap, par)
            P.op("act", actf(sq[:, :, :T], xin[:, :, :T], AF.Square), reads=B_xinl, writes=[B_sq])
            for c in range(8):
                P.op("dve", tsf(xg[:, c, :T], xin[:, c, :T], gcols[:, g0 + c:g0 + c + 1], ALU.mult), reads=B_xinl + [B_c], writes=[B_xg])
            for c in range(8):
                P.op("pe", mm(bank(6, 128, T), ones, sq[:, c, :T], c == 0, c == 7), reads=[B_sq, B_c], writes=[bk[6]])
            for si, (c0, tsz) in enumerate(subs):
                for c in range(8):
                    P.op("pe", mm(ps[:tsz, 7 * 512 + si: 7 * 512 + si + 1], sq[:, c, c0:c0 + tsz], ones[:, 0:1], c == 0, c == 7),
                         reads=[B_sq, B_c], writes=[bk[7]])
            rstd_from(bank(6, 128, T), rstd_b[:, :T], D, 128, T, [bk[6]], [B_rb], lntmp[:, :T] if T <= 128 else junk[:, :T])
            tszm = max(s[1] for s in subs)
            rstd_from(ps[:tszm, 7 * 512: 7 * 512 + nsub], rcol[:tszm, :nsub], D, tszm, nsub, [bk[7]], [B_rc], lntmp[:tszm, :nsub])
            P.op("dve", ttf(cosr[:, :T], cosF_p[:, :T], rstd_b[:, :T], ALU.mult), reads=B_csl + [B_rb], writes=[B_csr])
            P.op("dve", ttf(sinr[:, :T], sinF_p[:, :T], rstd_b[:, :T], ALU.mult), reads=B_csl + [B_rb], writes=[B_csr])

            def fm_plain(wcol, dst, B_dst, sc):
                b = fm_rot[0] % 4; fm_rot[0] += 1
                for c in range(8):
                    P.op("pe", mm(bank(b, 128, T), WBIG[:, c, wcol:wcol + 128], xg[:, c, :T], c == 0, c == 7), reads=[B_W, B_Wq, B_xg], writes=[bk[b]])
                P.op("dve", sttf(dst, bank(b, 128, T), sc, rstd_b[:, :T], ALU.mult, ALU.mult), reads=[bk[b], B_rb], writes=[B_dst])

            def fm_rope(wcol, scol, dst, B_dst, sc):
                b1 = fm_rot[0] % 4; fm_rot[0] += 1
                b2 = fm_rot[0] % 4; fm_rot[0] += 1
                for c in range(8):
                    P.op("pe", mm(bank(b1, 128, T), WBIG[:, c, wcol:wcol + 128], xg[:, c, :T], c == 0, c == 7), reads=[B_W, B_Wq, B_xg], writes=[bk[b1]])
                for c in range(8):
                    P.op("pe", mm(bank(b2, 128, T), WBIG[:, c, scol:scol + 128], xg[:, c, :T], c == 0, c == 7), reads=[B_W, B_Wq, B_xg], writes=[bk[b2]])
                k = t_rot[0] % 2; t_rot[0] += 1
                P.op("dve", sttf(t1[k][:, :T], bank(b1, 128, T), sc, cosr[:, :T], ALU.mult, ALU.mult), reads=[bk[b1], B_csr], writes=[B_t[k]])
                P.op("dve", sttf(t2[k][:, :T], bank(b2, 128, T), sc, sinr[:, :T], ALU.mult, ALU.mult), reads=[bk[b2], B_csr], writes=[B_t[k]])
                P.op("pool", ttf(dst, t1[k][:, :T], t2[k][:, :T], ALU.add), reads=[B_t[k]], writes=[B_dst])

            for g in range(4):
                fm_plain(C_SK + 128 * g, kT_dst(g), B_kdst, 1.0)
            for h in range(4):
                fm_rope(C_DK + 128 * h, C_DKS + 128 * h, kT_dst(4 + h), B_kdst, 1.0)
            if q_dst is not None:
                for g in range(4):
                    fm_plain(C_SQ + 128 * g, q_dst(g), B_qdst, 0.125)
                for h in range(4):
                    fm_rope(C_DQ + 128 * h, C_DQS + 128 * h, q_dst(4 + h), B_qdst, 0.125)
            for si, (c0, tsz) in enumerate(subs):
                if own:
                    zi = zt_rot[0] % 2; zt_rot[0] += 1
                    z = zt[zi]; Bz = B_zt[zi]
                for nb in (range(4) if own else (2, 3)):
                    b = 4 + tm_rot[0] % 2; tm_rot[0] += 1
                    for c in range(8):
                        P.op("pe", mm(bank(b, tsz, 512), xg[:, c, c0:c0 + tsz], WBIG[:, c, nb * 512:(nb + 1) * 512], c == 0, c == 7),
                             reads=[B_W, B_Wq, B_xg], writes=[bk[b]])
                    if own:
                        P.op("act", actf(z[:tsz, nb * 512:(nb + 1) * 512], bank(b, tsz, 512), AF.Copy, scale=rcol[:tsz, si:si + 1]),
                             reads=[bk[b], B_rc], writes=[Bz])
                    else:
                        P.op("act", actf(v_dst(si, tsz)[:, (nb - 2) * 512:(nb - 1) * 512], bank(b, tsz, 512), AF.Copy, scale=rcol[:tsz, si:si + 1]),
                             reads=[bk[b], B_rc], writes=[B_vdst])
                if own:
                    zv = z[:tsz, 0:512].rearrange("p (m h i) -> p m h i", h=2, i=32)
                    x1 = zv[:, :, 0, :]; x2 = zv[:, :, 1, :]
                    cb = cosT_t[:tsz, si, :].unsqueeze(1).to_broadcast([tsz, 8, 32])
                    sbb = sinT_t[:tsz, si, :].unsqueeze(1).to_broadcast([tsz, 8, 32])
                    ta, tb, tc, td = [rtmp[i][:tsz, :].rearrange("p (m i) -> p m i", i=32) for i in range(4)]
                    P.op("dve", ttf(ta, x1, cb, ALU.mult), reads=[Bz, B_cst], writes=[B_rt])
                    P.op("dve", ttf(tb, x2, sbb, ALU.mult), reads=[Bz, B_cst], writes=[B_rt])
                    P.op("dve", ttf(tc, x2, cb, ALU.mult), reads=[Bz, B_cst], writes=[B_rt])
                    P.op("dve", ttf(td, x1, sbb, ALU.mult), reads=[Bz, B_cst], writes=[B_rt])
                    P.op("dve", ttf(x1, ta, tb, ALU.subtract), reads=[B_rt], writes=[Bz])
                    P.op("dve", ttf(x2, tc, td, ALU.add), reads=[B_rt], writes=[Bz])
                    P.op("pool", cpf(v_dst(si, tsz), z[:tsz, 1024:2048]), reads=[Bz], writes=[B_vdst])
                    P.dma("sp", zt_out(si, tsz), z[:tsz, :], reads=[Bz])

        xT_v = dr["xT"].rearrange("(c p) t -> p c t", p=128)
        kst = [kTo, qT]; B_kst = [B_kTo, B_qT]
        vst = [Vo, mixT[:, :, :].rearrange("p (s a) t -> p s (a t)", a=2)]; B_vst = [B_Vo, B_mix]

        def store_scratch(slot, blk0, nblk):
            for g in range(8):
                P.dma("sp", kT_scr[g, :, blk0 * 128:(blk0 + nblk) * 128], kst[slot][:, g, :nblk * 128], reads=[B_kst[slot]])
                vc = 512 + 128 * g if g < 4 else 128 * (g - 4)
                P.dma("sp", v_scr[g, :, blk0:blk0 + nblk, :], vst[slot][:, :nblk, vc:vc + 128], reads=[B_vst[slot]])

        kb = SF[:, 8192:10240].bitcast(BF16).rearrange("p (s n) -> p s n", n=1024)
        B_kb = [B_cs, B_csr]
        cst2 = [(cosT_t, sinT_t, [B_cst]),
                (SF[:, 10240:10368].rearrange("p (s i) -> p s i", i=32), SF[:, 10368:10496].rearrange("p (s i) -> p s i", i=32), [B_rb])]
        p1rot = {"tm": 0, "tr": 0, "zk": 0}

        def p1_loads(i):
            par = (i + 1) % 2
            p0 = NMETA + TQ * i
            B_xinl = [B_xin0] if par == 0 else [B_zt[0], B_zt[1]]
            P.dma("sp", xin2[par][:, :, :], xT_v[:, :, p0:p0 + TQ], writes=B_xinl)
            P.dma("sp", cst2[par][0][:, :, :], dr["cosTp"][i], writes=cst2[par][2])
            P.dma("sp", cst2[par][1][:, :, :], dr["sinTp"][i], writes=cst2[par][2])

        def p1_prep(i):
            par = (i + 1) % 2
            xin = xin2[par]; sq = sq2[par]; xg = xg2[par]; B_sq = B_sq2[par]; B_xg = B_xg2[par]
            B_xinl = [B_xin0] if par == 0 else [B_zt[0], B_zt[1]]
            P.op("act", actf(sq[:, :, :], xin[:, :, :], AF.Square), reads=B_xinl, writes=[B_sq])
            for c in range(8):
                P.op("dve", tsf(xg[:, c, :], xin[:, c, :], gcols[:, c:c + 1], ALU.mult), reads=B_xinl + [B_c], writes=[B_xg])

        def p1_tile(i):
            par = (i + 1) % 2; sl = (i + 1) % 2
            xin = xin2[par]; sq = sq2[par]; xg = xg2[par]; B_sq = B_sq2[par]; B_xg = B_xg2[par]
            cT, sT, B_ct = cst2[par]
            for si in range(4):
                for c in range(8):
                    P.op("pe", mm(ps[:, 7 * 512 + si: 7 * 512 + si + 1], sq[:, c, si * 128:(si + 1) * 128], ones[:, 0:1], c == 0, c == 7),
                         reads=[B_sq, B_c], writes=[bk[7]])
            rstd_from(ps[:, 7 * 512: 7 * 512 + 4], rcol[:, :4], D, 128, 4, [bk[7]], [B_rc])
            for si in range(4):
                for nb in range(4):
                    b_ = p1rot["tm"] % 4; p1rot["tm"] += 1
                    for c in range(8):
                        P.op("pe", mm(bank(b_), xg[:, c, si * 128:(si + 1) * 128], WBIG[:, c, nb * 512:(nb + 1) * 512], c == 0, c == 7),
                             reads=[B_W, B_xg], writes=[bk[b_]])
                    if nb == 0:
                        k_ = p1rot["zk"] % 2; p1rot["zk"] += 1
                        zk = t1[k_]
                        P.op("act", actf(zk, bank(b_), AF.Copy, scale=rcol[:, si:si + 1]), reads=[bk[b_], B_rc], writes=[B_t[k_]])
                        zv = zk.rearrange("p (m h i) -> p m h i", h=2, i=32)
                        kv_ = kb[:, si, 0:512].rearrange("p (m h i) -> p m h i", h=2, i=32)
                        x1 = zv[:, :, 0, :]; x2 = zv[:, :, 1, :]
                        cb = cT[:, si, :].unsqueeze(1).to_broadcast([128, 8, 32])
                        sbb = sT[:, si, :].unsqueeze(1).to_broadcast([128, 8, 32])
                        ta, tb, tc, td = [rtmp[q_][:, :].rearrange("p (m i) -> p m i", i=32) for q_ in range(4)]
                        P.op("dve", ttf(ta, x1, cb, ALU.mult), reads=[B_t[k_]] + B_ct, writes=[B_rt])
                        P.op("dve", ttf(tb, x2, sbb, ALU.mult), reads=[B_t[k_]] + B_ct, writes=[B_rt])
                        P.op("dve", ttf(tc, x2, cb, ALU.mult), reads=[B_t[k_]] + B_ct, writes=[B_rt])
                        P.op("dve", ttf(td, x1, sbb, ALU.mult), reads=[B_t[k_]] + B_ct, writes=[B_rt])
                        P.op("dve", ttf(kv_[:, :, 0, :], ta, tb, ALU.subtract), reads=[B_rt], writes=B_kb)
                        P.op("dve", ttf(kv_[:, :, 1, :], tc, td, ALU.add), reads=[B_rt], writes=B_kb)
                    elif nb == 1:
                        P.op("act", actf(kb[:, si, 512:1024], bank(b_), AF.Copy, scale=rcol[:, si:si + 1]), reads=[bk[b_], B_rc], writes=B_kb)
                    else:
                        P.op("act", actf(vst[sl][:, si, (nb - 2) * 512:(nb - 1) * 512], bank(b_), AF.Copy, scale=rcol[:, si:si + 1]),
                             reads=[bk[b_], B_rc], writes=[B_vst[sl]])
            if i + 1 < 31:
                p1_prep(i + 1)
            for g in range(8):
                fc = 512 + 128 * g if g < 4 else 128 * (g - 4)
                b_ = 4 + p1rot["tr"] % 3; p1rot["tr"] += 1
                for si in range(4):
                    P.op("pe", mm(ps[:, b_ * 512 + si * 128: b_ * 512 + (si + 1) * 128], kb[:, si, fc:fc + 128], ident), reads=B_kb + [B_c], writes=[bk[b_]])
                eng = "dve" if g % 2 == 0 else "act"
                if eng == "dve":
                    P.op("dve", cpf(kst[sl][:, g, :], bank(b_)), reads=[bk[b_]], writes=[B_kst[sl]])
                else:
                    P.op("act", actf(kst[sl][:, g, :], bank(b_), AF.Copy), reads=[bk[b_]], writes=[B_kst[sl]])

        p1_loads(0)
        p1_prep(0)
        for i in range(31):
            sl = (i + 1) % 2
            if i + 1 < 31:
                p1_loads(i + 1)
            p1_tile(i)
            store_scratch(sl, 1 + 4 * i, 4)
            if i == 3:
                load_q_weights()
            if i == 5:
                P.dma("sp", wq_scr[:, :, :], WBIG[:, :, 2048:4096], reads=[B_Wq])
            if 6 <= i < 6 + len(conv_jobs):
                P.dma("pool", conv_jobs[i - 6][0], conv_jobs[i - 6][1])
        P.op("pool", msf(kst[0][:, :, 0:128], 0.0), writes=[B_kst[0]])
        P.op("pool", msf(vst[0][:, 0, :], 0.0), writes=[B_vst[0]])
        project(NMETA, [(0, NMETA)], xT_v[:, :, 0:NMETA], dr["cosF"][:, 0:NMETA], dr["sinF"][:, 0:NMETA], True,
                lambda g: kst[0][:, g, :NMETA], B_kst[0], lambda si, tsz: vst[0][:tsz, 0, :], B_vst[0],
                zt_out=lambda si, tsz: dr["metao"][:, :],
                cosT_ap=dr["cosMT"], sinT_ap=dr["sinMT"])
        store_scratch(0, 0, 1)
        P.barrier()

        def attend(W, q_ap, groups_blocks, finish, mid_hook=None):
            pending_fin = []
            for g in range(8):
                pro, blocks, after = groups_blocks(g)
                pro()
                n = len(blocks)
                if g < 4:
                    S = [(0, 1), (2, 3), (4, 5)]

                    def zmm(i):
                        bl = blocks[i]; s0, s1 = S[i % 3]; nk = bl["nk"]
                        P.op("pe", mm(bank(s0, nk, W), bl["kT"][0:64, :], q_ap(g)[0:64, :]), reads=bl["bufs"] + [B_qT], writes=[bk[s0]])
                        P.op("pe", mm(bank(s1, nk, W), bl["kT"][64:128, :], q_ap(g)[64:128, :]), reads=bl["bufs"] + [B_qT], writes=[bk[s1]])

                    def Sv(i, nk):
                        s0 = S[i % 3][0]
                        return ps[:nk, s0 * 512:(s0 + 2) * 512].rearrange("p (h w) -> p h w", h=2)[:, :, :W]

                    def u_(i):
                        bl = blocks[i]; nk = bl["nk"]; s0, s1 = S[i % 3]
                        P.op("act", actf(u_t[i % 2][:nk, :, :W], Sv(i, nk), AF.Exp, bias=bl["bias"]), reads=[bk[s0], bk[s1], B_c], writes=[B_u[i % 2]])

                    def L_(i):
                        bl = blocks[i]; nk = bl["nk"]
                        P.op("act", actf(L_t[i % 2][:nk, :, :W], u_t[i % 2][:nk, :, :W], AF.Ln, bias=smallc[:nk, 9:10]), reads=[B_u[i % 2], B_small], writes=[B_L[i % 2]])
                        if bl["msb"] is not None:
                            for h in range(2):
                                P.op("dve", ttf(L_t[i % 2][:nk, h, :W], L_t[i % 2][:nk, h, :W], bl["msb"], ALU.mult), reads=[B_L[i % 2], B_c], writes=[B_L[i % 2]])

                    def E_(k):
                        bl = blocks[k]; nk = bl["nk"]; s0, s1 = S[k % 3]
                        if k == 1:
                            P.op("dve", cpf(Lacc[1][:nk, :, :W], L_t[0][:nk, :, :W]), reads=[B_L[0]], writes=[B_Lacc[1]])
                        elif k > 1:
                            P.op("dve", ttf(Lacc[k % 2][:nk, :, :W], Lacc[(k - 1) % 2][:nk, :, :W], L_t[(k - 1) % 2][:nk, :, :W], ALU.add),
                                 reads=[B_L[(k - 1) % 2], B_Lacc[(k - 1) % 2]], writes=[B_Lacc[k % 2]])
                        for h, sb_ in ((0, s0), (1, s1)):
                            P.op("pe", mm(bank(sb_, nk, W), ntri[:nk, :nk], L_t[k % 2][:nk, h, :W], False, k == 0, sgc=True), reads=[B_L[k % 2], B_c], writes=[bk[sb_]])
                            if k > 0:
                                P.op("pe", mm(bank(sb_, nk, W), nones[:nk, :nk], Lacc[k % 2][:nk, h, :W], False, True, sgc=True), reads=[B_Lacc[k % 2], B_c], writes=[bk[sb_]])

                    zmm(0)
                    if n > 1:
                        zmm(1)
                    if n > 2:
                        zmm(2)
                    u_(0); L_(0)
                    if n > 1:
                        u_(1)
                    E_(0)
                    while pending_fin:
                        pending_fin.pop(0)()
                    for i in range(n):
                        bl = blocks[i]; nk = bl["nk"]; s0, s1 = S[i % 3]
                        if i + 1 < n:
                            L_(i + 1)
                        if i + 2 < n:
                            u_(i + 2)
                        P.op("act", actf(A_t[i % 2][:nk, :, :W], Sv(i, nk), AF.Exp, bias=bl["bias"]), reads=[bk[s0], bk[s1], B_c], writes=[B_A[i % 2]])
                        if bl["msb"] is not None:
                            for h in range(2):
                                P.op("dve" if h == 0 else "pool", ttf(A_t[i % 2][:nk, h, :W], A_t[i % 2][:nk, h, :W], bl["msb"], ALU.mult), reads=[B_A[i % 2], B_c], writes=[B_A[i % 2]])
                        if i + 3 < n:
                            zmm(i + 3)
                        if i + 1 < n:
                            E_(i + 1)
                        for h in range(2):
                            P.op("pe", mm(bank(6 + h, 128, W), bl["v"], A_t[i % 2][:nk, h, :W], i == 0, i == n - 1), reads=bl["bufs"] + [B_A[i % 2]], writes=[bk[6 + h]])
                        after(i)
                else:
                    S = [(0, 1), (2, 3)]
                    A3 = [A_t[0], A_t[1], Lacc[0]]; B_A3 = [B_A[0], B_A[1], B_Lacc[0]]

                    def qk(i):
                        bl = blocks[i]; s0, s1 = S[i % 2]; nk = bl["nk"]
                        P.op("pe", mm(bank(s0, nk, W), bl["kT"][0:64, :], q_ap(g)[0:64, :]), reads=bl["bufs"] + [B_qT], writes=[bk[s0]])
                        P.op("pe", mm(bank(s1, nk, W), bl["kT"][64:128, :], q_ap(g)[64:128, :]), reads=bl["bufs"] + [B_qT], writes=[bk[s1]])

                    qk(0)
                    if n > 1:
                        qk(1)
                    while pending_fin:
                        pending_fin.pop(0)()
                    for i in range(n):
                        bl = blocks[i]; nk = bl["nk"]; s0, s1 = S[i % 2]
                        At = A3[i % 3]; BAt = B_A3[i % 3]
                        Sv_ = ps[:nk, s0 * 512:(s0 + 2) * 512].rearrange("p (h w) -> p h w", h=2)[:, :, :W]
                        P.op("act", actf(At[:nk, :, :W], Sv_, AF.Exp, bias=bl["bias"]), reads=[bk[s0], bk[s1], B_c], writes=[BAt])
                        if bl["mdf"] is not None:
                            for h in range(2):
                                P.op("dve" if h == 0 else "pool", ttf(At[:nk, h, :W], At[:nk, h, :W], bl["mdf"], ALU.mult), reads=[BAt, B_c], writes=[BAt])
                        if i + 2 < n:
                            qk(i + 2)
                        for c in range(2):
                            P.op("pe", mm(bank(4 + 2 * c, 128, W), bl["v"], At[:nk, c, :W], i == 0, i == n - 1), reads=bl["bufs"] + [BAt], writes=[bk[4 + 2 * c]])
                        P.op("pe", mm(bank(5, 128, W), ones[:nk, :], At[:nk, 0, :W], i == 0, i == n - 1), reads=[BAt, B_c], writes=[bk[5]])
                        if i == 0:
                            if nk < 128:
                                P.op("dve", msf(Pacc[1][:, :W], 0.0), writes=[B_pa[1]])
                            P.op("dve", cpf(Pacc[1][:nk, :W], At[:nk, 1, :W]), reads=[BAt], writes=[B_pa[1]])
                        else:
                            P.op("dve", ttf(Pacc[1][:nk, :W], Pacc[1][:nk, :W], At[:nk, 1, :W], ALU.add), reads=[BAt], writes=[B_pa[1]])
                        after(i)
                    hi = L_t[0][:, 1, :W]; lo = L_t[1][:, 1, :W]
                    P.op("dve", cpf(hi, Pacc[1][:, :W]), reads=[B_pa[1]], writes=[B_L[0]])
                    P.op("dve", ttf(lo, Pacc[1][:, :W], hi, ALU.subtract), reads=[B_pa[1], B_L[0]], writes=[B_L[1]])
                    P.op("pe", mm(bank(7, 128, W), ones, hi, True, False), reads=[B_L[0], B_c], writes=[bk[7]])
                    P.op("pe", mm(bank(7, 128, W), ones, lo, False, True), reads=[B_L[1], B_c], writes=[bk[7]])
                if g < 4:
                    pending_fin.append(lambda g=g: finish(g))
                else:
                    finish(g)
                if g == 0 and mid_hook is not None:
                    mid_hook()
            while pending_fin:
                pending_fin.pop(0)()

        def make_finish(W, mix_dst):
            def finish(g):
                if g < 4:
                    P.op("dve", cpf(mix_dst(4 + g)[0:64, :], bank(6, 128, W)[0:64, :]), reads=[bk[6]], writes=[B_mix])
                    P.op("dve", cpf(mix_dst(4 + g)[64:128, :], bank(7, 128, W)[64:128, :]), reads=[bk[7]], writes=[B_mix])
                else:
                    h = g - 4
                    rl0, rl1, o0, o1, od, rs = [d_[:, :W] for d_ in dtmp]
                    P.op("act", actf(rl0, bank(5, 128, W), AF.Ln), reads=[bk[5]], writes=[B_dt])
                    P.op("act", actf(rl0, rl0, AF.Exp, scale=-1.0), reads=[B_dt], writes=[B_dt])
                    P.op("act", actf(rl1, bank(7, 128, W), AF.Ln), reads=[bk[7]], writes=[B_dt])
                    P.op("act", actf(rl1, rl1, AF.Exp, scale=-1.0), reads=[B_dt], writes=[B_dt])
                    P.op("dve", ttf(o0, bank(4, 128, W), rl0, ALU.mult), reads=[bk[4], B_dt], writes=[B_dt])
                    P.op("dve", ttf(o1, bank(6, 128, W), rl1, ALU.mult), reads=[bk[6], B_dt], writes=[B_dt])
                    P.op("dve", sttf(od, o1, smallc[:, 1:2], o0, ALU.mult, ALU.add), reads=[B_dt, B_small], writes=[B_dt])
                    P.op("act", actf(A_t[0][:, 0, :W], od, AF.Square), reads=[B_dt], writes=[B_A[0]])
                    P.op("pe", mm(bank(0, 128, W), ones, A_t[0][:, 0, :W]), reads=[B_A[0], B_c], writes=[bk[0]])
                    rstd_from(bank(0, 128, W), rs, 128, 128, W, [bk[0]], [B_dt], rl0)
                    P.op("dve", sttf(mix_dst(h), od, smallc[:, 2:3], rs, ALU.mult, ALU.mult), reads=[B_dt, B_small], writes=[B_mix])
            return finish

        def mlp_loads(T, subs, xq_ap, xqT_ap):
            P.dma("sp", r_t[:subs[0][1], :len(subs), :], xq_ap, writes=[B_r])
            P.dma("sp", xqT_t[:, :, :T], xqT_ap, writes=[B_xqT])

        def mlp(T, subs, xq_ap, xqT_ap, y_out, preloaded=False, after_down=None):
            nsub = len(subs)
            if not preloaded:
                mlp_loads(T, subs, xq_ap, xqT_ap)
            w_out_v = dr["w_out"].rearrange("(c p) n -> p c n", p=128)
            wo = [wbuf[i].rearrange("p (c n) -> p c n", n=512) for i in range(2)]
            for i in range(2):
                P.dma("sp", wo[i], wo_scr[i], writes=[B_wb[i]])
            P.op("dve", msf(mcol[:, :], 0.0), writes=[B_mc])
            rot = 0
            for si, (c0, tsz) in enumerate(subs):
                for nh in range(2):
                    b = rot % 2; rot += 1
                    for c in range(8):
                        P.op("pe", mm(bank(b, tsz, 512), mixT[:, c, c0:c0 + tsz], wo[nh][:, c, :], c == 0, c == 7), reads=[B_mix, B_wb[nh]], writes=[bk[b]])
                    P.op("dve", ttf(r_t[:tsz, si, nh * 512:(nh + 1) * 512], bank(b, tsz, 512), r_t[:tsz, si, nh * 512:(nh + 1) * 512], ALU.add),
                         reads=[bk[b], B_r], writes=[B_r])
                P.op("act", actf(junk[:tsz, :], r_t[:tsz, si, :], AF.Square, accum_out=mcol[:tsz, si:si + 1]), reads=[B_r, B_mc], writes=[B_mc, B_rt])
            tszm = max(s[1] for s in subs)
            rstd_from(mcol[:tszm, 0:nsub], mcol[:tszm, 8:8 + nsub], D, tszm, nsub, [B_mc], [B_mc], lntmp[:tszm, :nsub])
            P.op("dve", ttf(mcol[:tszm, 16:16 + nsub], mcol[:tszm, 8:8 + nsub], mcol[:tszm, 8:8 + nsub], ALU.mult), reads=[B_mc], writes=[B_mc])
            for nchunk in range(8):
                b = 2 + rot % 2; rot += 1
                for c in range(8):
                    P.op("pe", mm(bank(b, 128, T), wo[nchunk // 4][:, c, (nchunk % 4) * 128:(nchunk % 4 + 1) * 128], mixT[:, c, :T], c == 0, c == 7),
                         reads=[B_mix, B_wb[nchunk // 4]], writes=[bk[b]])
                k = nchunk % 2
                P.op("dve", ttf(x1tmp[k][:, :T], bank(b, 128, T), xqT_t[:, nchunk, :T], ALU.add), reads=[bk[b], B_xqT], writes=[B_x1t[k]])
                P.op("act", actf(x1g[:, nchunk, :T], x1tmp[k][:, :T], AF.Copy, scale=gcols[:, 8 + nchunk:9 + nchunk]), reads=[B_x1t[k], B_c], writes=[B_x1g])
            w_up_v = dr["w_up"].rearrange("(c p) n -> p c n", p=128)
            for pc in range(8):
                wi = pc % 2
                wu = wbuf[wi].rearrange("p (c n) -> p c n", n=512)
                P.dma("sp", wu, wu_scr[pc], writes=[B_wb[wi]])
                for mc in range(4):
                    m = pc * 4 + mc
                    b = 4 + rot % 4; rot += 1
                    for c in range(8):
                        P.op("pe", mm(bank(b, 128, T), wu[:, c, mc * 128:(mc + 1) * 128], x1g[:, c, :T], c == 0, c == 7), reads=[B_x1g, B_wb[wi]], writes=[bk[b]])
                    k = m % 2
                    P.op("act", actf(rl[k][:, :T], bank(b, 128, T), AF.Relu), reads=[bk[b]], writes=[B_rl[k]])
                    a_m = WBIG[:, m // 4, 2048 + (m % 4) * 512: 2048 + (m % 4) * 512 + T]
                    P.op("dve", ttf(a_m, rl[k][:, :T], rl[k][:, :T], ALU.mult), reads=[B_rl[k]], writes=[B_Wq])
            w_dn_v = dr["w_down"].rearrange("(m p) n -> p m n", p=128)
            for pc in range(8):
                wi = pc % 2
                wd = wbuf[wi].rearrange("p (m n) -> p m n", n=1024)
                P.dma("sp", wd, wd_scr[pc], writes=[B_wb[wi]])
                for mc in range(4):
                    m = pc * 4 + mc
                    a_m = WBIG[:, m // 4, 2048 + (m % 4) * 512: 2048 + (m % 4) * 512 + T]
                    for si, (c0, tsz) in enumerate(subs):
                        for nh in range(2):
                            b = 2 * si + nh
                            P.op("pe", mm(bank(b, tsz, 512), a_m[:, c0:c0 + tsz], wd[:, mc, nh * 512:(nh + 1) * 512], m == 0, m == 31),
                                 reads=[B_Wq, B_wb[wi]], writes=[bk[b]])
            if after_down is not None:
                after_down()
            for si, (c0, tsz) in enumerate(subs):
                for nh in range(2):
                    b = 2 * si + nh
                    P.op("dve", sttf(r_t[:tsz, si, nh * 512:(nh + 1) * 512], bank(b, tsz, 512), mcol[:tsz, 16 + si:17 + si], r_t[:tsz, si, nh * 512:(nh + 1) * 512], ALU.mult, ALU.add),
                         reads=[bk[b], B_mc, B_r], writes=[B_r])
                P.op("act", actf(junk[:tsz, :], r_t[:tsz, si, :], AF.Square, accum_out=mcol[:tsz, 4 + si:5 + si]), reads=[B_r, B_mc], writes=[B_mc, B_rt])
            rstd_from(mcol[:tszm, 4:4 + nsub], mcol[:tszm, 12:12 + nsub], D, tszm, nsub, [B_mc], [B_mc], lntmp[:tszm, :nsub])
            for si, (c0, tsz) in enumerate(subs):
                P.op("dve", sttf(r_t[:tsz, si, :], r_t[:tsz, si, :], mcol[:tsz, 12 + si:13 + si], gfin[:tsz, :], ALU.mult, ALU.mult), reads=[B_r, B_mc, B_c], writes=[B_r])
            P.dma("sp", y_out, r_t[:subs[0][1], :nsub, :], reads=[B_r])

        P.op("dve", msf(smallc[:, 9:10], 1.0), writes=[B_small])

        subs4 = [(s * 128, 128) for s in range(4)]
        for j in range(NSLOT):
            if j > 0:
                P.dma("sp", cosF_t[:, :TQ], dr["cosQ"][j], writes=[B_cs])
                P.dma("sp", sinF_t[:, :TQ], dr["sinQ"][j], writes=[B_cs])
                P.dma("sp", cosT_t[:, :4, :], dr["cosT"][j], writes=[B_cst])
                P.dma("sp", sinT_t[:, :4, :], dr["sinT"][j], writes=[B_cst])
            project(TQ, subs4, dr["xqT"][j].rearrange("(c p) t -> p c t", p=128), dr["cosQ"][j], dr["sinQ"][j], True,
                    lambda g: kTo[:, g, :], B_kTo, lambda si, tsz: Vo[:, si, :], B_Vo,
                    q_dst=lambda g: qT[:, g, :], B_qdst=B_qT,
                    zt_out=lambda si, tsz, j=j: dr["kvo"][j, si * 128:(si + 1) * 128, :],
                    cosT_ap=dr["cosT"][j], sinT_ap=dr["sinT"][j], do_loads=(j == 0))
            P.barrier()
            NB = 32 * j + 29
            nseg = (NB + 15) // 16
            segs = list(range(nseg - 1, -1, -1))
            items = [(g, s_) for g in range(8) for s_ in segs]
            loaded = {"n": 0}

            def load_item(t, NB=NB):
                g, s_ = items[t]
                sl = t % 3
                nb = min(16, NB - 16 * s_)
                P.dma("sp", Kseg[sl][:, :nb * 128], kT_scr[g, :, s_ * 2048: s_ * 2048 + nb * 128], writes=[B_seg[sl]])
                P.dma("sp", Vseg[sl][:, :nb, :], v_scr[g, :, 16 * s_:16 * s_ + nb, :], writes=[B_seg[sl]])

            def groups_blocks(g, j=j, NB=NB, nseg=nseg, segs=segs, items=items, loaded=loaded, load_item=load_item):
                def pro():
                    while loaded["n"] < min(3, len(items)):
                        load_item(loaded["n"]); loaded["n"] += 1

                vc = 512 + 128 * g if g < 4 else 128 * (g - 4)
                blocks = []
                for r in (3, 2, 1, 0):
                    blocks.append(dict(kT=kTo[:, g, r * 128:(r + 1) * 128], v=Vo[:, r, vc:vc + 128], bias=kbias[:, ZB:ZB + 1], nk=128,
                                       msb=dmask[:, r, :], mdf=dmask[:, 4 + r, :], bufs=[B_kTo, B_Vo], item=None, last=False))
                for sidx, s_ in enumerate(segs):
                    t = g * nseg + sidx
                    sl = t % 3
                    nb = min(16, NB - 16 * s_)
                    for bb in range(nb - 1, -1, -1):
                        b_ = 16 * s_ + bb
                        blocks.append(dict(kT=Kseg[sl][:, bb * 128:(bb + 1) * 128], v=Vseg[sl][:, bb, :], bias=kbias[:, j * NBLK + b_: j * NBLK + b_ + 1], nk=128,
                                           msb=None, mdf=None, bufs=[B_seg[sl]], item=t, last=(bb == 0)))

                def after(i):
                    bl = blocks[i]
                    if bl["item"] is not None and bl["last"] and loaded["n"] < len(items):
                        load_item(loaded["n"]); loaded["n"] += 1
                return pro, blocks, after

            xq_ap_j = dr["xq"][j].rearrange("(s p) n -> p s n", p=128)
            xqT_ap_j = dr["xqT"][j].rearrange("(c p) t -> p c t", p=128)
            attend(TQ, lambda g: qT[:, g, :], groups_blocks, make_finish(TQ, lambda ch: mixT[:, ch, :]),
                   mid_hook=lambda: mlp_loads(TQ, subs4, xq_ap_j, xqT_ap_j))
            P.barrier()
            def next_prefetch(j=j):
                if j + 1 < NSLOT:
                    P.dma("sp", WBIG[:, :, 2048:4096], wq_scr[:, :, :], writes=[B_Wq])
            if j + 1 < NSLOT:
                P.dma("sp", xin2[0][:, :, :TQ], dr["xqT"][j + 1].rearrange("(c p) t -> p c t", p=128), writes=[B_xin0])
            mlp(TQ, subs4, xq_ap_j, xqT_ap_j, dr["y"][j].rearrange("(s p) n -> p s n", p=128), preloaded=True, after_down=next_prefetch)
            P.barrier()

        P.dma("sp", WBIG[:, :, 2048:4096], wq_scr[:, :, :], writes=[B_Wq])
        subs_s = [(b * DEC_T, DEC_T) for b in range(BPC)]
        project(ST, subs_s, dr["xsT"].rearrange("(c p) t -> p c t", p=128), dr["cosS"][:, :], dr["sinS"][:, :], True,
                lambda g: kTo[:, g, :ST], B_kTo, lambda si, tsz: Vo[:tsz, si, :], B_Vo,
                q_dst=lambda g: qT[:, g, :ST], B_qdst=B_qT,
                zt_out=lambda si, tsz: dr["kvs"][si * DEC_T:(si + 1) * DEC_T, :],
                cosT_ap=dr["cosST"], sinT_ap=dr["sinST"])
        P.barrier()
        ckb = [SB[:, i * 8192:(i + 1) * 8192].rearrange("p (b n) -> p b n", n=1024) for i in range(2)]
        cvb = [WBIG[:, :, 2048 + i * 1024: 3072 + i * 1024] for i in range(2)]
        kTcb = [SF[:, 6144 + i * 4096: 10240 + i * 4096].bitcast(BF16).rearrange("p (g t) -> p g t", t=1024) for i in range(2)]
        B_ck = [Buf(), Buf()]; B_cv = [Buf(), Buf()]; B_kTc = [Buf(), Buf()]
        sm = SB[:, 16384:18432]
        L_f = [sm[:, i * 128:(i + 1) * 128] for i in range(2)]; B_Lf = [Buf(), Buf()]
        A_f = [sm[:, 256 + i * 128: 256 + (i + 1) * 128] for i in range(2)]; B_Af = [Buf(), Buf()]
        La_f = [sm[:, 512 + i * 128: 512 + (i + 1) * 128] for i in range(2)]; B_Laf = [Buf(), Buf()]
        P_f = [sm[:, 768 + i * 128: 768 + (i + 1) * 128] for i in range(3)]; B_Pf = [Buf(), Buf(), Buf()]
        sq_s = sm[:, 1152:1216]; B_sqs = Buf()
        u_f = [SF[:, i * 128:(i + 1) * 128] for i in range(2)]; B_uf = [Buf(), Buf()]
        mKV = SF[:, 3072:4096].bitcast(BF16)
        metaK = mKV[:, 0:1024].rearrange("p (g k) -> p g k", k=128); metaV = mKV[:, 1024:2048].rearrange("p (g k) -> p g k", k=128); B_mkv = Buf()
        rl_s = SF[:, 2048:2176]; o_s = SF[:, 2176:2304]; od_s = SF[:, 2304:2368]; rs_s = SF[:, 2368:2432]; B_fs = Buf()
        qz = SF[:, 2560:3072].bitcast(BF16).rearrange("p (m t) -> p m t", t=ST); B_qz = Buf()
        P.op("pool", msf(qz[:, :, :], 0.0), writes=[B_qz])
        qzs = qz[:, 0:8, :].rearrange("p (g two) t -> p g two t", two=2)
        qzd = qz[:, 8:16, :].rearrange("p (g two) t -> p g two t", two=2)
        P.op("pool", cpf(qzs[0:64, :, 0, :], qT[0:64, 0:4, 0:ST]), reads=[B_qT], writes=[B_qz])
        P.op("pool", cpf(qzs[64:128, :, 1, :], qT[64:128, 0:4, 0:ST]), reads=[B_qT], writes=[B_qz])
        P.op("pool", cpf(qzd[0:64, :, 0, :], qT[0:64, 4:8, 0:ST]), reads=[B_qT], writes=[B_qz])
        P.op("pool", cpf(qzd[64:128, :, 1, :], qT[64:128, 4:8, 0:ST]), reads=[B_qT], writes=[B_qz])
        for g in range(8):
            P.dma("sp", metaK[:, g, :], kT_scr[g, :, 0:128], writes=[B_mkv])
            P.dma("sp", metaV[:, g, :], v_scr[g, :, 0, :], writes=[B_mkv])

        def s_loads(bi):
            p_ = bi % 2
            for (nm, off) in (("cdk", 0), ("csk", 512)):
                P.dma("pool", ckb[p_][:, :, off:off + 512], dr[nm][bi].rearrange("(b p) n -> p b n", p=128), writes=[B_ck[p_]])
            for (nm, off) in (("cdv", 0), ("csv", 512)):
                P.dma("pool", cvb[p_][:, :, off:off + 512], dr[nm][bi].rearrange("(b p) n -> p b n", p=128), writes=[B_cv[p_], B_Wq])

        def s_transposes(bi):
            p_ = bi % 2
            rotk = 0
            for g in range(8):
                fc = 512 + 128 * g if g < 4 else 128 * (g - 4)
                for half in range(2):
                    b_ = rotk % 2; rotk += 1
                    for q4 in range(4):
                        blk = half * 4 + q4
                        P.op("pe", mm(ps[:, b_ * 512 + q4 * 128: b_ * 512 + (q4 + 1) * 128], ckb[p_][:, blk, fc:fc + 128], ident), reads=[B_ck[p_], B_c], writes=[bk[b_]])
                    P.op("dve", cpf(kTcb[p_][:, g, half * 512:(half + 1) * 512], bank(b_)), reads=[bk[b_]], writes=[B_kTc[p_]])

        W16 = DEC_T

        def s_attend(bi):
            p_ = bi % 2
            q0 = bi * W16
            vcol = lambda g: 512 + 128 * g if g < 4 else 128 * (g - 4)
            blocks = [dict(kT=lambda g: kTo[:, g, q0:q0 + W16], v=lambda g: Vo[:W16, bi, vcol(g):vcol(g) + 128], bias=kbias[:W16, ZB:ZB + 1], nk=W16,
                           mask=True, bufs=[B_kTo, B_Vo])]
            for blk in range(7, -1, -1):
                blocks.append(dict(kT=lambda g, blk=blk: kTcb[p_][:, g, blk * 128:(blk + 1) * 128], v=lambda g, blk=blk: cvb[p_][:, blk, vcol(g):vcol(g) + 128],
                                   bias=kbias[:, ZB:ZB + 1], nk=128, mask=False, bufs=[B_kTc[p_], B_cv[p_]]))
            blocks.append(dict(kT=lambda g: metaK[:, g, :], v=lambda g: metaV[:, g, :], bias=kbias[:, MB:MB + 1], nk=128, mask=False, bufs=[B_mkv]))
            n = len(blocks)
            SS = [0, 1, 2]; SD = [3, 4, 5]
            mask3 = dmask[:W16, 0, :W16].unsqueeze(1).to_broadcast([W16, 8, W16])

            def zmm(i):
                bl = blocks[i]; nk = bl["nk"]; sb_ = SS[i % 3]
                for h in range(8):
                    g = h // 2
                    P.op("pe", mm(ps[:nk, sb_ * 512 + h * W16: sb_ * 512 + (h + 1) * W16], bl["kT"](g), qz[:, h, q0:q0 + W16], h == 0, h == 7),
                         reads=bl["bufs"] + [B_qz], writes=[bk[sb_]])

            def uL(i):
                bl = blocks[i]; nk = bl["nk"]; sb_ = SS[i % 3]
                P.op("act", actf(u_f[i % 2][:nk, :], bank(sb_, nk, 128), AF.Exp, bias=bl["bias"]), reads=[bk[sb_], B_c], writes=[B_uf[i % 2]])
                P.op("act", actf(L_f[i % 2][:nk, :], u_f[i % 2][:nk, :], AF.Ln, bias=smallc[:nk, 9:10]), reads=[B_uf[i % 2], B_small], writes=[B_Lf[i % 2]])
                if bl["mask"]:
                    Lv = L_f[i % 2][:nk, :].rearrange("p (h w) -> p h w", w=W16)
                    P.op("dve", ttf(Lv, Lv, mask3, ALU.mult), reads=[B_Lf[i % 2], B_c], writes=[B_Lf[i % 2]])

            def qk(i):
                bl = blocks[i]; nk = bl["nk"]; sd_ = SD[i % 3]
                for m in range(8):
                    g = 4 + m // 2
                    P.op("pe", mm(ps[:nk, sd_ * 512 + m * W16: sd_ * 512 + (m + 1) * W16], bl["kT"](g), qz[:, 8 + m, q0:q0 + W16], m == 0, m == 7),
                         reads=bl["bufs"] + [B_qz], writes=[bk[sd_]])

            zmm(0); qk(0)
            if n > 1:
                zmm(1); qk(1)
            uL(0)
            for i in range(n):
                bl = blocks[i]; nk = bl["nk"]; sb_ = SS[i % 3]; sd_ = SD[i % 3]
                P.op("act", actf(P_f[i % 3][:nk, :], bank(sd_, nk, 128), AF.Exp, bias=bl["bias"]), reads=[bk[sd_], B_c], writes=[B_Pf[i % 3]])
                if i + 2 < n:
                    zmm(i + 2)
                P.op("pe", mm(bank(sb_, nk, 128), ntri[:nk, :nk], L_f[i % 2][:nk, :], False, i == 0, sgc=True), reads=[B_Lf[i % 2], B_c], writes=[bk[sb_]])
                if i > 0:
                    pk = blocks[i - 1]["nk"]
                    P.op("pe", mm(bank(sb_, nk, 128), nones[:pk, :nk], La_f[i % 2][:pk, :], False, True, sgc=True), reads=[B_Laf[i % 2], B_c], writes=[bk[sb_]])
                if i + 1 < n:
                    if i == 0:
                        P.op("dve", cpf(La_f[1][:nk, :], L_f[0][:nk, :]), reads=[B_Lf[0]], writes=[B_Laf[1]])
                    else:
                        pk = blocks[i - 1]["nk"]
                        if pk < nk:
                            P.op("dve", cpf(La_f[(i + 1) % 2][:nk, :], L_f[i % 2][:nk, :]), reads=[B_Lf[i % 2]], writes=[B_Laf[(i + 1) % 2]])
                            P.op("dve", ttf(La_f[(i + 1) % 2][:pk, :], La_f[(i + 1) % 2][:pk, :], La_f[i % 2][:pk, :], ALU.add), reads=[B_Laf[i % 2]], writes=[B_Laf[(i + 1) % 2]])
                        else:
                            P.op("dve", ttf(La_f[(i + 1) % 2][:nk, :], La_f[i % 2][:nk, :], L_f[i % 2][:nk, :], ALU.add), reads=[B_Lf[i % 2], B_Laf[i % 2]], writes=[B_Laf[(i + 1) % 2]])
                    uL(i + 1)
                if i + 2 < n:
                    qk(i + 2)
                for m in range(8):
                    g = 4 + m // 2
                    P.op("pe", mm(ps[:, 7 * 512 + m * W16: 7 * 512 + (m + 1) * W16], bl["v"](g), P_f[i % 3][:nk, m * W16:(m + 1) * W16], i == 0 and m == 0, False),
                         reads=bl["bufs"] + [B_Pf[i % 3]], writes=[bk[7]])
                P.op("pe", mm(ps[:, 7 * 512 + 128: 7 * 512 + 256], ones[:nk, :], P_f[i % 3][:nk, :], False, i == n - 1), reads=[B_Pf[i % 3], B_c], writes=[bk[7]])
                P.op("act", actf(A_f[i % 2][:nk, :], bank(sb_, nk, 128), AF.Exp, bias=bl["bias"]), reads=[bk[sb_], B_c], writes=[B_Af[i % 2]])
                if bl["mask"]:
                    Av = A_f[i % 2][:nk, :].rearrange("p (h w) -> p h w", w=W16)
                    P.op("dve", ttf(Av, Av, mask3, ALU.mult), reads=[B_Af[i % 2], B_c], writes=[B_Af[i % 2]])
                for h in range(8):
                    g = h // 2
                    P.op("pe", mm(ps[:, 6 * 512 + h * W16: 6 * 512 + (h + 1) * W16], bl["v"](g), A_f[i % 2][:nk, h * W16:(h + 1) * W16], i == 0 and h == 0, i == n - 1 and h == 7),
                         reads=bl["bufs"] + [B_Af[i % 2]], writes=[bk[6]])
            acc6 = ps[:, 6 * 512: 6 * 512 + 128].rearrange("p (g two w) -> p g two w", two=2, w=W16)
            P.op("dve", cpf(mixT[0:64, 4:8, q0:q0 + W16], acc6[0:64, :, 0, :]), reads=[bk[6]], writes=[B_mix])
            P.op("dve", cpf(mixT[64:128, 4:8, q0:q0 + W16], acc6[64:128, :, 1, :]), reads=[bk[6]], writes=[B_mix])
            P.op("dve", lambda e: e.reciprocal(out=rl_s, in_=ps[:, 7 * 512 + 128: 7 * 512 + 256]), reads=[bk[7]], writes=[B_fs])
            P.op("dve", ttf(o_s, ps[:, 7 * 512: 7 * 512 + 128], rl_s, ALU.mult), reads=[bk[7], B_fs], writes=[B_fs])
            ov = o_s.rearrange("p (h c w) -> p h c w", c=2, w=W16)
            odv = od_s.rearrange("p (h w) -> p h w", w=W16)
            P.op("dve", sttf(odv, ov[:, :, 1, :], smallc[:, 1:2], ov[:, :, 0, :], ALU.mult, ALU.add), reads=[B_fs, B_small], writes=[B_fs])
            P.op("act", actf(sq_s, od_s, AF.Square), reads=[B_fs], writes=[B_sqs])
            P.op("pe", mm(ps[:, 3 * 512: 3 * 512 + 64], ones, sq_s), reads=[B_sqs, B_c], writes=[bk[3]])
            rstd_from(ps[:, 3 * 512: 3 * 512 + 64], rs_s, 128, 128, 64, [bk[3]], [B_fs])
            P.op("dve", sttf(mixT[:, 0:4, q0:q0 + W16], odv, smallc[:, 2:3], rs_s.rearrange("p (h w) -> p h w", w=W16), ALU.mult, ALU.mult),
                 reads=[B_fs, B_small], writes=[B_mix])

        s_loads(0)
        s_loads(1)
        s_transposes(0)
        for bi in range(BPC):
            s_attend(bi)
            if bi + 2 < BPC:
                s_loads(bi + 2)
            if bi + 1 < BPC:
                s_transposes(bi + 1)
        P.barrier()
        mlp(ST, [(0, ST)], dr["xs"][:, :].unsqueeze(1), dr["xsT"].rearrange("(c p) t -> p c t", p=128), dr["ys"][:, :].unsqueeze(1))
        P.barrier()

        with nc.Block() as block:
            @block.tensor
            def _(e):
                P.replay("pe", e, sems, dsems)

            @block.scalar
            def _(e):
                P.replay("act", e, sems, dsems)

            @block.vector
            def _(e):
                P.replay("dve", e, sems, dsems)

            @block.gpsimd
            def _(e):
                P.replay("pool", e, sems, dsems)

            @block.sync
            def _(e):
                P.replay("sp", e, sems, dsems)
    return nc


_NC = None


def _rope_tables(pos):
    half = 32
    inv = (np.float32(10000.0) ** (-(np.arange(half, dtype=np.float32) / np.float32(half)))).astype(np.float32)
    ang = (pos.astype(np.float32)[:, None] * inv[None, :]).astype(np.float32)
    return np.cos(ang).astype(np.float32), np.sin(ang).astype(np.float32)


def _fmajor(cos, sin):
    p = np.arange(128); d = p % 64; f = d % 32
    sgn = np.where(d < 32, -1.0, 1.0).astype(np.float32)
    return np.ascontiguousarray(cos[:, f].T), np.ascontiguousarray((sin[:, f] * sgn[None, :]).T)


def kernel(x_prompt, x_sample, cache_diff_k, cache_diff_v, cache_sb_k, cache_sb_v, meta_tokens, g_mix, w_in,
           lambda_q1, lambda_k1, lambda_q2, lambda_k2, g_diff_head, w_out, g_mlp, w_up, w_down, g_final):
    global _NC
    f32 = np.float32
    bf = ml_dtypes.bfloat16
    A = lambda a: np.ascontiguousarray(np.asarray(a, dtype=f32))
    xp = A(x_prompt)[0]
    meta = A(meta_tokens)
    xT = np.ascontiguousarray(np.concatenate([meta, xp], axis=0).T)
    cosP, sinP = _rope_tables(np.arange(TP))
    cosF, sinF = _fmajor(cosP, sinP)
    gcols = np.concatenate([A(g_mix)[0].reshape(8, 128).T, A(g_mlp)[0].reshape(8, 128).T], axis=1)
    gfin = np.ascontiguousarray(np.broadcast_to(A(g_final)[None, :], (128, D)))
    ghead = A(g_diff_head)[0].reshape(128, 1)
    lamv = np.ascontiguousarray(np.broadcast_to(np.concatenate([A(lambda_q1)[0], A(lambda_k1)[0], A(lambda_q2)[0], A(lambda_k2)[0]])[None, :], (128, 256)))
    k = np.arange(128)[:, None]; q = np.arange(512)[None, :]
    dmask = np.zeros((128, 8, 512), f32)
    for r in range(4):
        dmask[:, r, :] = (128 * r + k < q)
        dmask[:, 4 + r, :] = ((128 * r + k) // 64 <= q // 64)
    dmask = dmask.astype(bf)
    consts = np.zeros((128, 512), f32)
    consts[:, 0:128] = -((np.arange(128)[:, None] >= np.arange(128)[None, :]).astype(f32))
    consts[:, 128:256] = -1.0
    consts[:, 256:384] = 1.0
    consts[:, 384:512] = np.eye(128)
    consts = consts.astype(bf)
    cosS_, sinS_ = _rope_tables(NMETA + PAST + np.arange(DEC_T))
    cosSF, sinSF = _fmajor(np.tile(cosS_, (BPC, 1)), np.tile(sinS_, (BPC, 1)))
    xs_all = A(x_sample)
    cdk = A(cache_diff_k)[0].reshape(DEC_B, PAST, 512); csk = A(cache_sb_k)[0].reshape(DEC_B, PAST, 512)
    cdv = A(cache_diff_v)[0].reshape(DEC_B, PAST, 512); csv = A(cache_sb_v)[0].reshape(DEC_B, PAST, 512)
    shared = dict(xT=xT, w_in=A(w_in)[0], w_out=A(w_out)[0], w_up=A(w_up)[0], w_down=A(w_down)[0], gcols=np.ascontiguousarray(gcols),
                  gfin=gfin, ghead=np.ascontiguousarray(ghead), lamv=lamv, cosF=cosF, sinF=sinF, dmask=dmask, consts=consts,
                  cosS=cosSF, sinS=sinSF, cosST=np.ascontiguousarray(np.broadcast_to(cosS_[:, None, :], (DEC_T, BPC, 32))), sinST=np.ascontiguousarray(np.broadcast_to(sinS_[:, None, :], (DEC_T, BPC, 32))),
                  cosMT=np.ascontiguousarray(cosP[0:NMETA, None, :]), sinMT=np.ascontiguousarray(sinP[0:NMETA, None, :]),
                  cosTp=np.ascontiguousarray(cosP[NMETA:NMETA + 31 * TQ].reshape(31, 4, 128, 32).transpose(0, 2, 1, 3)),
                  sinTp=np.ascontiguousarray(sinP[NMETA:NMETA + 31 * TQ].reshape(31, 4, 128, 32).transpose(0, 2, 1, 3)))
    in_maps = []
    for c in range(NCORES):
        m = dict(shared)
        tiles = [8 * j + c for j in range(NSLOT)]
        m["xqT"] = np.ascontiguousarray(np.stack([xp[TQ * g:TQ * (g + 1)].T for g in tiles]))
        m["xq"] = np.ascontiguousarray(np.stack([xp[TQ * g:TQ * (g + 1)] for g in tiles]))
        m["cosQ"] = np.ascontiguousarray(np.stack([cosF[:, NMETA + TQ * g: NMETA + TQ * (g + 1)] for g in tiles]))
        m["sinQ"] = np.ascontiguousarray(np.stack([sinF[:, NMETA + TQ * g: NMETA + TQ * (g + 1)] for g in tiles]))
        m["cosT"] = np.ascontiguousarray(np.stack([cosP[NMETA + TQ * g: NMETA + TQ * (g + 1)].reshape(4, 128, 32).transpose(1, 0, 2) for g in tiles]))
        m["sinT"] = np.ascontiguousarray(np.stack([sinP[NMETA + TQ * g: NMETA + TQ * (g + 1)].reshape(4, 128, 32).transpose(1, 0, 2) for g in tiles]))
        if c == 0:
            pass
        kb = np.zeros((128, NSLOT * NBLK + 2), f32)
        for j in range(NSLOT):
            for b in range(NBLK):
                if b == 0:
                    kb[16:, j * NBLK] = NEG
                elif (b - 1) >= 4 * (8 * j + c):
                    kb[:, j * NBLK + b] = NEG
        kb[16:, NSLOT * NBLK + 1] = NEG
        m["kbias"] = kb
        bs = slice(BPC * c, BPC * (c + 1))
        m["xsT"] = np.ascontiguousarray(xs_all[bs].reshape(ST, D).T)
        m["xs"] = np.ascontiguousarray(xs_all[bs].reshape(ST, D))
        m["cdk"] = np.ascontiguousarray(cdk[bs]); m["csk"] = np.ascontiguousarray(csk[bs])
        m["cdv"] = np.ascontiguousarray(cdv[bs]); m["csv"] = np.ascontiguousarray(csv[bs])
        in_maps.append(m)
    if _NC is None:
        _NC = build_program()
    res = run_bass_kernel_spmd(_NC, in_maps, core_ids=list(range(NCORES)))
    y_prompt = np.zeros((1, SEQ, D), f32)
    kv_p = np.zeros((TP, 2048), f32)
    y_sample = np.zeros((DEC_B, DEC_T, D), f32)
    kv_s = np.zeros((DEC_B, DEC_T, 2048), f32)
    for c in range(NCORES):
        r = res.results[c]
        for j in range(NSLOT):
            g = 8 * j + c
            y_prompt[0, TQ * g:TQ * (g + 1)] = r["y"][j]
            kv_p[NMETA + TQ * g: NMETA + TQ * (g + 1)] = r["kvo"][j]
        if c == 0:
            kv_p[0:NMETA] = r["metao"]
        y_sample[BPC * c:BPC * (c + 1)] = np.asarray(r["ys"]).reshape(BPC, DEC_T, D)
        kv_s[BPC * c:BPC * (c + 1)] = np.asarray(r["kvs"]).reshape(BPC, DEC_T, 2048)
    outs = (y_prompt, y_sample,
            kv_p[:, 0:512].reshape(1, 1, TP, 4, 2, 64).copy(), kv_p[:, 1024:1536].reshape(1, 1, TP, 4, 128).copy(),
            kv_p[:, 512:1024].reshape(1, 1, TP, 8, 64).copy(), kv_p[:, 1536:2048].reshape(1, 1, TP, 8, 64).copy(),
            kv_s[:, :, 0:512].reshape(1, DEC_B, DEC_T, 4, 2, 64).copy(), kv_s[:, :, 1024:1536].reshape(1, DEC_B, DEC_T, 4, 128).copy(),
            kv_s[:, :, 512:1024].reshape(1, DEC_B, DEC_T, 8, 64).copy(), kv_s[:, :, 1536:2048].reshape(1, DEC_B, DEC_T, 8, 64).copy())
    return outs
```

```python
import contextlib
import numpy as np
import ml_dtypes
import concourse.bass as bass
import concourse.mybir as mybir
from concourse.bass_utils import run_bass_kernel_spmd

F32 = mybir.dt.float32
BF16 = mybir.dt.bfloat16
AF = mybir.ActivationFunctionType
ALU = mybir.AluOpType

NCORES = 8
D = 1024
SEQ = 16384
NMETA = 16
TP = NMETA + SEQ
TQ = 512
NSLOT = 4
NBLK = 129
NEG = -30000.0
EPS = 1e-6
DEC_B = 32
DEC_T = 16
PAST = 1024
BPC = DEC_B // NCORES
ST = BPC * DEC_T
NDMA = 56
NDMA_HW = 44

C_DK, C_SK, C_DV, C_SV, C_DQ, C_SQ, C_DQS, C_DKS = 0, 512, 1024, 1536, 2048, 2560, 3072, 3584


class Buf:
    __slots__ = ("w", "r")

    def __init__(self):
        self.w = {}
        self.r = {}


class Prog:
    CE = ("pe", "act", "dve", "pool")

    def __init__(self):
        self.q = {e: [] for e in self.CE + ("sp",)}
        self.cnt = {e: 0 for e in self.CE}
        self.waited = {e: {} for e in self.CE + ("sp",)}
        self.dma_vals = [0] * NDMA
        self.rr = 0
        self.rr_sw = 0

    def _wait(self, eng, key, val):
        if self.waited[eng].get(key, 0) >= val:
            return
        self.waited[eng][key] = val
        self.q[eng].append(("w", key, val))

    def _deps(self, eng, reads, writes):
        for b in reads:
            for k, v in b.w.items():
                if not (eng == "pe" and k == "pe"):
                    self._wait(eng, k, v)
        for b in writes:
            for dct in (b.w, b.r):
                for k, v in dct.items():
                    if not (eng == "pe" and k == "pe"):
                        self._wait(eng, k, v)

    def _mark(self, key, val, reads, writes):
        for b in reads:
            if b.r.get(key, 0) < val:
                b.r[key] = val
        for b in writes:
            b.w[key] = val
            b.r = {}

    def op(self, eng, fn, reads=(), writes=()):
        self._deps(eng, reads, writes)
        self.cnt[eng] += 1
        self.q[eng].append(("o", fn))
        self._mark(eng, self.cnt[eng], reads, writes)

    def dma(self, eng, out, in_, reads=(), writes=()):
        if eng == "pool":
            i = NDMA_HW + self.rr_sw
            self.rr_sw = (self.rr_sw + 1) % (NDMA - NDMA_HW)
        else:
            i = self.rr
            self.rr = (i + 1) % NDMA_HW
        key = ("d", i)
        if self.dma_vals[i] > 0:
            self._wait(eng, key, self.dma_vals[i])
        self._deps(eng, reads, writes)
        self.dma_vals[i] += 16
        self.q[eng].append(("d", out, in_, i))
        self._mark(key, self.dma_vals[i], reads, writes)

    def barrier(self):
        for e in self.CE + ("sp",):
            for o in self.CE:
                if o != e and self.cnt[o] > 0:
                    self._wait(e, o, self.cnt[o])
            for i in range(NDMA):
                if self.dma_vals[i] > 0:
                    self._wait(e, ("d", i), self.dma_vals[i])

    def replay(self, eng, e, sems, dsems):
        for it in self.q[eng]:
            if it[0] == "w":
                k = it[1]
                s = dsems[k[1]] if isinstance(k, tuple) else sems[k]
                e.wait_ge(s, it[2])
            elif it[0] == "o":
                it[1](e).then_inc(sems[eng], 1)
            else:
                e.dma_start(out=it[1], in_=it[2]).then_inc(dsems[it[3]], 16)


def mm(out, lhsT, rhs, start=True, stop=True, sgc=False):
    return lambda e: e.matmul(out, lhsT=lhsT, rhs=rhs, start=start, stop=stop, skip_group_check=sgc)


def actf(out, in_, func, **kw):
    return lambda e: e.activation(out=out, in_=in_, func=func, **kw)


def ttf(out, a, b, op):
    return lambda e: e.tensor_tensor(out=out, in0=a, in1=b, op=op)


def tsf(out, a, s1, op0):
    return lambda e: e.tensor_scalar(out=out, in0=a, scalar1=s1, scalar2=None, op0=op0)


def sttf(out, in0, scalar, in1, op0, op1):
    return lambda e: e.scalar_tensor_tensor(out=out, in0=in0, scalar=scalar, in1=in1, op0=op0, op1=op1)


def cpf(out, in_):
    return lambda e: e.tensor_copy(out=out, in_=in_)


def msf(ap, v):
    return lambda e: e.memset(ap, v)


def build_program():
    nc = bass.Bass("TRN2", target_bir_lowering=False)
    P = Prog()
    dr = {}

    def din(n, shape, dt=F32):
        dr[n] = nc.dram_tensor(n, list(shape), dt, kind="ExternalInput").ap()

    def dout(n, shape, dt=F32):
        dr[n] = nc.dram_tensor(n, list(shape), dt, kind="ExternalOutput").ap()

    din("xT", [D, TP]); din("xqT", [NSLOT, D, TQ]); din("xq", [NSLOT, TQ, D])
    din("w_in", [D, 3 * D]); din("w_out", [D, D]); din("w_up", [D, 4 * D]); din("w_down", [4 * D, D])
    din("gcols", [128, 16]); din("gfin", [128, D]); din("ghead", [128, 1]); din("lamv", [128, 256])
    din("cosF", [128, TP]); din("sinF", [128, TP])
    din("cosQ", [NSLOT, 128, TQ]); din("sinQ", [NSLOT, 128, TQ])
    din("cosT", [NSLOT, 128, 4, 32]); din("sinT", [NSLOT, 128, 4, 32])
    din("dmask", [128, 8, 512], BF16); din("kbias", [128, NSLOT * NBLK + 2]); din("consts", [128, 512], BF16)
    din("xsT", [D, ST]); din("xs", [ST, D])
    din("cdk", [BPC, PAST, 512]); din("csk", [BPC, PAST, 512]); din("cdv", [BPC, PAST, 512]); din("csv", [BPC, PAST, 512])
    din("cosS", [128, ST]); din("sinS", [128, ST]); din("cosST", [16, BPC, 32]); din("sinST", [16, BPC, 32])
    din("cosMT", [16, 1, 32]); din("sinMT", [16, 1, 32])
    din("cosTp", [31, 128, 4, 32]); din("sinTp", [31, 128, 4, 32])
    dout("y", [NSLOT, TQ, D]); dout("kvo", [NSLOT, TQ, 2048]); dout("metao", [NMETA, 2048])
    dout("ys", [ST, D]); dout("kvs", [ST, 2048])
    kT_scr = nc.dram_tensor("kT_scr", [8, 128, NBLK * 128], BF16, kind="Internal").ap()
    v_scr = nc.dram_tensor("v_scr", [8, 128, NBLK, 128], BF16, kind="Internal").ap()
    wo_scr = nc.dram_tensor("wo_scr", [2, 128, 8, 512], BF16, kind="Internal").ap()
    wu_scr = nc.dram_tensor("wu_scr", [8, 128, 8, 512], BF16, kind="Internal").ap()
    wd_scr = nc.dram_tensor("wd_scr", [8, 128, 4, 1024], BF16, kind="Internal").ap()
    wq_scr = nc.dram_tensor("wq_scr", [128, 8, 2048], BF16, kind="Internal").ap()

    es = contextlib.ExitStack()
    with es:
        def sb(name, shape, dt):
            return es.enter_context(nc.sbuf_tensor(name, list(shape), dt))

        WBIG = sb("WBIG", [128, 8, 4096], BF16)
        consts = sb("consts_sb", [128, 512], BF16)
        dmask = sb("dmask_sb", [128, 8, 512], BF16)
        kbias = sb("kbias_sb", [128, NSLOT * NBLK + 2], F32)
        gfin = sb("gfin_sb", [128, D], F32)
        gcols = sb("gcols_sb", [128, 16], F32)
        smallc = sb("smallc", [128, 64], F32)
        lamv = sb("lamv_sb", [128, 256], F32)
        qT = sb("qT", [128, 8, 512], BF16)
        kTo = sb("kTo", [128, 8, 512], BF16)
        Vo = sb("Vo", [128, 4, 1024], BF16)
        mixT = sb("mixT", [128, 8, 512], BF16)
        SF = sb("SF", [128, 14336], F32)
        SB = sb("SB", [128, 18432], BF16)
        ps = es.enter_context(nc.psum_tensor("ps", [128, 4096], F32))
        sems = {e: es.enter_context(nc.semaphore("s_" + e)) for e in Prog.CE}
        dsems = [es.enter_context(nc.semaphore("d%d" % i)) for i in range(NDMA)]

        ntri = consts[:, 0:128]; nones = consts[:, 128:256]; ones = consts[:, 256:384]; ident = consts[:, 384:512]
        ZB = NSLOT * NBLK
        MB = NSLOT * NBLK + 1

        bk = [Buf() for _ in range(8)]

        def bank(b, rows=128, w=512):
            return ps[:rows, b * 512: b * 512 + w]

        B_W = Buf(); B_Wq = Buf(); B_c = Buf(); B_qT = Buf(); B_kTo = Buf(); B_Vo = Buf(); B_mix = Buf(); B_small = Buf()

        xin2 = [SF[:, i * 4096:(i + 1) * 4096].rearrange("p (c t) -> p c t", t=512) for i in range(2)]
        zt = [SF[:, 4096 + i * 2048: 4096 + (i + 1) * 2048] for i in range(2)]; B_zt = [Buf(), Buf()]
        B_xin0 = Buf()
        cosF_t = SF[:, 8192:8704]; sinF_t = SF[:, 8704:9216]; B_cs = Buf()
        cosr = SF[:, 9216:9728]; sinr = SF[:, 9728:10240]; B_csr = Buf()
        rstd_b = SF[:, 10240:10752]; B_rb = Buf()
        t1 = [SF[:, 10752 + i * 512: 10752 + (i + 1) * 512] for i in range(2)]
        t2 = [SF[:, 11776 + i * 512: 11776 + (i + 1) * 512] for i in range(2)]
        B_t = [Buf(), Buf()]
        cosT_t = SF[:, 12800:12928].rearrange("p (s i) -> p s i", i=32)
        sinT_t = SF[:, 12928:13056].rearrange("p (s i) -> p s i", i=32); B_cst = Buf()
        rtmp = [SF[:, 13056 + i * 256: 13056 + (i + 1) * 256] for i in range(4)]; B_rt = Buf()
        lntmp = SF[:, 14080:14336 - 128]
        rcol = SF[:, 14336 - 128: 14336 - 120]; B_rc = Buf()
        sq2 = [SB[:, i * 8192: i * 8192 + 4096].rearrange("p (c t) -> p c t", t=512) for i in range(2)]; B_sq2 = [Buf(), Buf()]
        xg2 = [SB[:, i * 8192 + 4096: i * 8192 + 8192].rearrange("p (c t) -> p c t", t=512) for i in range(2)]; B_xg2 = [Buf(), Buf()]
        u_t = [SF[:, i * 1024:(i + 1) * 1024].rearrange("p (h w) -> p h w", h=2) for i in range(2)]; B_u = [Buf(), Buf()]
        dtmp = [SF[:, 2048 + i * 512: 2048 + (i + 1) * 512] for i in range(6)]; B_dt = Buf()
        Pacc = [SF[:, 5120 + i * 512: 5120 + (i + 1) * 512] for i in range(2)]; B_pa = [Buf(), Buf()]
        Kseg = [SB[:, i * 2048:(i + 1) * 2048] for i in range(3)]
        Vseg = [SB[:, 6144 + i * 2048: 6144 + (i + 1) * 2048].rearrange("p (b c) -> p b c", c=128) for i in range(3)]
        B_seg = [Buf(), Buf(), Buf()]
        L_t = [SB[:, 12288 + i * 1024: 12288 + (i + 1) * 1024].rearrange("p (h w) -> p h w", h=2) for i in range(2)]
        A_t = [SB[:, 14336 + i * 1024: 14336 + (i + 1) * 1024].rearrange("p (h w) -> p h w", h=2) for i in range(2)]
        Lacc = [SB[:, 16384 + i * 1024: 16384 + (i + 1) * 1024].rearrange("p (h w) -> p h w", h=2) for i in range(2)]
        B_L = [Buf(), Buf()]; B_A = [Buf(), Buf()]; B_Lacc = [Buf(), Buf()]
        r_t = SF[:, 6144:10240].rearrange("p (s n) -> p s n", n=1024); B_r = Buf()
        xqT_t = SF[:, 10240:14336].rearrange("p (c t) -> p c t", t=512); B_xqT = Buf()
        rl = [SF[:, 4096 + i * 512: 4096 + (i + 1) * 512] for i in range(2)]; B_rl = [Buf(), Buf()]
        x1tmp = [SF[:, 5120 + i * 512: 5120 + (i + 1) * 512] for i in range(2)]; B_x1t = [Buf(), Buf()]
        mcol = smallc[:, 16:48]; B_mc = Buf()
        junk = SF[:, 4096:5120]
        wbuf = [SB[:, i * 4096:(i + 1) * 4096] for i in range(2)]; B_wb = [Buf(), Buf()]
        x1g = qT; B_x1g = B_qT

        def rstd_from(ss_ap, out_ap, n_feat, rows, width, reads, writes, tmp_ap=None):
            P.op("act", actf(out_ap, ss_ap, AF.Ln, scale=1.0 / n_feat, bias=smallc[:rows, 8:9]), reads=reads + [B_small], writes=writes)
            P.op("act", actf(out_ap, out_ap, AF.Exp, scale=-0.5), reads=writes, writes=writes)

        P.dma("sp", consts[:], dr["consts"][:], writes=[B_c])
        P.dma("sp", dmask[:], dr["dmask"][:], writes=[B_c])
        P.dma("sp", kbias[:], dr["kbias"][:], writes=[B_c])
        P.dma("sp", gfin[:], dr["gfin"][:], writes=[B_c])
        P.dma("sp", gcols[:], dr["gcols"][:], writes=[B_c])
        P.dma("sp", lamv[:], dr["lamv"][:], writes=[B_c])
        P.dma("sp", smallc[:, 0:1], dr["ghead"][:], writes=[B_small])
        w_in_v = dr["w_in"].rearrange("(c p) n -> p c n", p=128)
        for c0 in range(0, 8, 2):
            P.dma("pool", WBIG[:, c0:c0 + 2, 0:2048], w_in_v[:, c0:c0 + 2, 1024:3072], writes=[B_W])

        def load_q_weights():
            for c0 in range(0, 8, 2):
                P.dma("pool", WBIG[:, c0:c0 + 2, 2048:3072], w_in_v[:, c0:c0 + 2, 0:1024], writes=[B_Wq])
            for c in range(8):
                for (s0, d0) in ((C_DQ, C_DQS), (C_DK, C_DKS)):
                    src = WBIG[:, c, s0:s0 + 512].rearrange("p (m h i) -> p m h i", h=2, i=32)
                    dst = WBIG[:, c, d0:d0 + 512].rearrange("p (m h i) -> p m h i", h=2, i=32)
                    P.op("pool", cpf(dst[:, :, 0, :], src[:, :, 1, :]), reads=[B_W, B_Wq], writes=[B_Wq])
                    P.op("pool", cpf(dst[:, :, 1, :], src[:, :, 0, :]), reads=[B_W, B_Wq], writes=[B_Wq])

        w_out_v0 = dr["w_out"].rearrange("(c p) n -> p c n", p=128)
        w_up_v0 = dr["w_up"].rearrange("(c p) n -> p c n", p=128)
        w_dn_v0 = dr["w_down"].rearrange("(m p) n -> p m n", p=128)
        conv_jobs = [(wo_scr[i], w_out_v0[:, :, i * 512:(i + 1) * 512]) for i in range(2)]
        for pc in range(8):
            conv_jobs.append((wu_scr[pc], w_up_v0[:, :, pc * 512:(pc + 1) * 512]))
            conv_jobs.append((wd_scr[pc], w_dn_v0[:, pc * 4:(pc + 1) * 4, :]))
        P.op("dve", msf(smallc[:, 8:9], EPS), writes=[B_small])
        P.op("dve", msf(smallc[:, 3:5], 0.0), writes=[B_small])
        P.op("dve", ttf(lamv[:, 0:64], lamv[:, 0:64], lamv[:, 64:128], ALU.mult), reads=[B_c], writes=[B_c])
        P.op("dve", ttf(lamv[:, 128:192], lamv[:, 128:192], lamv[:, 192:256], ALU.mult), reads=[B_c], writes=[B_c])
        P.op("act", actf(lamv[:, 64:128], lamv[:, 0:64], AF.Copy, accum_out=smallc[:, 3:4]), reads=[B_c, B_small], writes=[B_c, B_small])
        P.op("act", actf(lamv[:, 192:256], lamv[:, 128:192], AF.Copy, accum_out=smallc[:, 4:5]), reads=[B_c, B_small], writes=[B_c, B_small])
        P.op("act", actf(smallc[:, 5:7], smallc[:, 3:5], AF.Exp), reads=[B_small], writes=[B_small])
        P.op("dve", sttf(smallc[:, 1:2], smallc[:, 6:7], -0.2, smallc[:, 5:6], ALU.add, ALU.subtract), reads=[B_small], writes=[B_small])
        P.op("dve", tsf(smallc[:, 2:3], smallc[:, 0:1], 0.8, ALU.mult), reads=[B_small], writes=[B_small])

        fm_rot = [0]; tm_rot = [0]; t_rot = [0]; zt_rot = [0]

        cs2 = [(cosF_t, sinF_t), (SF[:, 13056:13568], SF[:, 13568:14080])]

        def project_loads(T, subs, x_ap, cos_ap, sin_ap, own, cosT_ap=None, sinT_ap=None, par=0):
            nsub = len(subs)
            xin = xin2[par]
            B_xinl = [B_xin0] if par == 0 else [B_zt[0], B_zt[1]]
            B_csl = [B_cs] if par == 0 else [B_rt]
            P.dma("sp", xin[:, :, :T], x_ap, writes=B_xinl)
            P.dma("sp", cs2[par][0][:, :T], cos_ap, writes=B_csl)
            P.dma("sp", cs2[par][1][:, :T], sin_ap, writes=B_csl)
            if own:
                P.dma("sp", cosT_t[:subs[0][1], :nsub, :], cosT_ap, writes=[B_cst])
                P.dma("sp", sinT_t[:subs[0][1], :nsub, :], sinT_ap, writes=[B_cst])

        def project(T, subs, x_ap, cos_ap, sin_ap, own, kT_dst, B_kdst, v_dst, B_vdst, q_dst=None, B_qdst=None,
                    zt_out=None, cosT_ap=None, sinT_ap=None, g0=0, par=0, do_loads=True):
            nsub = len(subs)
            xin = xin2[par]; sq = sq2[par]; xg = xg2[par]; B_sq = B_sq2[par]; B_xg = B_xg2[par]
            B_xinl = [B_xin0] if par == 0 else [B_zt[0], B_zt[1]]
            B_csl = [B_cs] if par == 0 else [B_rt]
            cosF_p, sinF_p = cs2[par]
            if do_loads:
                project_loads(T, subs, x_ap, cos_ap, sin_ap, own, cosT_ap, sinT_ap, par)
            P.op("act", actf(sq[:, :, :T], xin[:, :, :T], AF.Square), reads=B_xinl, writes=[B_sq])
            for c in range(8):
                P.op("dve", tsf(xg[:, c, :T], xin[:, c, :T], gcols[:, g0 + c:g0 + c + 1], ALU.mult), reads=B_xinl + [B_c], writes=[B_xg])
            for c in range(8):
                P.op("pe", mm(bank(6, 128, T), ones, sq[:, c, :T], c == 0, c == 7), reads=[B_sq, B_c], writes=[bk[6]])
            for si, (c0, tsz) in enumerate(subs):
                for c in range(8):
                    P.op("pe", mm(ps[:tsz, 7 * 512 + si: 7 * 512 + si + 1], sq[:, c, c0:c0 + tsz], ones[:, 0:1], c == 0, c == 7),
                         reads=[B_sq, B_c], writes=[bk[7]])
            rstd_from(bank(6, 128, T), rstd_b[:, :T], D, 128, T, [bk[6]], [B_rb], lntmp[:, :T] if T <= 128 else junk[:, :T])
            tszm = max(s[1] for s in subs)
            rstd_from(ps[:tszm, 7 * 512: 7 * 512 + nsub], rcol[:tszm, :nsub], D, tszm, nsub, [bk[7]], [B_rc], lntmp[:tszm, :nsub])
            P.op("dve", ttf(cosr[:, :T], cosF_p[:, :T], rstd_b[:, :T], ALU.mult), reads=B_csl + [B_rb], writes=[B_csr])
            P.op("dve", ttf(sinr[:, :T], sinF_p[:, :T], rstd_b[:, :T], ALU.mult), reads=B_csl + [B_rb], writes=[B_csr])

            def fm_plain(wcol, dst, B_dst, sc):
                b = fm_rot[0] % 4; fm_rot[0] += 1
                for c in range(8):
                    P.op("pe", mm(bank(b, 128, T), WBIG[:, c, wcol:wcol + 128], xg[:, c, :T], c == 0, c == 7), reads=[B_W, B_Wq, B_xg], writes=[bk[b]])
                P.op("dve", sttf(dst, bank(b, 128, T), sc, rstd_b[:, :T], ALU.mult, ALU.mult), reads=[bk[b], B_rb], writes=[B_dst])

            def fm_rope(wcol, scol, dst, B_dst, sc):
                b1 = fm_rot[0] % 4; fm_rot[0] += 1
                b2 = fm_rot[0] % 4; fm_rot[0] += 1
                for c in range(8):
                    P.op("pe", mm(bank(b1, 128, T), WBIG[:, c, wcol:wcol + 128], xg[:, c, :T], c == 0, c == 7), reads=[B_W, B_Wq, B_xg], writes=[bk[b1]])
                for c in range(8):
                    P.op("pe", mm(bank(b2, 128, T), WBIG[:, c, scol:scol + 128], xg[:, c, :T], c == 0, c == 7), reads=[B_W, B_Wq, B_xg], writes=[bk[b2]])
                k = t_rot[0] % 2; t_rot[0] += 1
                P.op("dve", sttf(t1[k][:, :T], bank(b1, 128, T), sc, cosr[:, :T], ALU.mult, ALU.mult), reads=[bk[b1], B_csr], writes=[B_t[k]])
                P.op("dve", sttf(t2[k][:, :T], bank(b2, 128, T), sc, sinr[:, :T], ALU.mult, ALU.mult), reads=[bk[b2], B_csr], writes=[B_t[k]])
                P.op("pool", ttf(dst, t1[k][:, :T], t2[k][:, :T], ALU.add), reads=[B_t[k]], writes=[B_dst])

            for g in range(4):
                fm_plain(C_SK + 128 * g, kT_dst(g), B_kdst, 1.0)
            for h in range(4):
                fm_rope(C_DK + 128 * h, C_DKS + 128 * h, kT_dst(4 + h), B_kdst, 1.0)
            if q_dst is not None:
                for g in range(4):
                    fm_plain(C_SQ + 128 * g, q_dst(g), B_qdst, 0.125)
                for h in range(4):
                    fm_rope(C_DQ + 128 * h, C_DQS + 128 * h, q_dst(4 + h), B_qdst, 0.125)
            for si, (c0, tsz) in enumerate(subs):
                if own:
                    zi = zt_rot[0] % 2; zt_rot[0] += 1
                    z = zt[zi]; Bz = B_zt[zi]
                for nb in (range(4) if own else (2, 3)):
                    b = 4 + tm_rot[0] % 2; tm_rot[0] += 1
                    for c in range(8):
                        P.op("pe", mm(bank(b, tsz, 512), xg[:, c, c0:c0 + tsz], WBIG[:, c, nb * 512:(nb + 1) * 512], c == 0, c == 7),
                             reads=[B_W, B_Wq, B_xg], writes=[bk[b]])
                    if own:
                        P.op("act", actf(z[:tsz, nb * 512:(nb + 1) * 512], bank(b, tsz, 512), AF.Copy, scale=rcol[:tsz, si:si + 1]),
                             reads=[bk[b], B_rc], writes=[Bz])
                    else:
                        P.op("act", actf(v_dst(si, tsz)[:, (nb - 2) * 512:(nb - 1) * 512], bank(b, tsz, 512), AF.Copy, scale=rcol[:tsz, si:si + 1]),
                             reads=[bk[b], B_rc], writes=[B_vdst])
                if own:
                    zv = z[:tsz, 0:512].rearrange("p (m h i) -> p m h i", h=2, i=32)
                    x1 = zv[:, :, 0, :]; x2 = zv[:, :, 1, :]
                    cb = cosT_t[:tsz, si, :].unsqueeze(1).to_broadcast([tsz, 8, 32])
                    sbb = sinT_t[:tsz, si, :].unsqueeze(1).to_broadcast([tsz, 8, 32])
                    ta, tb, tc, td = [rtmp[i][:tsz, :].rearrange("p (m i) -> p m i", i=32) for i in range(4)]
                    P.op("dve", ttf(ta, x1, cb, ALU.mult), reads=[Bz, B_cst], writes=[B_rt])
                    P.op("dve", ttf(tb, x2, sbb, ALU.mult), reads=[Bz, B_cst], writes=[B_rt])
                    P.op("dve", ttf(tc, x2, cb, ALU.mult), reads=[Bz, B_cst], writes=[B_rt])
                    P.op("dve", ttf(td, x1, sbb, ALU.mult), reads=[Bz, B_cst], writes=[B_rt])
                    P.op("dve", ttf(x1, ta, tb, ALU.subtract), reads=[B_rt], writes=[Bz])
                    P.op("dve", ttf(x2, tc, td, ALU.add), reads=[B_rt], writes=[Bz])
                    P.op("pool", cpf(v_dst(si, tsz), z[:tsz, 1024:2048]), reads=[Bz], writes=[B_vdst])
                    P.dma("sp", zt_out(si, tsz), z[:tsz, :], reads=[Bz])

        xT_v = dr["xT"].rearrange("(c p) t -> p c t", p=128)
        kst = [kTo, qT]; B_kst = [B_kTo, B_qT]
        vst = [Vo, mixT[:, :, :].rearrange("p (s a) t -> p s (a t)", a=2)]; B_vst = [B_Vo, B_mix]

        def store_scratch(slot, blk0, nblk):
            for g in range(8):
                P.dma("sp", kT_scr[g, :, blk0 * 128:(blk0 + nblk) * 128], kst[slot][:, g, :nblk * 128], reads=[B_kst[slot]])
                vc = 512 + 128 * g if g < 4 else 128 * (g - 4)
                P.dma("sp", v_scr[g, :, blk0:blk0 + nblk, :], vst[slot][:, :nblk, vc:vc + 128], reads=[B_vst[slot]])

        kb = SF[:, 8192:10240].bitcast(BF16).rearrange("p (s n) -> p s n", n=1024)
        B_kb = [B_cs, B_csr]
        cst2 = [(cosT_t, sinT_t, [B_cst]),
                (SF[:, 10240:10368].rearrange("p (s i) -> p s i", i=32), SF[:, 10368:10496].rearrange("p (s i) -> p s i", i=32), [B_rb])]
        p1rot = {"tm": 0, "tr": 0, "zk": 0}

        def p1_loads(i):
            par = (i + 1) % 2
            p0 = NMETA + TQ * i
            B_xinl = [B_xin0] if par == 0 else [B_zt[0], B_zt[1]]
            P.dma("sp", xin2[par][:, :, :], xT_v[:, :, p0:p0 + TQ], writes=B_xinl)
            P.dma("sp", cst2[par][0][:, :, :], dr["cosTp"][i], writes=cst2[par][2])
            P.dma("sp", cst2[par][1][:, :, :], dr["sinTp"][i], writes=cst2[par][2])

        def p1_prep(i):
            par = (i + 1) % 2
            xin = xin2[par]; sq = sq2[par]; xg = xg2[par]; B_sq = B_sq2[par]; B_xg = B_xg2[par]
            B_xinl = [B_xin0] if par == 0 else [B_zt[0], B_zt[1]]
            P.op("act", actf(sq[:, :, :], xin[:, :, :], AF.Square), reads=B_xinl, writes=[B_sq])
            for c in range(8):
                P.op("dve", tsf(xg[:, c, :], xin[:, c, :], gcols[:, c:c + 1], ALU.mult), reads=B_xinl + [B_c], writes=[B_xg])

        def p1_tile(i):
            par = (i + 1) % 2; sl = (i + 1) % 2
            xin = xin2[par]; sq = sq2[par]; xg = xg2[par]; B_sq = B_sq2[par]; B_xg = B_xg2[par]
            cT, sT, B_ct = cst2[par]
            for si in range(4):
                for c in range(8):
                    P.op("pe", mm(ps[:, 7 * 512 + si: 7 * 512 + si + 1], sq[:, c, si * 128:(si + 1) * 128], ones[:, 0:1], c == 0, c == 7),
                         reads=[B_sq, B_c], writes=[bk[7]])
            rstd_from(ps[:, 7 * 512: 7 * 512 + 4], rcol[:, :4], D, 128, 4, [bk[7]], [B_rc])
            for si in range(4):
                for nb in range(4):
                    b_ = p1rot["tm"] % 4; p1rot["tm"] += 1
                    for c in range(8):
                        P.op("pe", mm(bank(b_), xg[:, c, si * 128:(si + 1) * 128], WBIG[:, c, nb * 512:(nb + 1) * 512], c == 0, c == 7),
                             reads=[B_W, B_xg], writes=[bk[b_]])
                    if nb == 0:
                        k_ = p1rot["zk"] % 2; p1rot["zk"] += 1
                        zk = t1[k_]
                        P.op("act", actf(zk, bank(b_), AF.Copy, scale=rcol[:, si:si + 1]), reads=[bk[b_], B_rc], writes=[B_t[k_]])
                        zv = zk.rearrange("p (m h i) -> p m h i", h=2, i=32)
                        kv_ = kb[:, si, 0:512].rearrange("p (m h i) -> p m h i", h=2, i=32)
                        x1 = zv[:, :, 0, :]; x2 = zv[:, :, 1, :]
                        cb = cT[:, si, :].unsqueeze(1).to_broadcast([128, 8, 32])
                        sbb = sT[:, si, :].unsqueeze(1).to_broadcast([128, 8, 32])
                        ta, tb, tc, td = [rtmp[q_][:, :].rearrange("p (m i) -> p m i", i=32) for q_ in range(4)]
                        P.op("dve", ttf(ta, x1, cb, ALU.mult), reads=[B_t[k_]] + B_ct, writes=[B_rt])
                        P.op("dve", ttf(tb, x2, sbb, ALU.mult), reads=[B_t[k_]] + B_ct, writes=[B_rt])
                        P.op("dve", ttf(tc, x2, cb, ALU.mult), reads=[B_t[k_]] + B_ct, writes=[B_rt])
                        P.op("dve", ttf(td, x1, sbb, ALU.mult), reads=[B_t[k_]] + B_ct, writes=[B_rt])
                        P.op("dve", ttf(kv_[:, :, 0, :], ta, tb, ALU.subtract), reads=[B_rt], writes=B_kb)
                        P.op("dve", ttf(kv_[:, :, 1, :], tc, td, ALU.add), reads=[B_rt], writes=B_kb)
                    elif nb == 1:
                        P.op("act", actf(kb[:, si, 512:1024], bank(b_), AF.Copy, scale=rcol[:, si:si + 1]), reads=[bk[b_], B_rc], writes=B_kb)
                    else:
                        P.op("act", actf(vst[sl][:, si, (nb - 2) * 512:(nb - 1) * 512], bank(b_), AF.Copy, scale=rcol[:, si:si + 1]),
                             reads=[bk[b_], B_rc], writes=[B_vst[sl]])
            if i + 1 < 31:
                p1_prep(i + 1)
            for g in range(8):
                fc = 512 + 128 * g if g < 4 else 128 * (g - 4)
                b_ = 4 + p1rot["tr"] % 3; p1rot["tr"] += 1
                for si in range(4):
                    P.op("pe", mm(ps[:, b_ * 512 + si * 128: b_ * 512 + (si + 1) * 128], kb[:, si, fc:fc + 128], ident), reads=B_kb + [B_c], writes=[bk[b_]])
                eng = "dve" if g % 2 == 0 else "act"
                if eng == "dve":
                    P.op("dve", cpf(kst[sl][:, g, :], bank(b_)), reads=[bk[b_]], writes=[B_kst[sl]])
                else:
                    P.op("act", actf(kst[sl][:, g, :], bank(b_), AF.Copy), reads=[bk[b_]], writes=[B_kst[sl]])

        p1_loads(0)
        p1_prep(0)
        for i in range(31):
            sl = (i + 1) % 2
            if i + 1 < 31:
                p1_loads(i + 1)
            p1_tile(i)
            store_scratch(sl, 1 + 4 * i, 4)
            if i == 3:
                load_q_weights()
            if i == 5:
                P.dma("sp", wq_scr[:, :, :], WBIG[:, :, 2048:4096], reads=[B_Wq])
            if 6 <= i < 6 + len(conv_jobs):
                P.dma("pool", conv_jobs[i - 6][0], conv_jobs[i - 6][1])
        P.op("pool", msf(kst[0][:, :, 0:128], 0.0), writes=[B_kst[0]])
        P.op("pool", msf(vst[0][:, 0, :], 0.0), writes=[B_vst[0]])
        project(NMETA, [(0, NMETA)], xT_v[:, :, 0:NMETA], dr["cosF"][:, 0:NMETA], dr["sinF"][:, 0:NMETA], True,
                lambda g: kst[0][:, g, :NMETA], B_kst[0], lambda si, tsz: vst[0][:tsz, 0, :], B_vst[0],
                zt_out=lambda si, tsz: dr["metao"][:, :],
                cosT_ap=dr["cosMT"], sinT_ap=dr["sinMT"])
        store_scratch(0, 0, 1)
        P.barrier()

        def attend(W, q_ap, groups_blocks, finish, mid_hook=None):
            pending_fin = []
            for g in range(8):
                pro, blocks, after = groups_blocks(g)
                pro()
                n = len(blocks)
                if g < 4:
                    S = [(0, 1), (2, 3), (4, 5)]

                    def zmm(i):
                        bl = blocks[i]; s0, s1 = S[i % 3]; nk = bl["nk"]; c0 = bl.get("c0", 0)
                        P.op("pe", mm(bank(s0, nk, W)[:, c0:W], bl["kT"][0:64, :], q_ap(g)[0:64, c0:W]), reads=bl["bufs"] + [B_qT], writes=[bk[s0]])
                        P.op("pe", mm(bank(s1, nk, W)[:, c0:W], bl["kT"][64:128, :], q_ap(g)[64:128, c0:W]), reads=bl["bufs"] + [B_qT], writes=[bk[s1]])

                    def Sv(i, nk):
                        s0 = S[i % 3][0]
                        return ps[:nk, s0 * 512:(s0 + 2) * 512].rearrange("p (h w) -> p h w", h=2)[:, :, :W]

                    def u_(i):
                        bl = blocks[i]; nk = bl["nk"]; s0, s1 = S[i % 3]; c0 = bl.get("c0", 0)
                        P.op("act", actf(u_t[i % 2][:nk, :, c0:W], Sv(i, nk)[:, :, c0:W], AF.Exp, bias=bl["bias"]), reads=[bk[s0], bk[s1], B_c], writes=[B_u[i % 2]])

                    def L_(i):
                        bl = blocks[i]; nk = bl["nk"]; c0 = bl.get("c0", 0)
                        if c0 > 0:
                            P.op("pool", msf(L_t[i % 2][:nk, :, 0:c0], 0.0), writes=[B_L[i % 2]])
                        P.op("act", actf(L_t[i % 2][:nk, :, c0:W], u_t[i % 2][:nk, :, c0:W], AF.Ln, bias=smallc[:nk, 9:10]), reads=[B_u[i % 2], B_small], writes=[B_L[i % 2]])
                        if bl["msb"] is not None:
                            for h in range(2):
                                P.op("dve", ttf(L_t[i % 2][:nk, h, c0:W], L_t[i % 2][:nk, h, c0:W], bl["msb"][:, c0:W], ALU.mult), reads=[B_L[i % 2], B_c], writes=[B_L[i % 2]])

                    def E_(k):
                        bl = blocks[k]; nk = bl["nk"]; s0, s1 = S[k % 3]
                        if k == 1:
                            P.op("dve", cpf(Lacc[1][:nk, :, :W], L_t[0][:nk, :, :W]), reads=[B_L[0]], writes=[B_Lacc[1]])
                        elif k > 1:
                            P.op("dve", ttf(Lacc[k % 2][:nk, :, :W], Lacc[(k - 1) % 2][:nk, :, :W], L_t[(k - 1) % 2][:nk, :, :W], ALU.add),
                                 reads=[B_L[(k - 1) % 2], B_Lacc[(k - 1) % 2]], writes=[B_Lacc[k % 2]])
                        for h, sb_ in ((0, s0), (1, s1)):
                            P.op("pe", mm(bank(sb_, nk, W), ntri[:nk, :nk], L_t[k % 2][:nk, h, :W], False, k == 0, sgc=True), reads=[B_L[k % 2], B_c], writes=[bk[sb_]])
                            if k > 0:
                                P.op("pe", mm(bank(sb_, nk, W), nones[:nk, :nk], Lacc[k % 2][:nk, h, :W], False, True, sgc=True), reads=[B_Lacc[k % 2], B_c], writes=[bk[sb_]])

                    zmm(0)
                    if n > 1:
                        zmm(1)
                    if n > 2:
                        zmm(2)
                    u_(0); L_(0)
                    if n > 1:
                        u_(1)
                    E_(0)
                    while pending_fin:
                        pending_fin.pop(0)()
                    for i in range(n):
                        bl = blocks[i]; nk = bl["nk"]; s0, s1 = S[i % 3]
                        if i + 1 < n:
                            L_(i + 1)
                        if i + 2 < n:
                            u_(i + 2)
                        c0 = bl.get("c0", 0)
                        P.op("act", actf(A_t[i % 2][:nk, :, c0:W], Sv(i, nk)[:, :, c0:W], AF.Exp, bias=bl["bias"]), reads=[bk[s0], bk[s1], B_c], writes=[B_A[i % 2]])
                        if bl["msb"] is not None:
                            for h in range(2):
                                P.op("dve" if h == 0 else "pool", ttf(A_t[i % 2][:nk, h, c0:W], A_t[i % 2][:nk, h, c0:W], bl["msb"][:, c0:W], ALU.mult), reads=[B_A[i % 2], B_c], writes=[B_A[i % 2]])
                        if i + 3 < n:
                            zmm(i + 3)
                        if i + 1 < n:
                            E_(i + 1)
                        for h in range(2):
                            P.op("pe", mm(bank(6 + h, 128, W)[:, c0:W], bl["v"], A_t[i % 2][:nk, h, c0:W], i == 0, i == n - 1), reads=bl["bufs"] + [B_A[i % 2]], writes=[bk[6 + h]])
                        after(i)
                else:
                    S = [(0, 1), (2, 3)]
                    A3 = [A_t[0], A_t[1], Lacc[0]]; B_A3 = [B_A[0], B_A[1], B_Lacc[0]]

                    def qk(i):
                        bl = blocks[i]; s0, s1 = S[i % 2]; nk = bl["nk"]; c0 = bl.get("c0", 0)
                        P.op("pe", mm(bank(s0, nk, W)[:, c0:W], bl["kT"][0:64, :], q_ap(g)[0:64, c0:W]), reads=bl["bufs"] + [B_qT], writes=[bk[s0]])
                        P.op("pe", mm(bank(s1, nk, W)[:, c0:W], bl["kT"][64:128, :], q_ap(g)[64:128, c0:W]), reads=bl["bufs"] + [B_qT], writes=[bk[s1]])

                    qk(0)
                    if n > 1:
                        qk(1)
                    while pending_fin:
                        pending_fin.pop(0)()
                    for i in range(n):
                        bl = blocks[i]; nk = bl["nk"]; s0, s1 = S[i % 2]
                        At = A3[i % 3]; BAt = B_A3[i % 3]
                        c0 = bl.get("c0", 0)
                        Sv_ = ps[:nk, s0 * 512:(s0 + 2) * 512].rearrange("p (h w) -> p h w", h=2)[:, :, c0:W]
                        P.op("act", actf(At[:nk, :, c0:W], Sv_, AF.Exp, bias=bl["bias"]), reads=[bk[s0], bk[s1], B_c], writes=[BAt])
                        if bl["mdf"] is not None:
                            for h in range(2):
                                P.op("dve" if h == 0 else "pool", ttf(At[:nk, h, c0:W], At[:nk, h, c0:W], bl["mdf"][:, c0:W], ALU.mult), reads=[BAt, B_c], writes=[BAt])
                        if i + 2 < n:
                            qk(i + 2)
                        for c in range(2):
                            P.op("pe", mm(bank(4 + 2 * c, 128, W)[:, c0:W], bl["v"], At[:nk, c, c0:W], i == 0, i == n - 1), reads=bl["bufs"] + [BAt], writes=[bk[4 + 2 * c]])
                        P.op("pe", mm(bank(5, 128, W)[:, c0:W], ones[:nk, :], At[:nk, 0, c0:W], i == 0, i == n - 1), reads=[BAt, B_c], writes=[bk[5]])
                        if i == 0:
                            if nk < 128 or c0 > 0:
                                P.op("dve", msf(Pacc[1][:, :W], 0.0), writes=[B_pa[1]])
                            P.op("dve", cpf(Pacc[1][:nk, c0:W], At[:nk, 1, c0:W]), reads=[BAt], writes=[B_pa[1]])
                        else:
                            P.op("dve", ttf(Pacc[1][:nk, c0:W], Pacc[1][:nk, c0:W], At[:nk, 1, c0:W], ALU.add), reads=[BAt], writes=[B_pa[1]])
                        after(i)
                    hi = L_t[0][:, 1, :W]; lo = L_t[1][:, 1, :W]
                    P.op("dve", cpf(hi, Pacc[1][:, :W]), reads=[B_pa[1]], writes=[B_L[0]])
                    P.op("dve", ttf(lo, Pacc[1][:, :W], hi, ALU.subtract), reads=[B_pa[1], B_L[0]], writes=[B_L[1]])
                    P.op("pe", mm(bank(7, 128, W), ones, hi, True, False), reads=[B_L[0], B_c], writes=[bk[7]])
                    P.op("pe", mm(bank(7, 128, W), ones, lo, False, True), reads=[B_L[1], B_c], writes=[bk[7]])
                if g < 4:
                    pending_fin.append(lambda g=g: finish(g))
                else:
                    finish(g)
                if g == 0 and mid_hook is not None:
                    mid_hook()
            while pending_fin:
                pending_fin.pop(0)()

        def make_finish(W, mix_dst):
            def finish(g):
                if g < 4:
                    P.op("dve", cpf(mix_dst(4 + g)[0:64, :], bank(6, 128, W)[0:64, :]), reads=[bk[6]], writes=[B_mix])
                    P.op("dve", cpf(mix_dst(4 + g)[64:128, :], bank(7, 128, W)[64:128, :]), reads=[bk[7]], writes=[B_mix])
                else:
                    h = g - 4
                    rl0, rl1, o0, o1, od, rs = [d_[:, :W] for d_ in dtmp]
                    P.op("act", actf(rl0, bank(5, 128, W), AF.Ln), reads=[bk[5]], writes=[B_dt])
                    P.op("act", actf(rl0, rl0, AF.Exp, scale=-1.0), reads=[B_dt], writes=[B_dt])
                    P.op("act", actf(rl1, bank(7, 128, W), AF.Ln), reads=[bk[7]], writes=[B_dt])
                    P.op("act", actf(rl1, rl1, AF.Exp, scale=-1.0), reads=[B_dt], writes=[B_dt])
                    P.op("dve", ttf(o0, bank(4, 128, W), rl0, ALU.mult), reads=[bk[4], B_dt], writes=[B_dt])
                    P.op("dve", ttf(o1, bank(6, 128, W), rl1, ALU.mult), reads=[bk[6], B_dt], writes=[B_dt])
                    P.op("dve", sttf(od, o1, smallc[:, 1:2], o0, ALU.mult, ALU.add), reads=[B_dt, B_small], writes=[B_dt])
                    P.op("act", actf(A_t[0][:, 0, :W], od, AF.Square), reads=[B_dt], writes=[B_A[0]])
                    P.op("pe", mm(bank(0, 128, W), ones, A_t[0][:, 0, :W]), reads=[B_A[0], B_c], writes=[bk[0]])
                    rstd_from(bank(0, 128, W), rs, 128, 128, W, [bk[0]], [B_dt], rl0)
                    P.op("dve", sttf(mix_dst(h), od, smallc[:, 2:3], rs, ALU.mult, ALU.mult), reads=[B_dt, B_small], writes=[B_mix])
            return finish

        def mlp_loads(T, subs, xq_ap, xqT_ap):
            P.dma("sp", r_t[:subs[0][1], :len(subs), :], xq_ap, writes=[B_r])
            P.dma("sp", xqT_t[:, :, :T], xqT_ap, writes=[B_xqT])

        def mlp(T, subs, xq_ap, xqT_ap, y_out, preloaded=False, after_down=None):
            nsub = len(subs)
            if not preloaded:
                mlp_loads(T, subs, xq_ap, xqT_ap)
            w_out_v = dr["w_out"].rearrange("(c p) n -> p c n", p=128)
            wo = [wbuf[i].rearrange("p (c n) -> p c n", n=512) for i in range(2)]
            for i in range(2):
                P.dma("sp", wo[i], wo_scr[i], writes=[B_wb[i]])
            P.op("dve", msf(mcol[:, :], 0.0), writes=[B_mc])
            rot = 0
            for si, (c0, tsz) in enumerate(subs):
                for nh in range(2):
                    b = rot % 2; rot += 1
                    for c in range(8):
                        P.op("pe", mm(bank(b, tsz, 512), mixT[:, c, c0:c0 + tsz], wo[nh][:, c, :], c == 0, c == 7), reads=[B_mix, B_wb[nh]], writes=[bk[b]])
                    P.op("dve", ttf(r_t[:tsz, si, nh * 512:(nh + 1) * 512], bank(b, tsz, 512), r_t[:tsz, si, nh * 512:(nh + 1) * 512], ALU.add),
                         reads=[bk[b], B_r], writes=[B_r])
                P.op("act", actf(junk[:tsz, :], r_t[:tsz, si, :], AF.Square, accum_out=mcol[:tsz, si:si + 1]), reads=[B_r, B_mc], writes=[B_mc, B_rt])
            tszm = max(s[1] for s in subs)
            rstd_from(mcol[:tszm, 0:nsub], mcol[:tszm, 8:8 + nsub], D, tszm, nsub, [B_mc], [B_mc], lntmp[:tszm, :nsub])
            P.op("dve", ttf(mcol[:tszm, 16:16 + nsub], mcol[:tszm, 8:8 + nsub], mcol[:tszm, 8:8 + nsub], ALU.mult), reads=[B_mc], writes=[B_mc])
            for nchunk in range(8):
                b = 2 + rot % 2; rot += 1
                for c in range(8):
                    P.op("pe", mm(bank(b, 128, T), wo[nchunk // 4][:, c, (nchunk % 4) * 128:(nchunk % 4 + 1) * 128], mixT[:, c, :T], c == 0, c == 7),
                         reads=[B_mix, B_wb[nchunk // 4]], writes=[bk[b]])
                k = nchunk % 2
                P.op("dve", ttf(x1tmp[k][:, :T], bank(b, 128, T), xqT_t[:, nchunk, :T], ALU.add), reads=[bk[b], B_xqT], writes=[B_x1t[k]])
                P.op("act", actf(x1g[:, nchunk, :T], x1tmp[k][:, :T], AF.Copy, scale=gcols[:, 8 + nchunk:9 + nchunk]), reads=[B_x1t[k], B_c], writes=[B_x1g])
            w_up_v = dr["w_up"].rearrange("(c p) n -> p c n", p=128)
            for pc in range(8):
                wi = pc % 2
                wu = wbuf[wi].rearrange("p (c n) -> p c n", n=512)
                P.dma("sp", wu, wu_scr[pc], writes=[B_wb[wi]])
                for mc in range(4):
                    m = pc * 4 + mc
                    b = 4 + rot % 4; rot += 1
                    for c in range(8):
                        P.op("pe", mm(bank(b, 128, T), wu[:, c, mc * 128:(mc + 1) * 128], x1g[:, c, :T], c == 0, c == 7), reads=[B_x1g, B_wb[wi]], writes=[bk[b]])
                    k = m % 2
                    P.op("act", actf(rl[k][:, :T], bank(b, 128, T), AF.Relu), reads=[bk[b]], writes=[B_rl[k]])
                    a_m = WBIG[:, m // 4, 2048 + (m % 4) * 512: 2048 + (m % 4) * 512 + T]
                    P.op("dve", ttf(a_m, rl[k][:, :T], rl[k][:, :T], ALU.mult), reads=[B_rl[k]], writes=[B_Wq])
            w_dn_v = dr["w_down"].rearrange("(m p) n -> p m n", p=128)
            for pc in range(8):
                wi = pc % 2
                wd = wbuf[wi].rearrange("p (m n) -> p m n", n=1024)
                P.dma("sp", wd, wd_scr[pc], writes=[B_wb[wi]])
                for mc in range(4):
                    m = pc * 4 + mc
                    a_m = WBIG[:, m // 4, 2048 + (m % 4) * 512: 2048 + (m % 4) * 512 + T]
                    for si, (c0, tsz) in enumerate(subs):
                        for nh in range(2):
                            b = 2 * si + nh
                            P.op("pe", mm(bank(b, tsz, 512), a_m[:, c0:c0 + tsz], wd[:, mc, nh * 512:(nh + 1) * 512], m == 0, m == 31),
                                 reads=[B_Wq, B_wb[wi]], writes=[bk[b]])
            if after_down is not None:
                after_down()
            for si, (c0, tsz) in enumerate(subs):
                for nh in range(2):
                    b = 2 * si + nh
                    P.op("dve", sttf(r_t[:tsz, si, nh * 512:(nh + 1) * 512], bank(b, tsz, 512), mcol[:tsz, 16 + si:17 + si], r_t[:tsz, si, nh * 512:(nh + 1) * 512], ALU.mult, ALU.add),
                         reads=[bk[b], B_mc, B_r], writes=[B_r])
                P.op("act", actf(junk[:tsz, :], r_t[:tsz, si, :], AF.Square, accum_out=mcol[:tsz, 4 + si:5 + si]), reads=[B_r, B_mc], writes=[B_mc, B_rt])
            rstd_from(mcol[:tszm, 4:4 + nsub], mcol[:tszm, 12:12 + nsub], D, tszm, nsub, [B_mc], [B_mc], lntmp[:tszm, :nsub])
            for si, (c0, tsz) in enumerate(subs):
                P.op("dve", sttf(r_t[:tsz, si, :], r_t[:tsz, si, :], mcol[:tsz, 12 + si:13 + si], gfin[:tsz, :], ALU.mult, ALU.mult), reads=[B_r, B_mc, B_c], writes=[B_r])
            P.dma("sp", y_out, r_t[:subs[0][1], :nsub, :], reads=[B_r])

        P.op("dve", msf(smallc[:, 9:10], 1.0), writes=[B_small])

        subs4 = [(s * 128, 128) for s in range(4)]
        for j in range(NSLOT):
            if j > 0:
                P.dma("sp", cosF_t[:, :TQ], dr["cosQ"][j], writes=[B_cs])
                P.dma("sp", sinF_t[:, :TQ], dr["sinQ"][j], writes=[B_cs])
                P.dma("sp", cosT_t[:, :4, :], dr["cosT"][j], writes=[B_cst])
                P.dma("sp", sinT_t[:, :4, :], dr["sinT"][j], writes=[B_cst])
            project(TQ, subs4, dr["xqT"][j].rearrange("(c p) t -> p c t", p=128), dr["cosQ"][j], dr["sinQ"][j], True,
                    lambda g: kTo[:, g, :], B_kTo, lambda si, tsz: Vo[:, si, :], B_Vo,
                    q_dst=lambda g: qT[:, g, :], B_qdst=B_qT,
                    zt_out=lambda si, tsz, j=j: dr["kvo"][j, si * 128:(si + 1) * 128, :],
                    cosT_ap=dr["cosT"][j], sinT_ap=dr["sinT"][j], do_loads=(j == 0))
            P.barrier()
            NB = 32 * j + 29
            nseg = (NB + 15) // 16
            segs = list(range(nseg - 1, -1, -1))
            items = [(g, s_) for g in range(8) for s_ in segs]
            loaded = {"n": 0}

            def load_item(t, NB=NB):
                g, s_ = items[t]
                sl = t % 3
                nb = min(16, NB - 16 * s_)
                P.dma("sp", Kseg[sl][:, :nb * 128], kT_scr[g, :, s_ * 2048: s_ * 2048 + nb * 128], writes=[B_seg[sl]])
                P.dma("sp", Vseg[sl][:, :nb, :], v_scr[g, :, 16 * s_:16 * s_ + nb, :], writes=[B_seg[sl]])

            def groups_blocks(g, j=j, NB=NB, nseg=nseg, segs=segs, items=items, loaded=loaded, load_item=load_item):
                def pro():
                    while loaded["n"] < min(3, len(items)):
                        load_item(loaded["n"]); loaded["n"] += 1

                vc = 512 + 128 * g if g < 4 else 128 * (g - 4)
                blocks = []
                for r in (3, 2, 1, 0):
                    blocks.append(dict(kT=kTo[:, g, r * 128:(r + 1) * 128], v=Vo[:, r, vc:vc + 128], bias=kbias[:, ZB:ZB + 1], nk=128,
                                       msb=dmask[:, r, :], mdf=dmask[:, 4 + r, :], bufs=[B_kTo, B_Vo], item=None, last=False, c0=128 * r))
                for sidx, s_ in enumerate(segs):
                    t = g * nseg + sidx
                    sl = t % 3
                    nb = min(16, NB - 16 * s_)
                    for bb in range(nb - 1, -1, -1):
                        b_ = 16 * s_ + bb
                        blocks.append(dict(kT=Kseg[sl][:, bb * 128:(bb + 1) * 128], v=Vseg[sl][:, bb, :], bias=kbias[:, j * NBLK + b_: j * NBLK + b_ + 1], nk=128,
                                           msb=None, mdf=None, bufs=[B_seg[sl]], item=t, last=(bb == 0)))

                def after(i):
                    bl = blocks[i]
                    if bl["item"] is not None and bl["last"] and loaded["n"] < len(items):
                        load_item(loaded["n"]); loaded["n"] += 1
                return pro, blocks, after

            xq_ap_j = dr["xq"][j].rearrange("(s p) n -> p s n", p=128)
            xqT_ap_j = dr["xqT"][j].rearrange("(c p) t -> p c t", p=128)
            attend(TQ, lambda g: qT[:, g, :], groups_blocks, make_finish(TQ, lambda ch: mixT[:, ch, :]),
                   mid_hook=lambda: mlp_loads(TQ, subs4, xq_ap_j, xqT_ap_j))
            P.barrier()
            def next_prefetch(j=j):
                if j + 1 < NSLOT:
                    P.dma("sp", WBIG[:, :, 2048:4096], wq_scr[:, :, :], writes=[B_Wq])
            if j + 1 < NSLOT:
                P.dma("sp", xin2[0][:, :, :TQ], dr["xqT"][j + 1].rearrange("(c p) t -> p c t", p=128), writes=[B_xin0])
            mlp(TQ, subs4, xq_ap_j, xqT_ap_j, dr["y"][j].rearrange("(s p) n -> p s n", p=128), preloaded=True, after_down=next_prefetch)
            P.barrier()

        P.dma("sp", WBIG[:, :, 2048:4096], wq_scr[:, :, :], writes=[B_Wq])
        subs_s = [(b * DEC_T, DEC_T) for b in range(BPC)]
        project(ST, subs_s, dr["xsT"].rearrange("(c p) t -> p c t", p=128), dr["cosS"][:, :], dr["sinS"][:, :], True,
                lambda g: kTo[:, g, :ST], B_kTo, lambda si, tsz: Vo[:tsz, si, :], B_Vo,
                q_dst=lambda g: qT[:, g, :ST], B_qdst=B_qT,
                zt_out=lambda si, tsz: dr["kvs"][si * DEC_T:(si + 1) * DEC_T, :],
                cosT_ap=dr["cosST"], sinT_ap=dr["sinST"])
        P.barrier()
        ckb = [SB[:, i * 8192:(i + 1) * 8192].rearrange("p (b n) -> p b n", n=1024) for i in range(2)]
        cvb = [WBIG[:, :, 2048 + i * 1024: 3072 + i * 1024] for i in range(2)]
        kTcb = [SF[:, 6144 + i * 4096: 10240 + i * 4096].bitcast(BF16).rearrange("p (g t) -> p g t", t=1024) for i in range(2)]
        B_ck = [Buf(), Buf()]; B_cv = [Buf(), Buf()]; B_kTc = [Buf(), Buf()]
        sm = SB[:, 16384:18432]
        L_f = [sm[:, i * 128:(i + 1) * 128] for i in range(2)]; B_Lf = [Buf(), Buf()]
        A_f = [sm[:, 256 + i * 128: 256 + (i + 1) * 128] for i in range(2)]; B_Af = [Buf(), Buf()]
        La_f = [sm[:, 512 + i * 128: 512 + (i + 1) * 128] for i in range(2)]; B_Laf = [Buf(), Buf()]
        P_f = [sm[:, 768 + i * 128: 768 + (i + 1) * 128] for i in range(3)]; B_Pf = [Buf(), Buf(), Buf()]
        sq_s = sm[:, 1152:1216]; B_sqs = Buf()
        u_f = [SF[:, i * 128:(i + 1) * 128] for i in range(2)]; B_uf = [Buf(), Buf()]
        mKV = SF[:, 3072:4096].bitcast(BF16)
        metaK = mKV[:, 0:1024].rearrange("p (g k) -> p g k", k=128); metaV = mKV[:, 1024:2048].rearrange("p (g k) -> p g k", k=128); B_mkv = Buf()
        rl_s = SF[:, 2048:2176]; o_s = SF[:, 2176:2304]; od_s = SF[:, 2304:2368]; rs_s = SF[:, 2368:2432]; B_fs = Buf()
        qz = SF[:, 2560:3072].bitcast(BF16).rearrange("p (m t) -> p m t", t=ST); B_qz = Buf()
        P.op("pool", msf(qz[:, :, :], 0.0), writes=[B_qz])
        qzs = qz[:, 0:8, :].rearrange("p (g two) t -> p g two t", two=2)
        qzd = qz[:, 8:16, :].rearrange("p (g two) t -> p g two t", two=2)
        P.op("pool", cpf(qzs[0:64, :, 0, :], qT[0:64, 0:4, 0:ST]), reads=[B_qT], writes=[B_qz])
        P.op("pool", cpf(qzs[64:128, :, 1, :], qT[64:128, 0:4, 0:ST]), reads=[B_qT], writes=[B_qz])
        P.op("pool", cpf(qzd[0:64, :, 0, :], qT[0:64, 4:8, 0:ST]), reads=[B_qT], writes=[B_qz])
        P.op("pool", cpf(qzd[64:128, :, 1, :], qT[64:128, 4:8, 0:ST]), reads=[B_qT], writes=[B_qz])
        for g in range(8):
            P.dma("sp", metaK[:, g, :], kT_scr[g, :, 0:128], writes=[B_mkv])
            P.dma("sp", metaV[:, g, :], v_scr[g, :, 0, :], writes=[B_mkv])

        def s_loads(bi):
            p_ = bi % 2
            for (nm, off) in (("cdk", 0), ("csk", 512)):
                P.dma("pool", ckb[p_][:, :, off:off + 512], dr[nm][bi].rearrange("(b p) n -> p b n", p=128), writes=[B_ck[p_]])
            for (nm, off) in (("cdv", 0), ("csv", 512)):
                P.dma("pool", cvb[p_][:, :, off:off + 512], dr[nm][bi].rearrange("(b p) n -> p b n", p=128), writes=[B_cv[p_], B_Wq])

        def s_transposes(bi):
            p_ = bi % 2
            rotk = 0
            for g in range(8):
                fc = 512 + 128 * g if g < 4 else 128 * (g - 4)
                for half in range(2):
                    b_ = rotk % 2; rotk += 1
                    for q4 in range(4):
                        blk = half * 4 + q4
                        P.op("pe", mm(ps[:, b_ * 512 + q4 * 128: b_ * 512 + (q4 + 1) * 128], ckb[p_][:, blk, fc:fc + 128], ident), reads=[B_ck[p_], B_c], writes=[bk[b_]])
                    P.op("dve", cpf(kTcb[p_][:, g, half * 512:(half + 1) * 512], bank(b_)), reads=[bk[b_]], writes=[B_kTc[p_]])

        W16 = DEC_T

        def s_attend(bi):
            p_ = bi % 2
            q0 = bi * W16
            vcol = lambda g: 512 + 128 * g if g < 4 else 128 * (g - 4)
            blocks = [dict(kT=lambda g: kTo[:, g, q0:q0 + W16], v=lambda g: Vo[:W16, bi, vcol(g):vcol(g) + 128], bias=kbias[:W16, ZB:ZB + 1], nk=W16,
                           mask=True, bufs=[B_kTo, B_Vo])]
            for blk in range(7, -1, -1):
                blocks.append(dict(kT=lambda g, blk=blk: kTcb[p_][:, g, blk * 128:(blk + 1) * 128], v=lambda g, blk=blk: cvb[p_][:, blk, vcol(g):vcol(g) + 128],
                                   bias=kbias[:, ZB:ZB + 1], nk=128, mask=False, bufs=[B_kTc[p_], B_cv[p_]]))
            blocks.append(dict(kT=lambda g: metaK[:, g, :], v=lambda g: metaV[:, g, :], bias=kbias[:, MB:MB + 1], nk=128, mask=False, bufs=[B_mkv]))
            n = len(blocks)
            SS = [0, 1, 2]; SD = [3, 4, 5]
            mask3 = dmask[:W16, 0, :W16].unsqueeze(1).to_broadcast([W16, 8, W16])

            def zmm(i):
                bl = blocks[i]; nk = bl["nk"]; sb_ = SS[i % 3]
                for h in range(8):
                    g = h // 2
                    P.op("pe", mm(ps[:nk, sb_ * 512 + h * W16: sb_ * 512 + (h + 1) * W16], bl["kT"](g), qz[:, h, q0:q0 + W16], h == 0, h == 7),
                         reads=bl["bufs"] + [B_qz], writes=[bk[sb_]])

            def uL(i):
                bl = blocks[i]; nk = bl["nk"]; sb_ = SS[i % 3]
                P.op("act", actf(u_f[i % 2][:nk, :], bank(sb_, nk, 128), AF.Exp, bias=bl["bias"]), reads=[bk[sb_], B_c], writes=[B_uf[i % 2]])
                P.op("act", actf(L_f[i % 2][:nk, :], u_f[i % 2][:nk, :], AF.Ln, bias=smallc[:nk, 9:10]), reads=[B_uf[i % 2], B_small], writes=[B_Lf[i % 2]])
                if bl["mask"]:
                    Lv = L_f[i % 2][:nk, :].rearrange("p (h w) -> p h w", w=W16)
                    P.op("dve", ttf(Lv, Lv, mask3, ALU.mult), reads=[B_Lf[i % 2], B_c], writes=[B_Lf[i % 2]])

            def qk(i):
                bl = blocks[i]; nk = bl["nk"]; sd_ = SD[i % 3]
                for m in range(8):
                    g = 4 + m // 2
                    P.op("pe", mm(ps[:nk, sd_ * 512 + m * W16: sd_ * 512 + (m + 1) * W16], bl["kT"](g), qz[:, 8 + m, q0:q0 + W16], m == 0, m == 7),
                         reads=bl["bufs"] + [B_qz], writes=[bk[sd_]])

            zmm(0); qk(0)
            if n > 1:
                zmm(1); qk(1)
            uL(0)
            for i in range(n):
                bl = blocks[i]; nk = bl["nk"]; sb_ = SS[i % 3]; sd_ = SD[i % 3]
                P.op("act", actf(P_f[i % 3][:nk, :], bank(sd_, nk, 128), AF.Exp, bias=bl["bias"]), reads=[bk[sd_], B_c], writes=[B_Pf[i % 3]])
                if i + 2 < n:
                    zmm(i + 2)
                P.op("pe", mm(bank(sb_, nk, 128), ntri[:nk, :nk], L_f[i % 2][:nk, :], False, i == 0, sgc=True), reads=[B_Lf[i % 2], B_c], writes=[bk[sb_]])
                if i > 0:
                    pk = blocks[i - 1]["nk"]
                    P.op("pe", mm(bank(sb_, nk, 128), nones[:pk, :nk], La_f[i % 2][:pk, :], False, True, sgc=True), reads=[B_Laf[i % 2], B_c], writes=[bk[sb_]])
                if i + 1 < n:
                    if i == 0:
                        P.op("dve", cpf(La_f[1][:nk, :], L_f[0][:nk, :]), reads=[B_Lf[0]], writes=[B_Laf[1]])
                    else:
                        pk = blocks[i - 1]["nk"]
                        if pk < nk:
                            P.op("dve", cpf(La_f[(i + 1) % 2][:nk, :], L_f[i % 2][:nk, :]), reads=[B_Lf[i % 2]], writes=[B_Laf[(i + 1) % 2]])
                            P.op("dve", ttf(La_f[(i + 1) % 2][:pk, :], La_f[(i + 1) % 2][:pk, :], La_f[i % 2][:pk, :], ALU.add), reads=[B_Laf[i % 2]], writes=[B_Laf[(i + 1) % 2]])
                        else:
                            P.op("dve", ttf(La_f[(i + 1) % 2][:nk, :], La_f[i % 2][:nk, :], L_f[i % 2][:nk, :], ALU.add), reads=[B_Lf[i % 2], B_Laf[i % 2]], writes=[B_Laf[(i + 1) % 2]])
                    uL(i + 1)
                if i + 2 < n:
                    qk(i + 2)
                for m in range(8):
                    g = 4 + m // 2
                    P.op("pe", mm(ps[:, 7 * 512 + m * W16: 7 * 512 + (m + 1) * W16], bl["v"](g), P_f[i % 3][:nk, m * W16:(m + 1) * W16], i == 0 and m == 0, False),
                         reads=bl["bufs"] + [B_Pf[i % 3]], writes=[bk[7]])
                P.op("pe", mm(ps[:, 7 * 512 + 128: 7 * 512 + 256], ones[:nk, :], P_f[i % 3][:nk, :], False, i == n - 1), reads=[B_Pf[i % 3], B_c], writes=[bk[7]])
                P.op("act", actf(A_f[i % 2][:nk, :], bank(sb_, nk, 128), AF.Exp, bias=bl["bias"]), reads=[bk[sb_], B_c], writes=[B_Af[i % 2]])
                if bl["mask"]:
                    Av = A_f[i % 2][:nk, :].rearrange("p (h w) -> p h w", w=W16)
                    P.op("dve", ttf(Av, Av, mask3, ALU.mult), reads=[B_Af[i % 2], B_c], writes=[B_Af[i % 2]])
                for h in range(8):
                    g = h // 2
                    P.op("pe", mm(ps[:, 6 * 512 + h * W16: 6 * 512 + (h + 1) * W16], bl["v"](g), A_f[i % 2][:nk, h * W16:(h + 1) * W16], i == 0 and h == 0, i == n - 1 and h == 7),
                         reads=bl["bufs"] + [B_Af[i % 2]], writes=[bk[6]])
            acc6 = ps[:, 6 * 512: 6 * 512 + 128].rearrange("p (g two w) -> p g two w", two=2, w=W16)
            P.op("dve", cpf(mixT[0:64, 4:8, q0:q0 + W16], acc6[0:64, :, 0, :]), reads=[bk[6]], writes=[B_mix])
            P.op("dve", cpf(mixT[64:128, 4:8, q0:q0 + W16], acc6[64:128, :, 1, :]), reads=[bk[6]], writes=[B_mix])
            P.op("dve", lambda e: e.reciprocal(out=rl_s, in_=ps[:, 7 * 512 + 128: 7 * 512 + 256]), reads=[bk[7]], writes=[B_fs])
            P.op("dve", ttf(o_s, ps[:, 7 * 512: 7 * 512 + 128], rl_s, ALU.mult), reads=[bk[7], B_fs], writes=[B_fs])
            ov = o_s.rearrange("p (h c w) -> p h c w", c=2, w=W16)
            odv = od_s.rearrange("p (h w) -> p h w", w=W16)
            P.op("dve", sttf(odv, ov[:, :, 1, :], smallc[:, 1:2], ov[:, :, 0, :], ALU.mult, ALU.add), reads=[B_fs, B_small], writes=[B_fs])
            P.op("act", actf(sq_s, od_s, AF.Square), reads=[B_fs], writes=[B_sqs])
            P.op("pe", mm(ps[:, 3 * 512: 3 * 512 + 64], ones, sq_s), reads=[B_sqs, B_c], writes=[bk[3]])
            rstd_from(ps[:, 3 * 512: 3 * 512 + 64], rs_s, 128, 128, 64, [bk[3]], [B_fs])
            P.op("dve", sttf(mixT[:, 0:4, q0:q0 + W16], odv, smallc[:, 2:3], rs_s.rearrange("p (h w) -> p h w", w=W16), ALU.mult, ALU.mult),
                 reads=[B_fs, B_small], writes=[B_mix])

        s_loads(0)
        s_loads(1)
        s_transposes(0)
        for bi in range(BPC):
            s_attend(bi)
            if bi + 2 < BPC:
                s_loads(bi + 2)
            if bi + 1 < BPC:
                s_transposes(bi + 1)
        P.barrier()
        mlp(ST, [(0, ST)], dr["xs"][:, :].unsqueeze(1), dr["xsT"].rearrange("(c p) t -> p c t", p=128), dr["ys"][:, :].unsqueeze(1))
        P.barrier()

        with nc.Block() as block:
            @block.tensor
            def _(e):
                P.replay("pe", e, sems, dsems)

            @block.scalar
            def _(e):
                P.replay("act", e, sems, dsems)

            @block.vector
            def _(e):
                P.replay("dve", e, sems, dsems)

            @block.gpsimd
            def _(e):
                P.replay("pool", e, sems, dsems)

            @block.sync
            def _(e):
                P.replay("sp", e, sems, dsems)
    return nc


_NC = None


def _rope_tables(pos):
    half = 32
    inv = (np.float32(10000.0) ** (-(np.arange(half, dtype=np.float32) / np.float32(half)))).astype(np.float32)
    ang = (pos.astype(np.float32)[:, None] * inv[None, :]).astype(np.float32)
    return np.cos(ang).astype(np.float32), np.sin(ang).astype(np.float32)


def _fmajor(cos, sin):
    p = np.arange(128); d = p % 64; f = d % 32
    sgn = np.where(d < 32, -1.0, 1.0).astype(np.float32)
    return np.ascontiguousarray(cos[:, f].T), np.ascontiguousarray((sin[:, f] * sgn[None, :]).T)


def kernel(x_prompt, x_sample, cache_diff_k, cache_diff_v, cache_sb_k, cache_sb_v, meta_tokens, g_mix, w_in,
           lambda_q1, lambda_k1, lambda_q2, lambda_k2, g_diff_head, w_out, g_mlp, w_up, w_down, g_final):
    global _NC
    f32 = np.float32
    bf = ml_dtypes.bfloat16
    A = lambda a: np.ascontiguousarray(np.asarray(a, dtype=f32))
    xp = A(x_prompt)[0]
    meta = A(meta_tokens)
    xT = np.ascontiguousarray(np.concatenate([meta, xp], axis=0).T)
    cosP, sinP = _rope_tables(np.arange(TP))
    cosF, sinF = _fmajor(cosP, sinP)
    gcols = np.concatenate([A(g_mix)[0].reshape(8, 128).T, A(g_mlp)[0].reshape(8, 128).T], axis=1)
    gfin = np.ascontiguousarray(np.broadcast_to(A(g_final)[None, :], (128, D)))
    ghead = A(g_diff_head)[0].reshape(128, 1)
    lamv = np.ascontiguousarray(np.broadcast_to(np.concatenate([A(lambda_q1)[0], A(lambda_k1)[0], A(lambda_q2)[0], A(lambda_k2)[0]])[None, :], (128, 256)))
    k = np.arange(128)[:, None]; q = np.arange(512)[None, :]
    dmask = np.zeros((128, 8, 512), f32)
    for r in range(4):
        dmask[:, r, :] = (128 * r + k < q)
        dmask[:, 4 + r, :] = ((128 * r + k) // 64 <= q // 64)
    dmask = dmask.astype(bf)
    consts = np.zeros((128, 512), f32)
    consts[:, 0:128] = -((np.arange(128)[:, None] >= np.arange(128)[None, :]).astype(f32))
    consts[:, 128:256] = -1.0
    consts[:, 256:384] = 1.0
    consts[:, 384:512] = np.eye(128)
    consts = consts.astype(bf)
    cosS_, sinS_ = _rope_tables(NMETA + PAST + np.arange(DEC_T))
    cosSF, sinSF = _fmajor(np.tile(cosS_, (BPC, 1)), np.tile(sinS_, (BPC, 1)))
    xs_all = A(x_sample)
    cdk = A(cache_diff_k)[0].reshape(DEC_B, PAST, 512); csk = A(cache_sb_k)[0].reshape(DEC_B, PAST, 512)
    cdv = A(cache_diff_v)[0].reshape(DEC_B, PAST, 512); csv = A(cache_sb_v)[0].reshape(DEC_B, PAST, 512)
    shared = dict(xT=xT, w_in=A(w_in)[0], w_out=A(w_out)[0], w_up=A(w_up)[0], w_down=A(w_down)[0], gcols=np.ascontiguousarray(gcols),
                  gfin=gfin, ghead=np.ascontiguousarray(ghead), lamv=lamv, cosF=cosF, sinF=sinF, dmask=dmask, consts=consts,
                  cosS=cosSF, sinS=sinSF, cosST=np.ascontiguousarray(np.broadcast_to(cosS_[:, None, :], (DEC_T, BPC, 32))), sinST=np.ascontiguousarray(np.broadcast_to(sinS_[:, None, :], (DEC_T, BPC, 32))),
                  cosMT=np.ascontiguousarray(cosP[0:NMETA, None, :]), sinMT=np.ascontiguousarray(sinP[0:NMETA, None, :]),
                  cosTp=np.ascontiguousarray(cosP[NMETA:NMETA + 31 * TQ].reshape(31, 4, 128, 32).transpose(0, 2, 1, 3)),
                  sinTp=np.ascontiguousarray(sinP[NMETA:NMETA + 31 * TQ].reshape(31, 4, 128, 32).transpose(0, 2, 1, 3)))
    in_maps = []
    for c in range(NCORES):
        m = dict(shared)
        tiles = [8 * j + c for j in range(NSLOT)]
        m["xqT"] = np.ascontiguousarray(np.stack([xp[TQ * g:TQ * (g + 1)].T for g in tiles]))
        m["xq"] = np.ascontiguousarray(np.stack([xp[TQ * g:TQ * (g + 1)] for g in tiles]))
        m["cosQ"] = np.ascontiguousarray(np.stack([cosF[:, NMETA + TQ * g: NMETA + TQ * (g + 1)] for g in tiles]))
        m["sinQ"] = np.ascontiguousarray(np.stack([sinF[:, NMETA + TQ * g: NMETA + TQ * (g + 1)] for g in tiles]))
        m["cosT"] = np.ascontiguousarray(np.stack([cosP[NMETA + TQ * g: NMETA + TQ * (g + 1)].reshape(4, 128, 32).transpose(1, 0, 2) for g in tiles]))
        m["sinT"] = np.ascontiguousarray(np.stack([sinP[NMETA + TQ * g: NMETA + TQ * (g + 1)].reshape(4, 128, 32).transpose(1, 0, 2) for g in tiles]))
        if c == 0:
            pass
        kb = np.zeros((128, NSLOT * NBLK + 2), f32)
        for j in range(NSLOT):
            for b in range(NBLK):
                if b == 0:
                    kb[16:, j * NBLK] = NEG
                elif (b - 1) >= 4 * (8 * j + c):
                    kb[:, j * NBLK + b] = NEG
        kb[16:, NSLOT * NBLK + 1] = NEG
        m["kbias"] = kb
        bs = slice(BPC * c, BPC * (c + 1))
        m["xsT"] = np.ascontiguousarray(xs_all[bs].reshape(ST, D).T)
        m["xs"] = np.ascontiguousarray(xs_all[bs].reshape(ST, D))
        m["cdk"] = np.ascontiguousarray(cdk[bs]); m["csk"] = np.ascontiguousarray(csk[bs])
        m["cdv"] = np.ascontiguousarray(cdv[bs]); m["csv"] = np.ascontiguousarray(csv[bs])
        in_maps.append(m)
    if _NC is None:
        _NC = build_program()
    res = run_bass_kernel_spmd(_NC, in_maps, core_ids=list(range(NCORES)))
    y_prompt = np.zeros((1, SEQ, D), f32)
    kv_p = np.zeros((TP, 2048), f32)
    y_sample = np.zeros((DEC_B, DEC_T, D), f32)
    kv_s = np.zeros((DEC_B, DEC_T, 2048), f32)
    for c in range(NCORES):
        r = res.results[c]
        for j in range(NSLOT):
            g = 8 * j + c
            y_prompt[0, TQ * g:TQ * (g + 1)] = r["y"][j]
            kv_p[NMETA + TQ * g: NMETA + TQ * (g + 1)] = r["kvo"][j]
        if c == 0:
            kv_p[0:NMETA] = r["metao"]
        y_sample[BPC * c:BPC * (c + 1)] = np.asarray(r["ys"]).reshape(BPC, DEC_T, D)
        kv_s[BPC * c:BPC * (c + 1)] = np.asarray(r["kvs"]).reshape(BPC, DEC_T, 2048)
    outs = (y_prompt, y_sample,
            kv_p[:, 0:512].reshape(1, 1, TP, 4, 2, 64).copy(), kv_p[:, 1024:1536].reshape(1, 1, TP, 4, 128).copy(),
            kv_p[:, 512:1024].reshape(1, 1, TP, 8, 64).copy(), kv_p[:, 1536:2048].reshape(1, 1, TP, 8, 64).copy(),
            kv_s[:, :, 0:512].reshape(1, DEC_B, DEC_T, 4, 2, 64).copy(), kv_s[:, :, 1024:1536].reshape(1, DEC_B, DEC_T, 4, 128).copy(),
            kv_s[:, :, 512:1024].reshape(1, DEC_B, DEC_T, 8, 64).copy(), kv_s[:, :, 1536:2048].reshape(1, DEC_B, DEC_T, 8, 64).copy())
    return outs
```
